# Optimizing a Trainium2 kernel written in Bass

```python
import math
import jax, jax.numpy as jnp
from jax import lax
import numpy as np

D_MODEL = 1024
BATCH = 8
SEQ = 2048
DEPTH = 2

CHUNK = 64
D_MIX = D_MODEL
GLA_HEADS = 4
GLA_WIDTH = D_MIX // 2
GLA_DV = GLA_WIDTH // GLA_HEADS
GLA_DK = GLA_DV // 2
GLA_GATE_RANK = 16
GLA_GATE_NORMALIZER = 16.0
SGU_WIDTH = D_MIX - GLA_WIDTH
SGU_GROUPS = 4
SGU_GROUP_DIM = SGU_WIDTH // SGU_GROUPS
SGU_WINDOW = 128
COL_Q = GLA_HEADS * GLA_DK
COL_K = GLA_HEADS * GLA_DK
COL_V = GLA_WIDTH
COL_R = GLA_WIDTH
COL_A = GLA_GATE_RANK
COL_SU = SGU_WIDTH
COL_SV = SGU_WIDTH
IN_COLS = COL_Q + COL_K + COL_V + COL_R + COL_A + COL_SU + COL_SV
PEER_HEADS = 8
PEER_NKEYS = 128
PEER_EXPERTS = PEER_NKEYS * PEER_NKEYS
PEER_TOPK = 16
PEER_DQ = 256
PEER_DQ_HALF = PEER_DQ // 2
PEER_HK = PEER_HEADS * PEER_TOPK
PEER_TOKEN_BLOCK = 128
LN_EPS = 1e-5
ALPHA = (2.0 * DEPTH) ** 0.25
BETA = (8.0 * DEPTH) ** -0.25

kernel_name = "hybrid_gla_gmlp_peer_deepnorm"


def _layer_norm(x, g, b):
    xf = x.astype(jnp.float32)
    mu = jnp.mean(xf, axis=-1, keepdims=True)
    var = jnp.mean(jnp.square(xf - mu), axis=-1, keepdims=True)
    y = (xf - mu) * lax.rsqrt(var + LN_EPS)
    return (y * g.astype(jnp.float32) + b.astype(jnp.float32)).astype(x.dtype)


def _rms_norm(x, g):
    xf = x.astype(jnp.float32)
    y = xf * lax.rsqrt(jnp.mean(jnp.square(xf), axis=-1, keepdims=True) + LN_EPS)
    return (y * g.astype(jnp.float32)).astype(x.dtype)


def _gla_chunked(q, k, v, logg):
    B, S, H, DK = q.shape
    DV = v.shape[-1]
    N = S // CHUNK

    def to_chunks(t):
        return t.astype(jnp.float32).reshape(B, N, CHUNK, H, t.shape[-1]).transpose(1, 0, 3, 2, 4)

    qc, kc, vc, gc = to_chunks(q), to_chunks(k), to_chunks(v), to_chunks(logg)
    causal = jnp.tril(jnp.ones((CHUNK, CHUNK), dtype=bool))

    def step(state, inp):
        qb, kb, vb, gb = inp
        bcum = jnp.cumsum(gb, axis=2)
        diff = bcum[:, :, :, None, :] - bcum[:, :, None, :, :]
        decay = jnp.exp(jnp.where(causal[:, :, None], diff, -jnp.inf))
        attn = jnp.einsum('bhtk,bhsk,bhtsk->bhts', qb, kb, decay)
        o = jnp.einsum('bhts,bhsv->bhtv', attn, vb) + \
            jnp.einsum('bhtk,bhkv->bhtv', qb * jnp.exp(bcum), state)
        b_last = bcum[:, :, -1:, :]
        state = jnp.exp(b_last[:, :, 0, :])[..., None] * state + \
            jnp.einsum('bhsk,bhsv->bhkv', kb * jnp.exp(b_last - bcum), vb)
        return state, o

    s0 = jnp.zeros((B, H, DK, DV), jnp.float32)
    _, o = lax.scan(step, s0, (qc, kc, vc, gc))
    return o.transpose(1, 0, 3, 2, 4).reshape(B, S, H, DV).astype(v.dtype)


def _spatial_gating(u_pre, v_pre, ln_g, ln_b, w_s, b_s):
    B, S, _ = u_pre.shape
    u = jax.nn.gelu(u_pre, approximate=False)
    v = _layer_norm(jax.nn.gelu(v_pre, approximate=False), ln_g, ln_b)
    blk = jnp.arange(SGU_WINDOW) // CHUNK
    mask = blk[None, :] <= blk[:, None]
    w = jnp.where(mask[None], w_s, 0)
    vb = v.reshape(B, S // SGU_WINDOW, SGU_WINDOW, SGU_GROUPS, SGU_GROUP_DIM)
    sv = jnp.einsum('gts,bnsgc->bntgc', w, vb) + b_s.T[:, :, None]
    return u * sv.reshape(B, S, SGU_WIDTH)


def _peer(x, wq, k1, k2, u_tab, v_tab):
    B, S, D = x.shape
    T = B * S
    xt = x.reshape(T, D)
    q = (xt @ wq).reshape(T, PEER_HEADS, 2, PEER_DQ_HALF)
    s1 = jnp.einsum('thd,nd->thn', q[:, :, 0], k1)
    s2 = jnp.einsum('thd,nd->thn', q[:, :, 1], k2)
    v1, i1 = lax.top_k(s1, PEER_TOPK)
    v2, i2 = lax.top_k(s2, PEER_TOPK)
    cand = (v1[..., :, None] + v2[..., None, :]).reshape(T, PEER_HEADS, PEER_TOPK * PEER_TOPK)
    cidx = (i1[..., :, None] * PEER_NKEYS + i2[..., None, :]).reshape(T, PEER_HEADS, PEER_TOPK * PEER_TOPK)
    sc, pos = lax.top_k(cand, PEER_TOPK)
    eidx = jnp.take_along_axis(cidx, pos, axis=-1).reshape(T, PEER_HK)
    gate = jax.nn.softmax(sc.astype(jnp.float32), axis=-1).astype(x.dtype).reshape(T, PEER_HK)
    nb = T // PEER_TOKEN_BLOCK

    def expert_block(args):
        xb, ib, gb = args
        ub = jnp.take(u_tab, ib, axis=0)
        act = jax.nn.gelu(jnp.einsum('td,tkd->tk', xb, ub), approximate=False)
        vb = jnp.take(v_tab, ib, axis=0)
        return jnp.einsum('tk,tkd->td', gb * act, vb)

    out = lax.map(expert_block, (xt.reshape(nb, PEER_TOKEN_BLOCK, D),
                                 eidx.reshape(nb, PEER_TOKEN_BLOCK, PEER_HK),
                                 gate.reshape(nb, PEER_TOKEN_BLOCK, PEER_HK)))
    return out.reshape(B, S, D)


def _mixer(x, w_in, w_gate_up, b_gate, gla_norm_g, sgu_ln_g, sgu_ln_b, sgu_w, sgu_b, w_out):
    B, S, _ = x.shape
    h = x @ w_in
    splits = np.cumsum([COL_Q, COL_K, COL_V, COL_R, COL_A, COL_SU]).tolist()
    q, k, v, r, a, su, sv = jnp.split(h, splits, axis=-1)
    logg = jax.nn.log_sigmoid((a @ w_gate_up + b_gate).astype(jnp.float32)) / GLA_GATE_NORMALIZER
    q = q.reshape(B, S, GLA_HEADS, GLA_DK) * (GLA_DK ** -0.5)
    k = k.reshape(B, S, GLA_HEADS, GLA_DK)
    v = v.reshape(B, S, GLA_HEADS, GLA_DV)
    o = _gla_chunked(q, k, v, logg.reshape(B, S, GLA_HEADS, GLA_DK))
    o = _rms_norm(o, gla_norm_g).reshape(B, S, GLA_WIDTH) * jax.nn.silu(r)
    g = _spatial_gating(su, sv, sgu_ln_g, sgu_ln_b, sgu_w, sgu_b)
    return jnp.concatenate([o, g], axis=-1) @ w_out


def setup_inputs(seed: int = 0) -> dict:
    key = jax.random.key(seed)
    ks = jax.random.split(key, 24)
    L, D = DEPTH, D_MODEL
    nrm = lambda k, shape, s: jax.random.normal(k, shape, jnp.float32) * s
    return {
        "x": nrm(ks[0], (BATCH, SEQ, D), 1.0),
        "ln_in_g": 1.0 + nrm(ks[1], (D,), 0.02),
        "ln_in_b": nrm(ks[2], (D,), 0.02),
        "w_in": nrm(ks[3], (L, D, IN_COLS), D ** -0.5),
        "w_gate_up": nrm(ks[4], (L, GLA_GATE_RANK, GLA_HEADS * GLA_DK), GLA_GATE_RANK ** -0.5),
        "b_gate": nrm(ks[5], (L, GLA_HEADS * GLA_DK), 0.1),
        "gla_norm_g": 1.0 + nrm(ks[6], (L, GLA_DV), 0.02),
        "sgu_ln_g": 1.0 + nrm(ks[7], (L, SGU_WIDTH), 0.02),
        "sgu_ln_b": nrm(ks[8], (L, SGU_WIDTH), 0.02),
        "sgu_w": nrm(ks[9], (L, SGU_GROUPS, SGU_WINDOW, SGU_WINDOW), SGU_WINDOW ** -0.5),
        "sgu_b": 1.0 + nrm(ks[10], (L, SGU_GROUPS, SGU_WINDOW), 0.02),
        "w_out": nrm(ks[11], (L, D_MIX, D), BETA * D_MIX ** -0.5),
        "ln1_g": 1.0 + nrm(ks[12], (L, D), 0.02),
        "ln1_b": nrm(ks[13], (L, D), 0.02),
        "peer_wq": nrm(ks[14], (L, D, PEER_HEADS * PEER_DQ), D ** -0.5),
        "peer_k1": nrm(ks[15], (L, PEER_NKEYS, PEER_DQ_HALF), PEER_DQ_HALF ** -0.5),
        "peer_k2": nrm(ks[16], (L, PEER_NKEYS, PEER_DQ_HALF), PEER_DQ_HALF ** -0.5),
        "peer_u": nrm(ks[17], (L, PEER_EXPERTS, D), D ** -0.5),
        "peer_v": nrm(ks[18], (L, PEER_EXPERTS, D), BETA * PEER_HEADS ** -0.5),
        "ln2_g": 1.0 + nrm(ks[19], (L, D), 0.02),
        "ln2_b": nrm(ks[20], (L, D), 0.02),
    }


def reference(x, ln_in_g, ln_in_b, w_in, w_gate_up, b_gate, gla_norm_g, sgu_ln_g, sgu_ln_b,
              sgu_w, sgu_b, w_out, ln1_g, ln1_b, peer_wq, peer_k1, peer_k2, peer_u, peer_v,
              ln2_g, ln2_b):
    h = _layer_norm(x, ln_in_g, ln_in_b)
    for l in range(DEPTH):
        mix = _mixer(h, w_in[l], w_gate_up[l], b_gate[l], gla_norm_g[l], sgu_ln_g[l], sgu_ln_b[l],
                     sgu_w[l], sgu_b[l], w_out[l])
        h = _layer_norm(ALPHA * h + mix, ln1_g[l], ln1_b[l])
        ffn = _peer(h, peer_wq[l], peer_k1[l], peer_k2[l], peer_u[l], peer_v[l])
        h = _layer_norm(ALPHA * h + ffn, ln2_g[l], ln2_b[l])
    return h
```

```python
import numpy as np
from contextlib import ExitStack
import concourse.bass as bass
import concourse.mybir as mybir
from concourse.bass_utils import run_bass_kernel_spmd

F32 = mybir.dt.float32
BF16 = mybir.dt.bfloat16
AF = mybir.ActivationFunctionType
ALU = mybir.AluOpType

D = 1024
NTOK = 2048
L = 2
INC = 2576
SEG = 1024
NTS = 8
NE = 16384
ALPHA = (2.0 * L) ** 0.25
EPS = 1e-5
ENGS = ["pe", "act", "dve", "pool", "sp"]


class Prog:
    def __init__(self, nc, same_engine_sync=True):
        self.nc = nc
        self.streams = {e: [] for e in ENGS}
        self.cnt = {e: 0 for e in ENGS}
        self.dma_cnt = {}
        self.lastw = {}
        self.readers = {}
        self.seen = {e: {} for e in ENGS}
        self.same_engine_sync = same_engine_sync

    def _need(self, eng, tok, waits):
        if tok is None:
            return
        sem, val = tok
        if sem == "E_" + eng and (eng == "pe" or not self.same_engine_sync):
            return
        if self.seen[eng].get(sem, 0) >= val:
            return
        if waits.get(sem, 0) < val:
            waits[sem] = val

    enabled = True
    capture = None

    def op(self, eng, fn, reads=(), writes=(), dma=None):
        if not self.enabled:
            return None
        if self.capture is not None:
            self.capture.append((eng, fn, tuple(reads), tuple(writes), dma))
            return None
        waits = {}
        for r in reads:
            self._need(eng, self.lastw.get(r), waits)
        for w in writes:
            self._need(eng, self.lastw.get(w), waits)
            for sem, val in self.readers.get(w, {}).items():
                self._need(eng, (sem, val), waits)
        for sem, val in waits.items():
            self.seen[eng][sem] = val
        if dma is None:
            self.cnt[eng] += 1
            tok = ("E_" + eng, self.cnt[eng])
        else:
            self.dma_cnt[dma] = self.dma_cnt.get(dma, 0) + 16
            tok = ("D_" + dma, self.dma_cnt[dma])
        self.streams[eng].append((fn, waits, tok))
        for r in reads:
            d = self.readers.setdefault(r, {})
            if d.get(tok[0], 0) < tok[1]:
                d[tok[0]] = tok[1]
        for w in writes:
            self.lastw[w] = tok
            self.readers[w] = {}
        return tok

    def wait_only(self, eng, toks):
        waits = {}
        for t in toks:
            self._need(eng, t, waits)
        for sem, val in waits.items():
            self.seen[eng][sem] = val
        if waits:
            self.streams[eng].append((None, waits, None))

    def all_tokens(self):
        toks = [("E_" + e, self.cnt[e]) for e in ENGS if self.cnt[e] > 0]
        toks += [("D_" + k, v) for k, v in self.dma_cnt.items()]
        return toks

    def barrier(self):
        toks = self.all_tokens()
        for e in ENGS:
            self.wait_only(e, toks)

    def emit(self, stack):
        nc = self.nc
        names = ["E_" + e for e in ENGS if self.cnt[e] > 0] + ["D_" + k for k in self.dma_cnt]
        sems = {n: stack.enter_context(nc.semaphore(n)) for n in names}
        block = stack.enter_context(nc.Block())
        deco = {"pe": block.tensor, "act": block.scalar, "dve": block.vector,
                "pool": block.gpsimd, "sp": block.sync}
        for e in ENGS:
            stream = self.streams[e]
            if not stream:
                continue

            def body(eng, stream=stream):
                for fn, waits, tok in stream:
                    for sem, val in waits.items():
                        eng.wait_ge(sems[sem], val)
                    if fn is None:
                        continue
                    inst = fn(eng)
                    inst.then_inc(sems[tok[0]], 16 if tok[0].startswith("D_") else 1)

            deco[e](body)


def I(name, *args, **kw):
    return lambda e: getattr(e, name)(*args, **kw)


class Buf:
    def __init__(self, t, F, base=0):
        self.t = t
        self.F = F
        self.base = base

    def a(self, off, *dims, p0=0, np_=128):
        return bass.AP(self.t, p0 * self.F + self.base + off, [[self.F, np_]] + [list(d) for d in dims])

    def sub(self, base):
        return Buf(self.t, self.F, self.base + base)


DBG_MAP = {}


def build(nseg=2, nlayers=L, stop=None, cut=99, dbg=None):
    nc = bass.Bass("TRN2", target_bir_lowering=False)
    DBG_MAP.clear()
    dbg_h = nc.dram_tensor("dbg", [128, 8192], F32, kind="ExternalOutput") if dbg is not None else None
    dt = lambda n, s: nc.dram_tensor(n, s, F32, kind="ExternalInput")
    x_h = dt("x", [NTOK, D])
    cst_h = dt("cst", [128, 640])
    lnin_h = dt("lnin", [2, D])
    win_h = dt("w_in", [L * D, INC])
    wout_h = dt("w_out", [L * D, D])
    wq_h = dt("wq", [L * D, 2048])
    wgu_h = dt("wgu", [L * 16, 256])
    bgate_h = dt("bgate", [L, 256])
    glag_h = dt("glag", [L, 128])
    sgln_h = dt("sgln", [L * 2, 512])
    sgwT_h = dt("sgwT", [L * 4 * 128, 128])
    sgb_h = dt("sgb", [L, 512])
    ln1_h = dt("ln1", [L * 2, D])
    ln2_h = dt("ln2", [L * 2, D])
    k1T_h = dt("k1T", [L * 128, 128])
    k2T_h = dt("k2T", [L * 128, 128])
    uT_h = dt("uT", [L * D, NE])
    vt_h = dt("vtab", [L * NE, D])
    out_h = nc.dram_tensor("out", [NTOK, D], F32, kind="ExternalOutput")
    DAP = lambda h, off, *dims: bass.AP(h, off, [list(d) for d in dims])

    with ExitStack() as st:
        def sb(name, F, dtype):
            return Buf(st.enter_context(nc.sbuf_tensor(name, [128, F], dtype)), F)

        htok = sb("htok", NTS * D, F32)
        hT = sb("hT", 8 * SEG, BF16)
        W = sb("W", 8 * INC, BF16)
        Areg = sb("Areg", 8 * D, BF16)
        ARENA_BYTES = 54 * 1024
        arena_t = st.enter_context(nc.sbuf_tensor("arena", [128, ARENA_BYTES // 2], BF16))
        arena_f = arena_t.bitcast(F32)
        lnG = sb("lnG", D, F32)
        lnB = sb("lnB", D, F32)
        xs = sb("xs", D, F32)
        yn = sb("yn", D, F32)
        tt = sb("tt", D, F32)
        hbf = sb("hbf", D, BF16)
        junk = hbf
        cst = sb("cstf", 640, F32)
        ident = sb("ident", 128, BF16)
        ones = sb("onesb", 128, BF16)
        stt_ = sb("stat", 64, F32)
        Sf = sb("Sf", L * 512, F32)
        Sb = sb("Sb", L * 512, BF16)
        small = sb("small", 128, F32)
        ps = [Buf(st.enter_context(nc.psum_tensor(f"ps{i}", [128, 512], F32)), 512) for i in range(8)]

        class Arena:
            def __init__(self):
                self.off = 0

            def alloc(self, nelem, dtype):
                sz = 4 if dtype == F32 else 2
                self.off = (self.off + 63) // 64 * 64
                o = self.off
                self.off += nelem * sz
                assert self.off <= ARENA_BYTES, ("arena overflow", self.off)
                if dtype == F32:
                    return Buf(arena_f, ARENA_BYTES // 4, o // 4)
                return Buf(arena_t, ARENA_BYTES // 2, o // 2)

        am = Arena()
        qTf = am.alloc(2048, BF16)
        kTf = am.alloc(2048, BF16)
        silur = am.alloc(2048, BF16)
        gsu = am.alloc(2048, BF16)
        aT = am.alloc(512, BF16)
        Lt = am.alloc(256, F32)
        eb = am.alloc(512, F32)
        enb = am.alloc(512, F32)
        erem = am.alloc(256, F32)
        qtil = am.alloc(512, BF16)
        ktil = am.alloc(512, BF16)
        khat = am.alloc(256, BF16)
        v_bf = am.alloc(512, BF16)
        attn_bf = am.alloc(512, BF16)
        sq = am.alloc(512, BF16)
        rstd = am.alloc(512, F32)
        t1 = am.alloc(512, F32)
        catT = am.alloc(1024, BF16)
        gv = am.alloc(512, F32)
        vn = am.alloc(512, BF16)
        sg_g = am.alloc(512, F32)
        sg_b = am.alloc(512, F32)
        WsT = am.alloc(512, BF16)
        wgu = am.alloc(256, BF16)
        bgate = am.alloc(256, BF16)
        gcol = am.alloc(1, F32)
        bs_f = am.alloc(512, F32)
        bs_hi = am.alloc(512, BF16)
        bs_lo = am.alloc(512, BF16)
        bs_t = am.alloc(512, F32)

        ap_ = Arena()
        qT = ap_.alloc(16 * 512, BF16)
        UT = ap_.alloc(2 * 4096, BF16)
        HT = ap_.alloc(2 * 512, BF16)
        gAs = ap_.alloc(8 * 512, BF16)
        Gb = ap_.alloc(2 * 512, BF16)
        K1T = ap_.alloc(128, BF16)
        K2T = ap_.alloc(128, BF16)
        c16 = ap_.alloc(512, F32)
        candb = ap_.alloc(256, F32)
        scr = ap_.alloc(256, F32)
        v16 = ap_.alloc(256, F32)
        PAB = Buf(yn.t.bitcast(BF16), 2 * D)
        Ssum = Buf(tt.t.bitcast(BF16), 2 * D)
        rawAT = Buf(xs.t.bitcast(BF16), 2 * D)
        Esl = ap_.alloc(3 * 512, BF16)
        Fsl = W.sub(16384)
        biasv = small.sub(0)
        negm = small.sub(32)
        Zs = small.sub(40)
        lnZ = small.sub(48)
        j16 = small.sub(64)
        sst = small.sub(96)

        P = Prog(nc)
        bank_ctr = [0]

        def bk():
            b = bank_ctr[0] % 8
            bank_ctr[0] += 1
            return b

        pr = lambda b: f"ps{b}"

        def mark(n):
            P.enabled = n <= cut

        dbg_col = [0]

        def dump(key, name, apfn, ncols, reads, np_=128):
            if dbg is None or tuple(dbg) != tuple(key) or not P.enabled:
                return
            c0 = dbg_col[0]
            dbg_col[0] += ncols
            DBG_MAP[name] = (c0, ncols, np_)
            P.op("pool", I("dma_start", out=bass.AP(dbg_h, c0, [[8192, np_], [1, ncols]]), in_=apfn()),
                 reads=reads, dma="dbg")

        P.op("sp", I("dma_start", out=cst.a(0, [1, 640]), in_=DAP(cst_h, 0, [640, 128], [1, 640])),
             writes=["cst"], dma="cst")
        P.op("pool", I("dma_start", out=ident.a(0, [1, 128]), in_=DAP(cst_h, 0, [640, 128], [1, 128])),
             writes=["ident"], dma="ident")
        P.op("pool", I("dma_start", out=ones.a(0, [1, 128]), in_=DAP(cst_h, 512, [640, 128], [1, 128])),
             writes=["ones"], dma="ones")
        P.op("dve", I("memset", Sf.a(0, [1, L * 512]), 0.0), writes=[f"Sf{l}" for l in range(L)])
        P.op("dve", I("memset", Sb.a(0, [1, L * 512]), 0.0), writes=[f"Sb{l}" for l in range(L)])
        CI, CTRI, CTRI2, CCAUS = 0, 128, 256, 384

        def load_ln_params(h, row0, scale):
            P.op("sp", I("dma_start", out=lnG.a(0, [1, D]), in_=DAP(h, row0 * D, [0, 128], [1, D])),
                 writes=["lnG"], dma="lnG")
            P.op("sp", I("dma_start", out=lnB.a(0, [1, D]), in_=DAP(h, (row0 + 1) * D, [0, 128], [1, D])),
                 writes=["lnB"], dma="lnB")
            if scale != 1.0:
                P.op("pool", I("tensor_scalar", out=lnG.a(0, [1, D]), in0=lnG.a(0, [1, D]), scalar1=scale,
                                                       scalar2=None, op0=ALU.mult), reads=["lnG"], writes=["lnG"])
                P.op("pool", I("tensor_scalar", out=lnB.a(0, [1, D]), in0=lnB.a(0, [1, D]), scalar1=scale,
                                                       scalar2=None, op0=ALU.mult), reads=["lnB"], writes=["lnB"])

        stat_ctr = [0]

        def stats_chain(sbuf, s0, n):
            c = lambda k: sbuf.a(s0 + k, [1, 1])
            r = "statchain"
            P.op("dve", I("tensor_scalar", out=c(2), in0=c(0), scalar1=1.0 / n, scalar2=None, op0=ALU.mult),
                 reads=[r], writes=[r])
            P.op("dve", I("tensor_tensor", out=c(3), in0=c(2), in1=c(2), op=ALU.mult), reads=[r], writes=[r])
            P.op("dve", I("scalar_tensor_tensor", out=c(4), in0=c(1), scalar=1.0 / n, in1=c(3),
                                                         op0=ALU.mult, op1=ALU.subtract), reads=[r], writes=[r])
            P.op("act", I("activation", out=c(5), in_=c(4), func=AF.Ln, bias=EPS), reads=[r], writes=[r])
            P.op("act", I("activation", out=c(6), in_=c(5), func=AF.Exp, scale=-0.5), reads=[r], writes=[r])
            P.op("dve", I("scalar_tensor_tensor", out=c(7), in0=c(2), scalar=-1.0, in1=c(6),
                                                         op0=ALU.mult, op1=ALU.mult), reads=[r], writes=[r])

        def layer_norm(src, src_res, ti, final, seg):
            s0 = (stat_ctr[0] % 4) * 8
            stat_ctr[0] += 1
            hres = f"htok{ti}"
            P.op("act", I("activation", out=junk.a(0, [1, D]), in_=src, func=AF.Identity,
                                               accum_out=stt_.a(s0, [1, 1])),
                 reads=[src_res], writes=["hbf", "statchain"])
            P.op("act", I("activation", out=junk.a(0, [1, D]), in_=src, func=AF.Square,
                                               accum_out=stt_.a(s0 + 1, [1, 1])),
                 reads=[src_res], writes=["hbf", "statchain"])
            stats_chain(stt_, s0, float(D))
            P.op("act", I("activation", out=yn.a(0, [1, D]), in_=src, func=AF.Identity,
                                               scale=stt_.a(s0 + 6, [1, 1]), bias=stt_.a(s0 + 7, [1, 1])),
                 reads=[src_res, "statchain"], writes=["yn"])
            P.op("dve", I("tensor_tensor", out=tt.a(0, [1, D]), in0=yn.a(0, [1, D]), in1=lnG.a(0, [1, D]),
                                                   op=ALU.mult), reads=["yn", "lnG"], writes=["tt"])
            P.op("dve", I("tensor_tensor", out=htok.a(ti * D, [1, D]), in0=tt.a(0, [1, D]),
                                                  in1=lnB.a(0, [1, D]), op=ALU.add),
                 reads=["tt", "lnB", src_res], writes=[hres])
            if final:
                row0 = seg * SEG + ti * 128
                P.op("sp", I("dma_start", out=DAP(out_h, row0 * D, [D, 128], [1, D]), in_=htok.a(ti * D, [1, D])),
                     reads=[hres], dma=f"out{ti}")
                return
            P.op("act", I("activation", out=hbf.a(0, [1, D]), in_=htok.a(ti * D, [1, D]), func=AF.Copy,
                                               scale=1.0 / ALPHA), reads=[hres], writes=["hbf"])
            for half in range(2):
                b = bk()
                for k4 in range(4):
                    kc = half * 4 + k4
                    P.op("pe", I("matmul", ps[b].a(k4 * 128, [1, 128]), hbf.a(kc * 128, [1, 128]), ident.a(0, [1, 128]),
                        start=True, stop=True), reads=["hbf", "ident"], writes=[pr(b)])
                eng = "act" if half == 0 else "dve"
                dst = hT.a((half * 4) * SEG + ti * 128, [SEG, 4], [1, 128])
                srcp = ps[b].a(0, [128, 4], [1, 128])
                if eng == "act":
                    P.op("act", I("copy", out=dst, in_=srcp), reads=[pr(b)], writes=[f"hT{ti}"])
                else:
                    P.op("dve", I("tensor_copy", out=dst, in_=srcp), reads=[pr(b)],
                         writes=[f"hT{ti}"])

        def mixer(seg, l):
            for c0, cw, rn in ((0, 1280, "Wa"), (1280, 1296, "Wb")):
                P.op("pool", I("dma_start", out=W.a(c0, [INC, 8], [1, cw]),
                    in_=DAP(win_h, l * D * INC + c0, [INC, 128], [128 * INC, 8], [1, cw])),
                    writes=[rn], dma=rn)

            def wres(c0, n):
                r = []
                if c0 < 1280:
                    r.append("Wa")
                if c0 + n > 1280:
                    r.append("Wb")
                return r
            P.op("pool", I("dma_start", out=Areg.a(0, [D, 8], [1, D]),
                                               in_=DAP(wout_h, l * D * D, [D, 128], [128 * D, 8], [1, D])),
                 writes=["A0", "A1"], dma="A")
            P.op("dve", I("memset", wgu.a(0, [1, 256], np_=32), 0.0), writes=["wgu"])
            P.op("dve", I("memset", bgate.a(0, [1, 256], np_=32), 0.0), writes=["bgate"])
            P.op("dve", I("memset", aT.a(0, [1, 512], np_=32), 0.0), writes=["aT"])
            P.op("dve", I("memset", bs_hi.a(0, [1, 512], np_=32), 0.0), writes=["bs_hi"])
            P.op("dve", I("memset", bs_lo.a(0, [1, 512], np_=32), 0.0), writes=["bs_lo"])
            P.op("pool", I("dma_start", out=wgu.a(0, [1, 256], np_=16), in_=DAP(wgu_h, l * 16 * 256, [256, 16], [1, 256])),
                 writes=["wgu"], dma="wgu")
            P.op("pool", I("dma_start", out=bgate.a(0, [1, 256], np_=1), in_=DAP(bgate_h, l * 256, [256, 1], [1, 256])),
                 writes=["bgate"], dma="bgate")
            P.op("pool", I("dma_start", out=WsT.a(0, [128, 4], [1, 128]),
                                               in_=DAP(sgwT_h, l * 4 * 128 * 128, [128, 128], [128 * 128, 4], [1, 128])),
                 writes=["WsT"], dma="WsT")
            P.op("sp", I("dma_start", out=gcol.a(0, [1, 1]), in_=DAP(glag_h, l * 128, [1, 128], [1, 1])),
                 writes=["gcol"], dma="gcol")
            P.op("sp", I("dma_start", out=sg_g.a(0, [1, 512]), in_=DAP(sgln_h, (2 * l) * 512, [0, 128], [1, 512])),
                 writes=["sg_g"], dma="sg_g")
            P.op("sp", I("dma_start", out=sg_b.a(0, [1, 512]), in_=DAP(sgln_h, (2 * l + 1) * 512, [0, 128], [1, 512])),
                 writes=["sg_b"], dma="sg_b")
            P.op("sp", I("dma_start", out=bs_f.a(0, [1, 512], np_=1), in_=DAP(sgb_h, l * 512, [512, 1], [1, 512])),
                 writes=["bs_f"], dma="bs_f")
            load_ln_params(ln1_h, 2 * l, ALPHA)
            P.op("dve", I("memset", WsT.a(0, [128, 4], [1, 64], p0=64, np_=64), 0.0), reads=["WsT"], writes=["WsT"])
            P.op("dve", I("tensor_copy", out=bs_hi.a(0, [1, 512], np_=1), in_=bs_f.a(0, [1, 512], np_=1)),
                 reads=["bs_f"], writes=["bs_hi"])
            P.op("dve", I("tensor_tensor", out=bs_t.a(0, [1, 512], np_=1), in0=bs_f.a(0, [1, 512], np_=1),
                                                  in1=bs_hi.a(0, [1, 512], np_=1), op=ALU.subtract),
                 reads=["bs_f", "bs_hi"], writes=["bs_t"])
            P.op("dve", I("tensor_copy", out=bs_lo.a(0, [1, 512], np_=1), in_=bs_t.a(0, [1, 512], np_=1)),
                 reads=["bs_t"], writes=["bs_lo"])

            def proj_fm(c0, m, q):
                b = bk()
                hres = [f"hT{4 * q + j}" for j in range(4)]
                for kc in range(8):
                    P.op("pe", I("matmul", ps[b].a(0, [1, 512], np_=m), W.a(kc * INC + c0, [1, m]), hT.a(kc * SEG + q * 512, [1, 512]),
                        start=(kc == 0), stop=(kc == 7)), reads=wres(c0, m) + hres, writes=[pr(b)])
                return b

            def proj_tm(c0, n, ti):
                b = bk()
                for kc in range(8):
                    P.op("pe", I("matmul", ps[b].a(0, [1, n]), hT.a(kc * SEG + ti * 128, [1, 128]), W.a(kc * INC + c0, [1, n]),
                        start=(kc == 0), stop=(kc == 7)), reads=wres(c0, n) + [f"hT{ti}"], writes=[pr(b)])
                return b

            for q in range(2):
                mark(1)
                for i in range(4):
                    b = proj_fm(i * 64, 64, q)
                    P.op("act", I("copy", out=qTf.a(i * 512, [1, 512], np_=64),
                                                           in_=ps[b].a(0, [1, 512], np_=64)),
                         reads=[pr(b)], writes=["qTf"])
                for i in range(4):
                    b = proj_fm(256 + i * 64, 64, q)
                    P.op("dve", I("tensor_copy", out=kTf.a(i * 512, [1, 512], np_=64),
                                                                  in_=ps[b].a(0, [1, 512], np_=64)),
                         reads=[pr(b)], writes=["kTf"])
                for i in range(4):
                    b = proj_fm(1024 + i * 128, 128, q)
                    P.op("act", I("activation", out=silur.a(i * 512, [1, 512]), in_=ps[b].a(0, [1, 512]),
                                                                 func=AF.Silu), reads=[pr(b)], writes=["silur"])
                for i in range(4):
                    b = proj_fm(1552 + i * 128, 128, q)
                    P.op("act", I("activation", out=gsu.a(i * 512, [1, 512]), in_=ps[b].a(0, [1, 512]),
                                                                 func=AF.Gelu), reads=[pr(b)], writes=["gsu"])
                b = proj_fm(1536, 16, q)
                P.op("act", I("copy", out=aT.a(0, [1, 512], np_=16), in_=ps[b].a(0, [1, 512], np_=16)),
                     reads=[pr(b)], writes=["aT"])

                def tile_body(j, part):
                    ti = 4 * q + j
                    tc0 = j * 128
                    key = (seg, l, ti)
                    if part == 0:
                        mark(2)
                        b = bk()
                        P.op("pe", I("matmul", ps[b].a(0, [1, 256]), aT.a(tc0, [1, 128], np_=32),
                                                           wgu.a(0, [1, 256], np_=32), start=True, stop=False),
                             reads=["aT", "wgu"], writes=[pr(b)])
                        P.op("pe", I("matmul", ps[b].a(0, [1, 256]), ones.a(0, [1, 128], np_=32),
                                                           bgate.a(0, [1, 256], np_=32), start=False, stop=True),
                             reads=["ones", "bgate"], writes=[pr(b)])
                        P.op("act", I("activation", out=Lt.a(0, [1, 256]), in_=ps[b].a(0, [1, 256]), func=AF.Exp,
                                                                scale=-1.0), reads=[pr(b)], writes=["Lt"])
                        dump(key, "aT", lambda: aT.a(0, [1, 512], np_=16), 512, ["aT"], np_=16)
                        dump(key, "wgu", lambda: wgu.a(0, [1, 256], np_=16), 256, ["wgu"], np_=16)
                        dump(key, "bgate", lambda: bgate.a(0, [1, 256], np_=1), 256, ["bgate"], np_=1)
                        dump(key, "expnx", lambda: Lt.a(0, [1, 256]), 256, ["Lt"])
                        P.op("act", I("activation", out=Lt.a(0, [1, 256]), in_=Lt.a(0, [1, 256]), func=AF.Ln, bias=1.0),
                             reads=["Lt"], writes=["Lt"])
                        mark(3)
                        dump(key, "Lt", lambda: Lt.a(0, [1, 256]), 256, ["Lt"])
                        b2 = bk()
                        for h in range(4):
                            P.op("pe", I("matmul", ps[b2].a(h * 128, [1, 128], np_=64), Lt.a(h * 64, [1, 64]),
                                                                      cst.a(CTRI, [1, 128]), start=True, stop=True),
                                 reads=["Lt", "cst"], writes=[pr(b2)])
                        b3 = bk()
                        P.op("pe", I("matmul", ps[b3].a(0, [1, 256]), cst.a(CTRI2, [1, 128]), Lt.a(0, [1, 256]),
                                                             start=True, stop=True), reads=["Lt", "cst"], writes=[pr(b3)])
                        P.op("act", I("activation", out=eb.a(0, [1, 512], np_=64), in_=ps[b2].a(0, [1, 512], np_=64),
                                                                  func=AF.Exp), reads=[pr(b2)], writes=["eb"])
                        P.op("act", I("activation", out=enb.a(0, [1, 512], np_=64), in_=ps[b2].a(0, [1, 512], np_=64),
                                                                  func=AF.Exp, scale=-1.0), reads=[pr(b2)], writes=["enb"])
                        P.op("act", I("activation", out=erem.a(0, [1, 256]), in_=ps[b3].a(0, [1, 256]), func=AF.Exp),
                             reads=[pr(b3)], writes=["erem"])
                        P.op("dve", I("scalar_tensor_tensor", out=qtil.a(0, [128, 4], [1, 128], np_=64), in0=qTf.a(tc0, [512, 4], [1, 128], np_=64), scalar=0.125,
                            in1=eb.a(0, [128, 4], [1, 128], np_=64), op0=ALU.mult, op1=ALU.mult),
                            reads=["qTf", "eb"], writes=["qtil"])
                        P.op("dve", I("tensor_tensor", out=ktil.a(0, [128, 4], [1, 128], np_=64), in0=kTf.a(tc0, [512, 4], [1, 128], np_=64),
                            in1=enb.a(0, [128, 4], [1, 128], np_=64), op=ALU.mult), reads=["kTf", "enb"], writes=["ktil"])
                        dump(key, "eb", lambda: eb.a(0, [1, 512], np_=64), 512, ["eb"], np_=64)
                        dump(key, "erem", lambda: erem.a(0, [1, 256]), 256, ["erem"])
                        dump(key, "qtil", lambda: qtil.a(0, [1, 512], np_=64), 512, ["qtil"], np_=64)
                        dump(key, "ktil", lambda: ktil.a(0, [1, 512], np_=64), 512, ["ktil"], np_=64)
                        mark(4)
                        b4 = proj_tm(256, 256, ti)
                        mark(4.02)
                        P.op("dve", I("tensor_tensor", out=khat.a(0, [1, 256]), in0=ps[b4].a(0, [1, 256]),
                                                                     in1=erem.a(0, [1, 256]), op=ALU.mult),
                             reads=[pr(b4), "erem"], writes=["khat"])
                        mark(4.03)
                        b5 = proj_tm(512, 512, ti)
                        mark(4.04)
                        P.op("act", I("copy", out=v_bf.a(0, [1, 512]), in_=ps[b5].a(0, [1, 512])),
                             reads=[pr(b5)], writes=["v_bf"])
                    if part == 1:
                        mark(4.2)
                        b6 = bk()
                        for h in range(4):
                            c, pb = h // 2, (h % 2) * 64
                            P.op("pe", I("matmul", ps[b6].a(h * 128, [1, 128]), ktil.a(h * 128, [1, 128], np_=64),
                                qtil.a(h * 128, [1, 128], np_=64), start=True, stop=True),
                                reads=["ktil", "qtil"], writes=[pr(b6)])
                        mark(4.4)
                        P.op("dve", I("tensor_tensor", out=attn_bf.a(0, [128, 4], [1, 128]), in0=ps[b6].a(0, [128, 4], [1, 128]),
                            in1=cst.a(CCAUS, [0, 4], [1, 128]), op=ALU.mult), reads=[pr(b6), "cst"], writes=["attn_bf"])
                        dump(key, "khat", lambda: khat.a(0, [1, 256]), 256, ["khat"])
                        dump(key, "v_bf", lambda: v_bf.a(0, [1, 512]), 512, ["v_bf"])
                        dump(key, "attn", lambda: attn_bf.a(0, [1, 512]), 512, ["attn_bf"])
                        mark(4.6)
                        b7 = bk()
                        for h in range(4):
                            c, pb = h // 2, (h % 2) * 64
                            P.op("pe", I("matmul", ps[b7].a(h * 128, [1, 128]), v_bf.a(h * 128, [1, 128]), attn_bf.a(h * 128, [1, 128]),
                                start=True, stop=False), reads=["v_bf", "attn_bf"], writes=[pr(b7)])
                            P.op("pe", I("matmul", ps[b7].a(h * 128, [1, 128]), Sb.a(l * 512 + h * 128, [1, 128], np_=64),
                                qtil.a(h * 128, [1, 128], np_=64), start=False, stop=True),
                                reads=[f"Sb{l}", "qtil"], writes=[pr(b7)])
                        mark(5)
                        b8 = bk()
                        for h in range(4):
                            P.op("pe", I("matmul", ps[b8].a(h * 128, [1, 128], np_=64), khat.a(h * 64, [1, 64]), v_bf.a(h * 128, [1, 128]),
                                start=True, stop=True), reads=["khat", "v_bf"], writes=[pr(b8)])
                        for h in range(4):
                            so = l * 512 + h * 128
                            P.op("dve", I("scalar_tensor_tensor", out=Sf.a(so, [1, 128], np_=64), in0=Sf.a(so, [1, 128], np_=64),
                                scalar=eb.a(h * 128 + 127, [1, 1], np_=64),
                                in1=ps[b8].a(h * 128, [1, 128], np_=64),
                                op0=ALU.mult, op1=ALU.add), reads=[f"Sf{l}", "eb", pr(b8)], writes=[f"Sf{l}"])
                        P.op("act", I("copy", out=Sb.a(l * 512, [1, 512], np_=64), in_=Sf.a(l * 512, [1, 512], np_=64)),
                             reads=[f"Sf{l}"], writes=[f"Sb{l}"])
                        mark(6)
                        P.op("act", I("activation", out=sq.a(0, [1, 512]), in_=ps[b7].a(0, [1, 512]), func=AF.Square),
                             reads=[pr(b7)], writes=["sq"])
                        b9 = bk()
                        P.op("pe", I("matmul", ps[b9].a(0, [1, 512]), ones.a(0, [1, 128]), sq.a(0, [1, 512]),
                                                             start=True, stop=True), reads=["ones", "sq"], writes=[pr(b9)])
                        P.op("act", I("activation", out=rstd.a(0, [1, 512]), in_=ps[b9].a(0, [1, 512]), func=AF.Ln,
                                                              scale=1.0 / 128.0, bias=EPS), reads=[pr(b9)], writes=["rstd"])
                        P.op("act", I("activation", out=rstd.a(0, [1, 512]), in_=rstd.a(0, [1, 512]), func=AF.Exp, scale=-0.5),
                             reads=["rstd"], writes=["rstd"])
                        P.op("dve", I("scalar_tensor_tensor", out=t1.a(0, [1, 512]), in0=ps[b7].a(0, [1, 512]), scalar=gcol.a(0, [1, 1]), in1=rstd.a(0, [1, 512]),
                            op0=ALU.mult, op1=ALU.mult), reads=[pr(b7), "gcol", "rstd"], writes=["t1"])
                        P.op("dve", I("tensor_tensor", out=catT.a(0, [128, 4], [1, 128]), in0=t1.a(0, [128, 4], [1, 128]),
                            in1=silur.a(tc0, [512, 4], [1, 128]), op=ALU.mult), reads=["t1", "silur"], writes=["catT_o"])
                        dump(key, "t1", lambda: t1.a(0, [1, 512]), 512, ["t1"])
                        dump(key, "Sf", lambda: Sf.a(l * 512, [1, 512], np_=64), 512, [f"Sf{l}"], np_=64)
                    if part == 2:
                        mark(7)
                        b10 = proj_tm(2064, 512, ti)
                        P.op("act", I("activation", out=gv.a(0, [1, 512]), in_=ps[b10].a(0, [1, 512]), func=AF.Gelu,
                                                                    accum_out=sst.a(0, [1, 1])),
                             reads=[pr(b10)], writes=["gv", "statchain"])
                        P.op("act", I("activation", out=junk.a(0, [1, 512]), in_=gv.a(0, [1, 512]), func=AF.Square,
                                                           accum_out=sst.a(1, [1, 1])),
                             reads=["gv"], writes=["hbf", "statchain"])
                        stats_chain(sst, 0, 512.0)
                        P.op("act", I("activation", out=gv.a(0, [1, 512]), in_=gv.a(0, [1, 512]), func=AF.Identity,
                                                           scale=sst.a(6, [1, 1]), bias=sst.a(7, [1, 1])),
                             reads=["gv", "statchain"], writes=["gv"])
                        P.op("dve", I("tensor_tensor", out=gv.a(0, [1, 512]), in0=gv.a(0, [1, 512]), in1=sg_g.a(0, [1, 512]),
                                                               op=ALU.mult), reads=["gv", "sg_g"], writes=["gv"])
                        P.op("dve", I("tensor_tensor", out=vn.a(0, [1, 512]), in0=gv.a(0, [1, 512]), in1=sg_b.a(0, [1, 512]),
                                                              op=ALU.add), reads=["gv", "sg_b"], writes=["vn"])
                        b11 = bk()
                        for g in range(4):
                            P.op("pe", I("matmul", ps[b11].a(g * 128, [1, 128]), vn.a(g * 128, [1, 128]), WsT.a(g * 128, [1, 128]),
                                start=True, stop=False), reads=["vn", "WsT"], writes=[pr(b11)])
                            P.op("pe", I("matmul", ps[b11].a(g * 128, [1, 128]), ones.a(0, [1, 128], np_=32), bs_hi.a(g * 128, [1, 128], np_=32),
                                start=False, stop=False), reads=["ones", "bs_hi"], writes=[pr(b11)])
                            P.op("pe", I("matmul", ps[b11].a(g * 128, [1, 128]), ones.a(0, [1, 128], np_=32), bs_lo.a(g * 128, [1, 128], np_=32),
                                start=False, stop=True), reads=["ones", "bs_lo"], writes=[pr(b11)])
                        P.op("dve", I("tensor_tensor", out=catT.a(512, [128, 4], [1, 128]), in0=ps[b11].a(0, [128, 4], [1, 128]),
                            in1=gsu.a(tc0, [512, 4], [1, 128]), op=ALU.mult), reads=[pr(b11), "gsu"], writes=["catT_g"])
                        dump(key, "vn", lambda: vn.a(0, [1, 512]), 512, ["vn"])
                        dump(key, "catT", lambda: catT.a(0, [1, 1024]), 1024, ["catT_o", "catT_g"])
                        mark(8)
                        for dh in range(2):
                            b12 = bk()
                            for cc in range(8):
                                P.op("pe", I("matmul", ps[b12].a(0, [1, 512]), catT.a(cc * 128, [1, 128]), Areg.a(cc * D + dh * 512, [1, 512]),
                                    start=(cc == 0), stop=(cc == 7)),
                                    reads=["catT_o", "catT_g", "A0", "A1"], writes=[pr(b12)])
                            P.op("dve", I("tensor_tensor", out=htok.a(ti * D + dh * 512, [1, 512]), in0=ps[b12].a(0, [1, 512]),
                                in1=htok.a(ti * D + dh * 512, [1, 512]), op=ALU.add),
                                reads=[pr(b12), f"htok{ti}"], writes=[f"htok{ti}"])
                        dump(key, "hpre", lambda: htok.a(ti * D, [1, D]), 1024, [f"htok{ti}"])
                    if part == 3:
                        layer_norm(htok.a(ti * D, [1, D]), f"htok{ti}", ti, False, seg)

                order = [(0, 0), (0, 1), (1, 0), (0, 2), (1, 1), (2, 0), (1, 2), (2, 1), (3, 0), (2, 2), (3, 1), (3, 2)]
                pending = []
                for (jj, part) in order:
                    P.capture = []
                    tile_body(jj, part)
                    ops = P.capture
                    P.capture = None
                    step = max(1, len(ops) // max(1, len(pending))) if pending else 0
                    for i, o in enumerate(ops):
                        P.op(*o)
                        if pending and (i + 1) % step == 0:
                            P.op(*pending.pop(0))
                    while pending:
                        P.op(*pending.pop(0))
                    if part == 2:
                        P.capture = []
                        tile_body(jj, 3)
                        pending = P.capture
                        P.capture = None
                while pending:
                    P.op(*pending.pop(0))

        def peer(seg, l, final):
            for g4 in range(4):
                P.op("pool", I("dma_start", out=W.a(g4 * 512, [2048, 8], [1, 512]),
                               in_=DAP(wq_h, l * D * 2048 + g4 * 512, [2048, 128], [128 * 2048, 8], [1, 512])),
                     writes=[f"Wq{g4}"], dma=f"Wq{g4}")
            P.op("pool", I("dma_start", out=K1T.a(0, [1, 128]), in_=DAP(k1T_h, l * 128 * 128, [128, 128], [1, 128])),
                 writes=["K1T"], dma="K1T")
            P.op("pool", I("dma_start", out=K2T.a(0, [1, 128]), in_=DAP(k2T_h, l * 128 * 128, [128, 128], [1, 128])),
                 writes=["K2T"], dma="K2T")
            load_ln_params(ln2_h, 2 * l, 1.0 if final else ALPHA)
            cgi = [0]

            def load_ut(cg):
                s = cg % 2
                P.op("pool", I("dma_start",
                    out=UT.a(s * 4096, [512, 8], [1, 512]),
                    in_=DAP(uT_h, l * D * NE + cg * 512, [NE, 128], [128 * NE, 8], [1, 512])),
                    writes=[f"UT{s}"], dma=f"UT{s}")

            def load_v(cg):
                s = cg % 2
                P.op("pool", I("dma_start",
                    out=Areg.a(s * 4096, [D, 4], [1, D]),
                    in_=DAP(vt_h, (l * NE + cg * 512) * D, [D, 128], [128 * D, 4], [1, D])),
                    writes=[f"A{s}"], dma=f"V{s}")

            for blk in range(2):
                t0 = blk * 4
                bc0 = blk * 512
                hres = [f"hT{t0 + j}" for j in range(4)]
                for jq in range(16):
                    b = bk()
                    for kc in range(8):
                        P.op("pe", I("matmul", ps[b].a(0, [1, 512]), W.a(kc * 2048 + jq * 128, [1, 128]), hT.a(kc * SEG + bc0, [1, 512]),
                            start=(kc == 0), stop=(kc == 7)), reads=[f"Wq{jq // 4}"] + hres, writes=[pr(b)])
                    if jq % 2 == 0:
                        P.op("act", I("copy", out=qT.a(jq * 512, [1, 512]), in_=ps[b].a(0, [1, 512])),
                             reads=[pr(b)], writes=["qT"])
                    else:
                        P.op("dve", I("tensor_copy", out=qT.a(jq * 512, [1, 512]), in_=ps[b].a(0, [1, 512])),
                             reads=[pr(b)], writes=["qT"])
                for j in range(4):
                    tc0 = j * 128
                    banks = [bk() for _ in range(4)]
                    for jq in range(16):
                        b = banks[jq // 4]
                        KT = K1T if jq % 2 == 0 else K2T
                        P.op("pe", I("matmul", ps[b].a((jq % 4) * 128, [1, 128]), qT.a(jq * 512 + tc0, [1, 128]), KT.a(0, [1, 128]),
                            start=True, stop=True), reads=["qT", "K1T", "K2T"], writes=[pr(b)])
                    for jq in range(16):
                        b = banks[jq // 4]
                        sl = ps[b].a((jq % 4) * 128, [1, 128])
                        P.op("dve", I("max", out=v16.a(jq * 16, [1, 8]), in_=sl),
                             reads=[pr(b)], writes=["v16"])
                        P.op("dve", I("match_replace", out=scr.a(0, [1, 128]), in_to_replace=v16.a(jq * 16, [1, 8]), in_values=sl, imm_value=-1e30),
                            reads=[pr(b), "v16"], writes=["scr"])
                        P.op("dve", I("max", out=v16.a(jq * 16 + 8, [1, 8]), in_=scr.a(0, [1, 128])),
                             reads=["scr"], writes=["v16"])
                    cres = f"c16_{j}"
                    for h in range(8):
                        co = j * 128 + h * 16
                        P.op("dve", I("tensor_tensor", out=candb.a(0, [16, 16], [1, 16]), in0=v16.a(2 * h * 16, [1, 16], [0, 16]),
                            in1=v16.a((2 * h + 1) * 16, [0, 16], [1, 16]), op=ALU.add),
                            reads=["v16"], writes=["candb"])
                        P.op("dve", I("max", out=c16.a(co, [1, 8]), in_=candb.a(0, [1, 256])),
                             reads=["candb"], writes=[cres])
                        P.op("dve", I("match_replace", out=scr.a(0, [1, 256]), in_to_replace=c16.a(co, [1, 8]), in_values=candb.a(0, [1, 256]),
                            imm_value=-1e30), reads=["candb", cres], writes=["scr"])
                        P.op("dve", I("max", out=c16.a(co + 8, [1, 8]), in_=scr.a(0, [1, 256])),
                             reads=["scr"], writes=[cres])
                    P.op("dve", I("tensor_scalar", out=negm.a(0, [1, 8]), in0=c16.a(j * 128, [16, 8]), scalar1=-1.0,
                                                          scalar2=None, op0=ALU.mult), reads=[cres], writes=["negm"])
                    for h in range(8):
                        co = j * 128 + h * 16
                        P.op("act", I("activation", out=j16.a(0, [1, 16]), in_=c16.a(co, [1, 16]), func=AF.Exp, bias=negm.a(h, [1, 1]),
                            accum_out=Zs.a(h, [1, 1])), reads=[cres, "negm"], writes=["j16", "Zs"])
                    P.op("act", I("activation", out=lnZ.a(0, [1, 8]), in_=Zs.a(0, [1, 8]), func=AF.Ln),
                         reads=["Zs"], writes=["lnZ"])
                    P.op("dve", I("tensor_tensor", out=biasv.a(j * 8, [1, 8]), in0=negm.a(0, [1, 8]),
                                                          in1=lnZ.a(0, [1, 8]), op=ALU.subtract),
                         reads=["negm", "lnZ"], writes=[f"biasv{j}"])
                NY = 4
                YB = [0, 1, 2, 4]

                def emit_Y(idx, cg, j, h):
                    yb = YB[idx % NY]
                    tc0 = j * 128
                    P.op("pe", I("matmul", ps[yb].a(0, [1, 512]), qT.a((2 * h + 1) * 512 + tc0, [1, 128]),
                                 K2T.a(0, [0, 4], [1, 128]), start=True, stop=False),
                         reads=["qT", "K2T"], writes=[pr(yb)])
                    P.op("pe", I("matmul", ps[yb].a(0, [1, 512]), qT.a((2 * h) * 512 + tc0, [1, 128]),
                                 K1T.a(cg * 4, [1, 4], [0, 128]), start=False, stop=True),
                         reads=["qT", "K1T"], writes=[pr(yb)])

                def emit_EF(idx, cg, j, h):
                    yb = YB[idx % NY]
                    es = idx % 3
                    fs = idx % 8
                    P.op("act", I("activation", out=Esl.a(es * 512, [1, 512]), in_=ps[yb].a(0, [1, 512]), func=AF.Exp,
                                  bias=biasv.a(j * 8 + h, [1, 1])), reads=[pr(yb), f"biasv{j}"], writes=[f"E{es}"])
                    P.op("dve", I("scalar_tensor_tensor", out=Fsl.a(fs * 512, [1, 512]), in0=ps[yb].a(0, [1, 512]),
                                  scalar=c16.a(j * 128 + h * 16 + 15, [1, 1]), in1=Esl.a(es * 512, [1, 512]),
                                  op0=ALU.is_ge, op1=ALU.mult), reads=[pr(yb), f"c16_{j}", f"E{es}"], writes=[f"F{fs}"])

                def emit_T(idx, cg, j, h):
                    if h == 3:
                        P.op("pool", I("tensor_tensor", out=PAB.a(0, [1, 1024]), in0=Fsl.a(0, [1, 1024]),
                                       in1=Fsl.a(1024, [1, 1024]), op=ALU.add),
                             reads=["F0", "F1", "F2", "F3"], writes=["yn"])
                    elif h == 7:
                        P.op("pool", I("tensor_tensor", out=PAB.a(1024, [1, 1024]), in0=Fsl.a(2048, [1, 1024]),
                                       in1=Fsl.a(3072, [1, 1024]), op=ALU.add),
                             reads=["F4", "F5", "F6", "F7"], writes=["yn"])

                def tail0(cg, j):
                    P.op("pool", I("tensor_tensor", out=Ssum.a(0, [1, 1024]), in0=PAB.a(0, [1, 1024]),
                                   in1=PAB.a(1024, [1, 1024]), op=ALU.add), reads=["yn"], writes=["tt"])

                def emit_AT_mm(cg, c, kc):
                    s = cg % 2
                    P.op("pe", I("matmul", ps[5].a(0, [1, 512]), UT.a(s * 4096 + kc * 512 + c * 128, [1, 128]),
                                 hT.a(kc * SEG + bc0, [1, 512]), start=(kc == 0), stop=(kc == 7)),
                         reads=[f"UT{s}"] + hres, writes=[pr(5)])

                def emit_AT_copy(c):
                    P.op("act", I("copy", out=rawAT.a(c * 512, [1, 512]), in_=ps[5].a(0, [1, 512])),
                         reads=[pr(5)], writes=["xs"])

                def emit_AT_gelu(cg):
                    ga = cg % 2
                    P.op("act", I("activation", out=gAs.a(ga * 4 * 512, [1, 2048]), in_=rawAT.a(0, [1, 2048]),
                                  func=AF.Gelu), reads=["xs"], writes=[f"gA{ga}_{c}" for c in range(4)])

                octr = [0]

                def tail1(cg, j):
                    tcn = cg * 4 + j
                    gb = 3
                    hs = tcn % 2
                    for c in range(4):
                        for half in range(2):
                            P.op("pe", I("matmul", ps[gb].a(c * 128, [1, 128]), Ssum.a(half * 512 + c * 128, [1, 128]),
                                         ident.a(0, [1, 128]), start=(half == 0), stop=(half == 1)),
                                 reads=["tt", "ident"], writes=[pr(gb)])

                def tail2(cg, j):
                    s = cg % 2
                    ga = cg % 2
                    tcn = cg * 4 + j
                    gb = 3
                    hs = tcn % 2
                    P.op("dve", I("tensor_tensor", out=HT.a(hs * 512, [128, 4], [1, 128]),
                                  in0=ps[gb].a(0, [128, 4], [1, 128]),
                                  in1=gAs.a(ga * 4 * 512 + j * 128, [512, 4], [1, 128]), op=ALU.mult),
                         reads=[pr(gb)] + [f"gA{ga}_{c}" for c in range(4)], writes=[f"HT{hs}"])

                def tail_mm(cg, j, c):
                    s = cg % 2
                    hs = (cg * 4 + j) % 2
                    for dh in range(2):
                        ob = 6 + dh
                        P.op("pe", I("matmul", ps[ob].a(0, [1, 512]), HT.a(hs * 512 + c * 128, [1, 128]),
                                     Areg.a(s * 4096 + c * D + dh * 512, [1, 512]), start=(c == 0), stop=(c == 3)),
                             reads=[f"HT{hs}", f"A{s}"], writes=[pr(ob)])

                def tail3(cg, j, dh):
                    ti = t0 + j
                    ob = 6 + dh
                    P.op("dve", I("tensor_tensor", out=htok.a(ti * D + dh * 512, [1, 512]), in0=ps[ob].a(0, [1, 512]),
                                  in1=htok.a(ti * D + dh * 512, [1, 512]), op=ALU.add),
                         reads=[pr(ob), f"htok{ti}"], writes=[f"htok{ti}"])

                NCG = 32
                load_ut(0)
                load_ut(1)
                load_v(0)
                for c in range(4):
                    for kc in range(8):
                        emit_AT_mm(0, c, kc)
                    emit_AT_copy(c)
                emit_AT_gelu(0)
                seq = [(cg, j, h) for cg in range(NCG) for j in range(4) for h in range(8)]
                for k in range(NY - 1):
                    emit_Y(k, *seq[k])
                pend_copy = None
                pend_gelu = None
                for idx, (cg, j, h) in enumerate(seq):
                    if pend_copy is not None:
                        emit_AT_copy(pend_copy)
                        pend_copy = None
                        if pend_gelu is not None:
                            emit_AT_gelu(pend_gelu)
                            pend_gelu = None
                    if j == 0 and h == 0 and cg + 2 < NCG:
                        load_ut(cg + 2)
                    if idx >= 8:
                        pcg, pj, _ = seq[idx - 8]
                        if h == 0:
                            tail0(pcg, pj)
                        elif h == 4:
                            tail1(pcg, pj)
                        elif h == 5:
                            tail2(pcg, pj)
                        elif h >= 6:
                            tail_mm(pcg, pj, h - 6)
                    if idx >= 16:
                        ppcg, ppj, _ = seq[idx - 16]
                        if h < 2:
                            tail_mm(ppcg, ppj, h + 2)
                        elif h < 4:
                            tail3(ppcg, ppj, h - 2)
                    if j == 1 and h == 4 and cg + 1 < NCG:
                        load_v(cg + 1)
                    emit_EF(idx, cg, j, h)
                    emit_T(idx, cg, j, h)
                    if cg + 1 < NCG:
                        emit_AT_mm(cg + 1, j, h)
                        if h == 7:
                            pend_copy = j
                            if j == 3:
                                pend_gelu = cg + 1
                    if idx + NY - 1 < len(seq):
                        emit_Y(idx + NY - 1, *seq[idx + NY - 1])
                tail_mm(NCG - 1, 2, 2)
                tail_mm(NCG - 1, 2, 3)
                tail3(NCG - 1, 2, 0)
                tail3(NCG - 1, 2, 1)
                tail0(NCG - 1, 3)
                tail1(NCG - 1, 3)
                tail2(NCG - 1, 3)
                for c in range(4):
                    tail_mm(NCG - 1, 3, c)
                tail3(NCG - 1, 3, 0)
                tail3(NCG - 1, 3, 1)
                for j in range(4):
                    ti = t0 + j
                    layer_norm(htok.a(ti * D, [1, D]), f"htok{ti}", ti, final, seg)

        for seg in range(nseg):
            load_ln_params(lnin_h, 0, ALPHA)
            for ti in range(NTS):
                row0 = seg * SEG + ti * 128
                P.op("sp", I("dma_start", out=xs.a(0, [1, D]), in_=DAP(x_h, row0 * D, [D, 128], [1, D])),
                     writes=["xs"], dma="xs")
                layer_norm(xs.a(0, [1, D]), "xs", ti, stop == "ln_in", seg)
            if stop == "ln_in":
                continue
            for l in range(nlayers):
                P.barrier()
                mixer(seg, l)
                mark(0)
                P.barrier()
                if stop == "mixer" and l == nlayers - 1:
                    for ti in range(NTS):
                        row0 = seg * SEG + ti * 128
                        P.op("sp", I("dma_start", out=DAP(out_h, row0 * D, [D, 128], [1, D]), in_=htok.a(ti * D, [1, D])),
                            reads=[f"htok{ti}"], dma=f"out{ti}")
                    continue
                peer(seg, l, final=(l == nlayers - 1))
        P.barrier()
        P.emit(st)
    return nc


def _consts():
    c = np.zeros((128, 5, 128), np.float32)
    s = np.arange(128)[:, None]
    t = np.arange(128)[None, :]
    c[:, 0] = np.eye(128)
    c[:, 1] = np.where(s <= t, -1.0 / 16.0, 0.0)
    c[:, 2] = np.where(s > t, -1.0 / 16.0, 0.0)
    c[:, 3] = np.where(s <= t, 1.0, 0.0)
    c[:, 4] = 1.0
    return np.ascontiguousarray(c.reshape(128, 640))


def prep_shared(inp):
    f = lambda a: np.ascontiguousarray(np.asarray(a, dtype=np.float32))
    sh = {
        "cst": _consts(),
        "lnin": f(np.stack([inp["ln_in_g"], inp["ln_in_b"]])),
        "w_in": f(np.asarray(inp["w_in"]).reshape(L * D, INC)),
        "w_out": f(np.asarray(inp["w_out"]).reshape(L * D, D)),
        "wq": f(np.asarray(inp["peer_wq"]).reshape(L * D, 2048)),
        "wgu": f(np.asarray(inp["w_gate_up"]).reshape(L * 16, 256)),
        "bgate": f(inp["b_gate"]),
        "glag": f(inp["gla_norm_g"]),
        "sgln": f(np.stack([np.asarray(inp["sgu_ln_g"]), np.asarray(inp["sgu_ln_b"])], axis=1).reshape(L * 2, 512)),
        "sgwT": f(np.asarray(inp["sgu_w"]).transpose(0, 1, 3, 2).reshape(L * 4 * 128, 128)),
        "sgb": f(np.asarray(inp["sgu_b"]).reshape(L, 512)),
        "ln1": f(np.stack([np.asarray(inp["ln1_g"]), np.asarray(inp["ln1_b"])], axis=1).reshape(L * 2, D)),
        "ln2": f(np.stack([np.asarray(inp["ln2_g"]), np.asarray(inp["ln2_b"])], axis=1).reshape(L * 2, D)),
        "k1T": f(np.asarray(inp["peer_k1"]).transpose(0, 2, 1).reshape(L * 128, 128)),
        "k2T": f(np.asarray(inp["peer_k2"]).transpose(0, 2, 1).reshape(L * 128, 128)),
        "uT": f(np.asarray(inp["peer_u"]).transpose(0, 2, 1).reshape(L * D, NE)),
        "vtab": f(np.asarray(inp["peer_v"]).reshape(L * NE, D)),
    }
    return sh


def kernel(**inputs):
    x = np.asarray(inputs["x"], dtype=np.float32)
    nb = x.shape[0]
    sh = prep_shared(inputs)
    nc = build()
    in_maps = []
    for b in range(nb):
        m = dict(sh)
        m["x"] = np.ascontiguousarray(x[b])
        in_maps.append(m)
    res = run_bass_kernel_spmd(nc, in_maps, core_ids=list(range(nb)))
    return np.stack([np.asarray(r["out"]) for r in res.results], axis=0).astype(np.float32)
```

```python
import numpy as np
from contextlib import ExitStack
import concourse.bass as bass
import concourse.mybir as mybir
from concourse.bass_utils import run_bass_kernel_spmd

F32 = mybir.dt.float32
BF16 = mybir.dt.bfloat16
AF = mybir.ActivationFunctionType
ALU = mybir.AluOpType

D = 1024
NTOK = 2048
L = 2
INC = 2576
SEG = 1024
NTS = 8
NE = 16384
ALPHA = (2.0 * L) ** 0.25
EPS = 1e-5
ENGS = ["pe", "act", "dve", "pool", "sp"]


class Prog:
    def __init__(self, nc, same_engine_sync=True):
        self.nc = nc
        self.streams = {e: [] for e in ENGS}
        self.cnt = {e: 0 for e in ENGS}
        self.dma_cnt = {}
        self.lastw = {}
        self.readers = {}
        self.seen = {e: {} for e in ENGS}
        self.same_engine_sync = same_engine_sync

    def _need(self, eng, tok, waits):
        if tok is None:
            return
        sem, val = tok
        if sem == "E_" + eng and (eng == "pe" or not self.same_engine_sync):
            return
        if self.seen[eng].get(sem, 0) >= val:
            return
        if waits.get(sem, 0) < val:
            waits[sem] = val

    enabled = True
    capture = None

    def op(self, eng, fn, reads=(), writes=(), dma=None):
        if not self.enabled:
            return None
        if self.capture is not None:
            self.capture.append((eng, fn, tuple(reads), tuple(writes), dma))
            return None
        waits = {}
        for r in reads:
            self._need(eng, self.lastw.get(r), waits)
        for w in writes:
            self._need(eng, self.lastw.get(w), waits)
            for sem, val in self.readers.get(w, {}).items():
                self._need(eng, (sem, val), waits)
        for sem, val in waits.items():
            self.seen[eng][sem] = val
        if dma is None:
            self.cnt[eng] += 1
            tok = ("E_" + eng, self.cnt[eng])
        else:
            self.dma_cnt[dma] = self.dma_cnt.get(dma, 0) + 16
            tok = ("D_" + dma, self.dma_cnt[dma])
        self.streams[eng].append((fn, waits, tok))
        for r in reads:
            d = self.readers.setdefault(r, {})
            if d.get(tok[0], 0) < tok[1]:
                d[tok[0]] = tok[1]
        for w in writes:
            self.lastw[w] = tok
            self.readers[w] = {}
        return tok

    def grab(self, fn, *args):
        self.capture = []
        fn(*args)
        ops = self.capture
        self.capture = None
        return ops

    def emit_merged(self, a, b):
        na, nb = len(a), len(b)
        ia = ib = 0
        while ia < na or ib < nb:
            if ib >= nb or (ia < na and ia * nb <= ib * na):
                self.op(*a[ia])
                ia += 1
            else:
                self.op(*b[ib])
                ib += 1

    def wait_only(self, eng, toks):
        waits = {}
        for t in toks:
            self._need(eng, t, waits)
        for sem, val in waits.items():
            self.seen[eng][sem] = val
        if waits:
            self.streams[eng].append((None, waits, None))

    def all_tokens(self):
        toks = [("E_" + e, self.cnt[e]) for e in ENGS if self.cnt[e] > 0]
        toks += [("D_" + k, v) for k, v in self.dma_cnt.items()]
        return toks

    def barrier(self):
        toks = self.all_tokens()
        for e in ENGS:
            self.wait_only(e, toks)

    def emit(self, stack):
        nc = self.nc
        names = ["E_" + e for e in ENGS if self.cnt[e] > 0] + ["D_" + k for k in self.dma_cnt]
        sems = {n: stack.enter_context(nc.semaphore(n)) for n in names}
        block = stack.enter_context(nc.Block())
        deco = {"pe": block.tensor, "act": block.scalar, "dve": block.vector,
                "pool": block.gpsimd, "sp": block.sync}
        for e in ENGS:
            stream = self.streams[e]
            if not stream:
                continue

            def body(eng, stream=stream):
                for fn, waits, tok in stream:
                    for sem, val in waits.items():
                        eng.wait_ge(sems[sem], val)
                    if fn is None:
                        continue
                    inst = fn(eng)
                    inst.then_inc(sems[tok[0]], 16 if tok[0].startswith("D_") else 1)

            deco[e](body)


def I(name, *args, **kw):
    return lambda e: getattr(e, name)(*args, **kw)


class Buf:
    def __init__(self, t, F, base=0):
        self.t = t
        self.F = F
        self.base = base

    def a(self, off, *dims, p0=0, np_=128):
        return bass.AP(self.t, p0 * self.F + self.base + off, [[self.F, np_]] + [list(d) for d in dims])

    def sub(self, base):
        return Buf(self.t, self.F, self.base + base)


DBG_MAP = {}


def build(nseg=2, nlayers=L, stop=None, cut=99, dbg=None):
    nc = bass.Bass("TRN2", target_bir_lowering=False)
    DBG_MAP.clear()
    dbg_h = nc.dram_tensor("dbg", [128, 8192], F32, kind="ExternalOutput") if dbg is not None else None
    dt = lambda n, s: nc.dram_tensor(n, s, F32, kind="ExternalInput")
    x_h = dt("x", [NTOK, D])
    cst_h = dt("cst", [128, 640])
    lnin_h = dt("lnin", [2, D])
    win_h = dt("w_in", [L * D, INC])
    wout_h = dt("w_out", [L * D, D])
    wq_h = dt("wq", [L * D, 2048])
    wgu_h = dt("wgu", [L * 16, 256])
    bgate_h = dt("bgate", [L, 256])
    glag_h = dt("glag", [L, 128])
    sgln_h = dt("sgln", [L * 2, 512])
    sgwT_h = dt("sgwT", [L * 4 * 128, 128])
    sgb_h = dt("sgb", [L, 512])
    ln1_h = dt("ln1", [L * 2, D])
    ln2_h = dt("ln2", [L * 2, D])
    k1T_h = dt("k1T", [L * 128, 128])
    k2T_h = dt("k2T", [L * 128, 128])
    uT_h = dt("uT", [L * D, NE])
    vt_h = dt("vtab", [L * NE, D])
    out_h = nc.dram_tensor("out", [NTOK, D], F32, kind="ExternalOutput")
    DAP = lambda h, off, *dims: bass.AP(h, off, [list(d) for d in dims])

    with ExitStack() as st:
        def sb(name, F, dtype):
            return Buf(st.enter_context(nc.sbuf_tensor(name, [128, F], dtype)), F)

        htok = sb("htok", NTS * D, F32)
        hT = sb("hT", 8 * SEG, BF16)
        W = sb("W", 8 * INC, BF16)
        Areg = sb("Areg", 8 * D, BF16)
        ARENA_BYTES = 54 * 1024
        arena_t = st.enter_context(nc.sbuf_tensor("arena", [128, ARENA_BYTES // 2], BF16))
        arena_f = arena_t.bitcast(F32)
        lnG = sb("lnG", D, F32)
        lnB = sb("lnB", D, F32)
        xs = sb("xs", D, F32)
        yn = sb("yn", D, F32)
        tt = sb("tt", D, F32)
        hbf = sb("hbf", D, BF16)
        junk = hbf
        cst = sb("cstf", 640, F32)
        ident = sb("ident", 128, BF16)
        ones = sb("onesb", 128, BF16)
        stt_ = sb("stat", 64, F32)
        Sf = sb("Sf", L * 512, F32)
        Sb = sb("Sb", L * 512, BF16)
        small = sb("small", 128, F32)
        ps = [Buf(st.enter_context(nc.psum_tensor(f"ps{i}", [128, 512], F32)), 512) for i in range(8)]

        class Arena:
            def __init__(self):
                self.off = 0

            def alloc(self, nelem, dtype):
                sz = 4 if dtype == F32 else 2
                self.off = (self.off + 63) // 64 * 64
                o = self.off
                self.off += nelem * sz
                assert self.off <= ARENA_BYTES, ("arena overflow", self.off)
                if dtype == F32:
                    return Buf(arena_f, ARENA_BYTES // 4, o // 4)
                return Buf(arena_t, ARENA_BYTES // 2, o // 2)

        am = Arena()
        qTf = am.alloc(2048, BF16)
        kTf = am.alloc(2048, BF16)
        silur = am.alloc(2048, BF16)
        gsu = am.alloc(2048, BF16)
        aT = am.alloc(512, BF16)
        Lt = am.alloc(256, F32)
        eb = am.alloc(512, F32)
        enb = am.alloc(512, F32)
        erem = am.alloc(256, F32)
        qtil = am.alloc(512, BF16)
        ktil = am.alloc(512, BF16)
        khat = am.alloc(256, BF16)
        v_bf = am.alloc(512, BF16)
        attn_bf = am.alloc(512, BF16)
        sq = am.alloc(512, BF16)
        rstd = am.alloc(512, F32)
        t1 = am.alloc(512, F32)
        catT = am.alloc(1024, BF16)
        gv = am.alloc(512, F32)
        vn = am.alloc(512, BF16)
        sg_g = am.alloc(512, F32)
        sg_b = am.alloc(512, F32)
        WsT = am.alloc(512, BF16)
        wgu = am.alloc(256, BF16)
        bgate = am.alloc(256, BF16)
        gcol = am.alloc(1, F32)
        bs_f = am.alloc(512, F32)
        bs_hi = am.alloc(512, BF16)
        bs_lo = am.alloc(512, BF16)
        bs_t = am.alloc(512, F32)

        ap_ = Arena()
        qT = ap_.alloc(16 * 512, BF16)
        UT = ap_.alloc(2 * 4096, BF16)
        HT = ap_.alloc(2 * 512, BF16)
        gAs = ap_.alloc(8 * 512, BF16)
        Gb = ap_.alloc(2 * 512, BF16)
        K1T = ap_.alloc(128, BF16)
        K2T = ap_.alloc(128, BF16)
        c16 = ap_.alloc(512, F32)
        candb = ap_.alloc(256, F32)
        scr = ap_.alloc(256, F32)
        v16 = ap_.alloc(256, F32)
        PAB = Buf(yn.t.bitcast(BF16), 2 * D)
        Ssum = Buf(tt.t.bitcast(BF16), 2 * D)
        rawAT = Buf(xs.t.bitcast(BF16), 2 * D)
        Esl = ap_.alloc(3 * 512, BF16)
        Fsl = W.sub(16384)
        biasv = small.sub(0)
        negm = small.sub(32)
        Zs = small.sub(40)
        lnZ = small.sub(48)
        j16 = small.sub(64)
        sst = small.sub(96)

        P = Prog(nc)
        bank_ctr = [0]

        bank_pool = [list(range(8))]

        def bk():
            pool = bank_pool[0]
            b = pool[bank_ctr[0] % len(pool)]
            bank_ctr[0] += 1
            return b

        def grab_with_pool(pool, fn, *args):
            bank_pool[0] = pool
            ops = P.grab(fn, *args)
            bank_pool[0] = list(range(8))
            return ops

        pr = lambda b: f"ps{b}"

        def mark(n):
            P.enabled = n <= cut

        dbg_col = [0]

        def dump(key, name, apfn, ncols, reads, np_=128):
            if dbg is None or tuple(dbg) != tuple(key) or not P.enabled:
                return
            c0 = dbg_col[0]
            dbg_col[0] += ncols
            DBG_MAP[name] = (c0, ncols, np_)
            P.op("pool", I("dma_start", out=bass.AP(dbg_h, c0, [[8192, np_], [1, ncols]]), in_=apfn()),
                 reads=reads, dma="dbg")

        P.op("sp", I("dma_start", out=cst.a(0, [1, 640]), in_=DAP(cst_h, 0, [640, 128], [1, 640])),
             writes=["cst"], dma="cst")
        P.op("pool", I("dma_start", out=ident.a(0, [1, 128]), in_=DAP(cst_h, 0, [640, 128], [1, 128])),
             writes=["ident"], dma="ident")
        P.op("pool", I("dma_start", out=ones.a(0, [1, 128]), in_=DAP(cst_h, 512, [640, 128], [1, 128])),
             writes=["ones"], dma="ones")
        P.op("dve", I("memset", Sf.a(0, [1, L * 512]), 0.0), writes=[f"Sf{l}" for l in range(L)])
        P.op("dve", I("memset", Sb.a(0, [1, L * 512]), 0.0), writes=[f"Sb{l}" for l in range(L)])
        CI, CTRI, CTRI2, CCAUS = 0, 128, 256, 384

        def load_ln_params(h, row0, scale):
            P.op("sp", I("dma_start", out=lnG.a(0, [1, D]), in_=DAP(h, row0 * D, [0, 128], [1, D])),
                 writes=["lnG"], dma="lnG")
            P.op("sp", I("dma_start", out=lnB.a(0, [1, D]), in_=DAP(h, (row0 + 1) * D, [0, 128], [1, D])),
                 writes=["lnB"], dma="lnB")
            if scale != 1.0:
                P.op("pool", I("tensor_scalar", out=lnG.a(0, [1, D]), in0=lnG.a(0, [1, D]), scalar1=scale,
                                                       scalar2=None, op0=ALU.mult), reads=["lnG"], writes=["lnG"])
                P.op("pool", I("tensor_scalar", out=lnB.a(0, [1, D]), in0=lnB.a(0, [1, D]), scalar1=scale,
                                                       scalar2=None, op0=ALU.mult), reads=["lnB"], writes=["lnB"])

        stat_ctr = [0]

        def stats_chain(sbuf, s0, n):
            c = lambda k: sbuf.a(s0 + k, [1, 1])
            r = "statchain"
            P.op("dve", I("tensor_scalar", out=c(2), in0=c(0), scalar1=1.0 / n, scalar2=None, op0=ALU.mult),
                 reads=[r], writes=[r])
            P.op("dve", I("tensor_tensor", out=c(3), in0=c(2), in1=c(2), op=ALU.mult), reads=[r], writes=[r])
            P.op("dve", I("scalar_tensor_tensor", out=c(4), in0=c(1), scalar=1.0 / n, in1=c(3),
                                                         op0=ALU.mult, op1=ALU.subtract), reads=[r], writes=[r])
            P.op("act", I("activation", out=c(5), in_=c(4), func=AF.Ln, bias=EPS), reads=[r], writes=[r])
            P.op("act", I("activation", out=c(6), in_=c(5), func=AF.Exp, scale=-0.5), reads=[r], writes=[r])
            P.op("dve", I("scalar_tensor_tensor", out=c(7), in0=c(2), scalar=-1.0, in1=c(6),
                                                         op0=ALU.mult, op1=ALU.mult), reads=[r], writes=[r])

        def layer_norm(src, src_res, ti, final, seg):
            s0 = (stat_ctr[0] % 4) * 8
            stat_ctr[0] += 1
            hres = f"htok{ti}"
            P.op("act", I("activation", out=junk.a(0, [1, D]), in_=src, func=AF.Identity,
                                               accum_out=stt_.a(s0, [1, 1])),
                 reads=[src_res], writes=["hbf", "statchain"])
            P.op("act", I("activation", out=junk.a(0, [1, D]), in_=src, func=AF.Square,
                                               accum_out=stt_.a(s0 + 1, [1, 1])),
                 reads=[src_res], writes=["hbf", "statchain"])
            stats_chain(stt_, s0, float(D))
            P.op("act", I("activation", out=yn.a(0, [1, D]), in_=src, func=AF.Identity,
                                               scale=stt_.a(s0 + 6, [1, 1]), bias=stt_.a(s0 + 7, [1, 1])),
                 reads=[src_res, "statchain"], writes=["yn"])
            P.op("dve", I("tensor_tensor", out=tt.a(0, [1, D]), in0=yn.a(0, [1, D]), in1=lnG.a(0, [1, D]),
                                                   op=ALU.mult), reads=["yn", "lnG"], writes=["tt"])
            P.op("dve", I("tensor_tensor", out=htok.a(ti * D, [1, D]), in0=tt.a(0, [1, D]),
                                                  in1=lnB.a(0, [1, D]), op=ALU.add),
                 reads=["tt", "lnB", src_res], writes=[hres])
            if final:
                row0 = seg * SEG + ti * 128
                P.op("sp", I("dma_start", out=DAP(out_h, row0 * D, [D, 128], [1, D]), in_=htok.a(ti * D, [1, D])),
                     reads=[hres], dma=f"out{ti}")
                return
            P.op("act", I("activation", out=hbf.a(0, [1, D]), in_=htok.a(ti * D, [1, D]), func=AF.Copy,
                                               scale=1.0 / ALPHA), reads=[hres], writes=["hbf"])
            for half in range(2):
                b = bk()
                for k4 in range(4):
                    kc = half * 4 + k4
                    P.op("pe", I("matmul", ps[b].a(k4 * 128, [1, 128]), hbf.a(kc * 128, [1, 128]), ident.a(0, [1, 128]),
                        start=True, stop=True), reads=["hbf", "ident"], writes=[pr(b)])
                eng = "act" if half == 0 else "dve"
                dst = hT.a((half * 4) * SEG + ti * 128, [SEG, 4], [1, 128])
                srcp = ps[b].a(0, [128, 4], [1, 128])
                if eng == "act":
                    P.op("act", I("copy", out=dst, in_=srcp), reads=[pr(b)], writes=[f"hT{ti}"])
                else:
                    P.op("dve", I("tensor_copy", out=dst, in_=srcp), reads=[pr(b)],
                         writes=[f"hT{ti}"])

        def mixer(seg, l):
            for c0, cw, rn in ((0, 1280, "Wa"), (1280, 1296, "Wb")):
                P.op("pool", I("dma_start", out=W.a(c0, [INC, 8], [1, cw]),
                    in_=DAP(win_h, l * D * INC + c0, [INC, 128], [128 * INC, 8], [1, cw])),
                    writes=[rn], dma=rn)

            def wres(c0, n):
                r = []
                if c0 < 1280:
                    r.append("Wa")
                if c0 + n > 1280:
                    r.append("Wb")
                return r
            P.op("pool", I("dma_start", out=Areg.a(0, [D, 8], [1, D]),
                                               in_=DAP(wout_h, l * D * D, [D, 128], [128 * D, 8], [1, D])),
                 writes=["A0", "A1"], dma="A")
            P.op("dve", I("memset", wgu.a(0, [1, 256], np_=32), 0.0), writes=["wgu"])
            P.op("dve", I("memset", bgate.a(0, [1, 256], np_=32), 0.0), writes=["bgate"])
            P.op("dve", I("memset", aT.a(0, [1, 512], np_=32), 0.0), writes=["aT"])
            P.op("dve", I("memset", bs_hi.a(0, [1, 512], np_=32), 0.0), writes=["bs_hi"])
            P.op("dve", I("memset", bs_lo.a(0, [1, 512], np_=32), 0.0), writes=["bs_lo"])
            P.op("pool", I("dma_start", out=wgu.a(0, [1, 256], np_=16), in_=DAP(wgu_h, l * 16 * 256, [256, 16], [1, 256])),
                 writes=["wgu"], dma="wgu")
            P.op("pool", I("dma_start", out=bgate.a(0, [1, 256], np_=1), in_=DAP(bgate_h, l * 256, [256, 1], [1, 256])),
                 writes=["bgate"], dma="bgate")
            P.op("pool", I("dma_start", out=WsT.a(0, [128, 4], [1, 128]),
                                               in_=DAP(sgwT_h, l * 4 * 128 * 128, [128, 128], [128 * 128, 4], [1, 128])),
                 writes=["WsT"], dma="WsT")
            P.op("sp", I("dma_start", out=gcol.a(0, [1, 1]), in_=DAP(glag_h, l * 128, [1, 128], [1, 1])),
                 writes=["gcol"], dma="gcol")
            P.op("sp", I("dma_start", out=sg_g.a(0, [1, 512]), in_=DAP(sgln_h, (2 * l) * 512, [0, 128], [1, 512])),
                 writes=["sg_g"], dma="sg_g")
            P.op("sp", I("dma_start", out=sg_b.a(0, [1, 512]), in_=DAP(sgln_h, (2 * l + 1) * 512, [0, 128], [1, 512])),
                 writes=["sg_b"], dma="sg_b")
            P.op("sp", I("dma_start", out=bs_f.a(0, [1, 512], np_=1), in_=DAP(sgb_h, l * 512, [512, 1], [1, 512])),
                 writes=["bs_f"], dma="bs_f")
            load_ln_params(ln1_h, 2 * l, ALPHA)
            P.op("dve", I("memset", WsT.a(0, [128, 4], [1, 64], p0=64, np_=64), 0.0), reads=["WsT"], writes=["WsT"])
            P.op("dve", I("tensor_copy", out=bs_hi.a(0, [1, 512], np_=1), in_=bs_f.a(0, [1, 512], np_=1)),
                 reads=["bs_f"], writes=["bs_hi"])
            P.op("dve", I("tensor_tensor", out=bs_t.a(0, [1, 512], np_=1), in0=bs_f.a(0, [1, 512], np_=1),
                                                  in1=bs_hi.a(0, [1, 512], np_=1), op=ALU.subtract),
                 reads=["bs_f", "bs_hi"], writes=["bs_t"])
            P.op("dve", I("tensor_copy", out=bs_lo.a(0, [1, 512], np_=1), in_=bs_t.a(0, [1, 512], np_=1)),
                 reads=["bs_t"], writes=["bs_lo"])

            def proj_fm(c0, m, q):
                b = bk()
                hres = [f"hT{4 * q + j}" for j in range(4)]
                for kc in range(8):
                    P.op("pe", I("matmul", ps[b].a(0, [1, 512], np_=m), W.a(kc * INC + c0, [1, m]), hT.a(kc * SEG + q * 512, [1, 512]),
                        start=(kc == 0), stop=(kc == 7)), reads=wres(c0, m) + hres, writes=[pr(b)])
                return b

            def proj_tm(c0, n, ti):
                b = bk()
                for kc in range(8):
                    P.op("pe", I("matmul", ps[b].a(0, [1, n]), hT.a(kc * SEG + ti * 128, [1, 128]), W.a(kc * INC + c0, [1, n]),
                        start=(kc == 0), stop=(kc == 7)), reads=wres(c0, n) + [f"hT{ti}"], writes=[pr(b)])
                return b

            for q in range(2):
                mark(1)
                for i in range(4):
                    b = proj_fm(i * 64, 64, q)
                    P.op("act", I("copy", out=qTf.a(i * 512, [1, 512], np_=64),
                                                           in_=ps[b].a(0, [1, 512], np_=64)),
                         reads=[pr(b)], writes=["qTf"])
                for i in range(4):
                    b = proj_fm(256 + i * 64, 64, q)
                    P.op("dve", I("tensor_copy", out=kTf.a(i * 512, [1, 512], np_=64),
                                                                  in_=ps[b].a(0, [1, 512], np_=64)),
                         reads=[pr(b)], writes=["kTf"])
                for i in range(4):
                    b = proj_fm(1024 + i * 128, 128, q)
                    P.op("act", I("activation", out=silur.a(i * 512, [1, 512]), in_=ps[b].a(0, [1, 512]),
                                                                 func=AF.Silu), reads=[pr(b)], writes=["silur"])
                for i in range(4):
                    b = proj_fm(1552 + i * 128, 128, q)
                    P.op("act", I("activation", out=gsu.a(i * 512, [1, 512]), in_=ps[b].a(0, [1, 512]),
                                                                 func=AF.Gelu), reads=[pr(b)], writes=["gsu"])
                b = proj_fm(1536, 16, q)
                P.op("act", I("copy", out=aT.a(0, [1, 512], np_=16), in_=ps[b].a(0, [1, 512], np_=16)),
                     reads=[pr(b)], writes=["aT"])

                def tile_body(j, part):
                    ti = 4 * q + j
                    tc0 = j * 128
                    key = (seg, l, ti)
                    if part == 0:
                        mark(2)
                        b = bk()
                        P.op("pe", I("matmul", ps[b].a(0, [1, 256]), aT.a(tc0, [1, 128], np_=32),
                                                           wgu.a(0, [1, 256], np_=32), start=True, stop=False),
                             reads=["aT", "wgu"], writes=[pr(b)])
                        P.op("pe", I("matmul", ps[b].a(0, [1, 256]), ones.a(0, [1, 128], np_=32),
                                                           bgate.a(0, [1, 256], np_=32), start=False, stop=True),
                             reads=["ones", "bgate"], writes=[pr(b)])
                        P.op("act", I("activation", out=Lt.a(0, [1, 256]), in_=ps[b].a(0, [1, 256]), func=AF.Exp,
                                                                scale=-1.0), reads=[pr(b)], writes=["Lt"])
                        dump(key, "aT", lambda: aT.a(0, [1, 512], np_=16), 512, ["aT"], np_=16)
                        dump(key, "wgu", lambda: wgu.a(0, [1, 256], np_=16), 256, ["wgu"], np_=16)
                        dump(key, "bgate", lambda: bgate.a(0, [1, 256], np_=1), 256, ["bgate"], np_=1)
                        dump(key, "expnx", lambda: Lt.a(0, [1, 256]), 256, ["Lt"])
                        P.op("act", I("activation", out=Lt.a(0, [1, 256]), in_=Lt.a(0, [1, 256]), func=AF.Ln, bias=1.0),
                             reads=["Lt"], writes=["Lt"])
                        mark(3)
                        dump(key, "Lt", lambda: Lt.a(0, [1, 256]), 256, ["Lt"])
                        b2 = bk()
                        for h in range(4):
                            P.op("pe", I("matmul", ps[b2].a(h * 128, [1, 128], np_=64), Lt.a(h * 64, [1, 64]),
                                                                      cst.a(CTRI, [1, 128]), start=True, stop=True),
                                 reads=["Lt", "cst"], writes=[pr(b2)])
                        b3 = bk()
                        P.op("pe", I("matmul", ps[b3].a(0, [1, 256]), cst.a(CTRI2, [1, 128]), Lt.a(0, [1, 256]),
                                                             start=True, stop=True), reads=["Lt", "cst"], writes=[pr(b3)])
                        P.op("act", I("activation", out=eb.a(0, [1, 512], np_=64), in_=ps[b2].a(0, [1, 512], np_=64),
                                                                  func=AF.Exp), reads=[pr(b2)], writes=["eb"])
                        P.op("act", I("activation", out=enb.a(0, [1, 512], np_=64), in_=ps[b2].a(0, [1, 512], np_=64),
                                                                  func=AF.Exp, scale=-1.0), reads=[pr(b2)], writes=["enb"])
                        P.op("act", I("activation", out=erem.a(0, [1, 256]), in_=ps[b3].a(0, [1, 256]), func=AF.Exp),
                             reads=[pr(b3)], writes=["erem"])
                        P.op("dve", I("scalar_tensor_tensor", out=qtil.a(0, [128, 4], [1, 128], np_=64), in0=qTf.a(tc0, [512, 4], [1, 128], np_=64), scalar=0.125,
                            in1=eb.a(0, [128, 4], [1, 128], np_=64), op0=ALU.mult, op1=ALU.mult),
                            reads=["qTf", "eb"], writes=["qtil"])
                        P.op("dve", I("tensor_tensor", out=ktil.a(0, [128, 4], [1, 128], np_=64), in0=kTf.a(tc0, [512, 4], [1, 128], np_=64),
                            in1=enb.a(0, [128, 4], [1, 128], np_=64), op=ALU.mult), reads=["kTf", "enb"], writes=["ktil"])
                        dump(key, "eb", lambda: eb.a(0, [1, 512], np_=64), 512, ["eb"], np_=64)
                        dump(key, "erem", lambda: erem.a(0, [1, 256]), 256, ["erem"])
                        dump(key, "qtil", lambda: qtil.a(0, [1, 512], np_=64), 512, ["qtil"], np_=64)
                        dump(key, "ktil", lambda: ktil.a(0, [1, 512], np_=64), 512, ["ktil"], np_=64)
                        mark(4)
                        b4 = proj_tm(256, 256, ti)
                        mark(4.02)
                        P.op("dve", I("tensor_tensor", out=khat.a(0, [1, 256]), in0=ps[b4].a(0, [1, 256]),
                                                                     in1=erem.a(0, [1, 256]), op=ALU.mult),
                             reads=[pr(b4), "erem"], writes=["khat"])
                        mark(4.03)
                        b5 = proj_tm(512, 512, ti)
                        mark(4.04)
                        P.op("act", I("copy", out=v_bf.a(0, [1, 512]), in_=ps[b5].a(0, [1, 512])),
                             reads=[pr(b5)], writes=["v_bf"])
                    if part == 1:
                        mark(4.2)
                        b6 = bk()
                        for h in range(4):
                            c, pb = h // 2, (h % 2) * 64
                            P.op("pe", I("matmul", ps[b6].a(h * 128, [1, 128]), ktil.a(h * 128, [1, 128], np_=64),
                                qtil.a(h * 128, [1, 128], np_=64), start=True, stop=True),
                                reads=["ktil", "qtil"], writes=[pr(b6)])
                        mark(4.4)
                        P.op("dve", I("tensor_tensor", out=attn_bf.a(0, [128, 4], [1, 128]), in0=ps[b6].a(0, [128, 4], [1, 128]),
                            in1=cst.a(CCAUS, [0, 4], [1, 128]), op=ALU.mult), reads=[pr(b6), "cst"], writes=["attn_bf"])
                        dump(key, "khat", lambda: khat.a(0, [1, 256]), 256, ["khat"])
                        dump(key, "v_bf", lambda: v_bf.a(0, [1, 512]), 512, ["v_bf"])
                        dump(key, "attn", lambda: attn_bf.a(0, [1, 512]), 512, ["attn_bf"])
                        mark(4.6)
                        b7 = bk()
                        for h in range(4):
                            c, pb = h // 2, (h % 2) * 64
                            P.op("pe", I("matmul", ps[b7].a(h * 128, [1, 128]), v_bf.a(h * 128, [1, 128]), attn_bf.a(h * 128, [1, 128]),
                                start=True, stop=False), reads=["v_bf", "attn_bf"], writes=[pr(b7)])
                            P.op("pe", I("matmul", ps[b7].a(h * 128, [1, 128]), Sb.a(l * 512 + h * 128, [1, 128], np_=64),
                                qtil.a(h * 128, [1, 128], np_=64), start=False, stop=True),
                                reads=[f"Sb{l}", "qtil"], writes=[pr(b7)])
                        mark(5)
                        b8 = bk()
                        for h in range(4):
                            P.op("pe", I("matmul", ps[b8].a(h * 128, [1, 128], np_=64), khat.a(h * 64, [1, 64]), v_bf.a(h * 128, [1, 128]),
                                start=True, stop=True), reads=["khat", "v_bf"], writes=[pr(b8)])
                        for h in range(4):
                            so = l * 512 + h * 128
                            P.op("dve", I("scalar_tensor_tensor", out=Sf.a(so, [1, 128], np_=64), in0=Sf.a(so, [1, 128], np_=64),
                                scalar=eb.a(h * 128 + 127, [1, 1], np_=64),
                                in1=ps[b8].a(h * 128, [1, 128], np_=64),
                                op0=ALU.mult, op1=ALU.add), reads=[f"Sf{l}", "eb", pr(b8)], writes=[f"Sf{l}"])
                        P.op("act", I("copy", out=Sb.a(l * 512, [1, 512], np_=64), in_=Sf.a(l * 512, [1, 512], np_=64)),
                             reads=[f"Sf{l}"], writes=[f"Sb{l}"])
                        mark(6)
                        P.op("act", I("activation", out=sq.a(0, [1, 512]), in_=ps[b7].a(0, [1, 512]), func=AF.Square),
                             reads=[pr(b7)], writes=["sq"])
                        b9 = bk()
                        P.op("pe", I("matmul", ps[b9].a(0, [1, 512]), ones.a(0, [1, 128]), sq.a(0, [1, 512]),
                                                             start=True, stop=True), reads=["ones", "sq"], writes=[pr(b9)])
                        P.op("act", I("activation", out=rstd.a(0, [1, 512]), in_=ps[b9].a(0, [1, 512]), func=AF.Ln,
                                                              scale=1.0 / 128.0, bias=EPS), reads=[pr(b9)], writes=["rstd"])
                        P.op("act", I("activation", out=rstd.a(0, [1, 512]), in_=rstd.a(0, [1, 512]), func=AF.Exp, scale=-0.5),
                             reads=["rstd"], writes=["rstd"])
                        P.op("dve", I("scalar_tensor_tensor", out=t1.a(0, [1, 512]), in0=ps[b7].a(0, [1, 512]), scalar=gcol.a(0, [1, 1]), in1=rstd.a(0, [1, 512]),
                            op0=ALU.mult, op1=ALU.mult), reads=[pr(b7), "gcol", "rstd"], writes=["t1"])
                        P.op("dve", I("tensor_tensor", out=catT.a(0, [128, 4], [1, 128]), in0=t1.a(0, [128, 4], [1, 128]),
                            in1=silur.a(tc0, [512, 4], [1, 128]), op=ALU.mult), reads=["t1", "silur"], writes=["catT_o"])
                        dump(key, "t1", lambda: t1.a(0, [1, 512]), 512, ["t1"])
                        dump(key, "Sf", lambda: Sf.a(l * 512, [1, 512], np_=64), 512, [f"Sf{l}"], np_=64)
                    if part == 2:
                        mark(7)
                        b10 = proj_tm(2064, 512, ti)
                        P.op("act", I("activation", out=gv.a(0, [1, 512]), in_=ps[b10].a(0, [1, 512]), func=AF.Gelu,
                                                                    accum_out=sst.a(0, [1, 1])),
                             reads=[pr(b10)], writes=["gv", "statchain"])
                        P.op("act", I("activation", out=junk.a(0, [1, 512]), in_=gv.a(0, [1, 512]), func=AF.Square,
                                                           accum_out=sst.a(1, [1, 1])),
                             reads=["gv"], writes=["hbf", "statchain"])
                        stats_chain(sst, 0, 512.0)
                        P.op("act", I("activation", out=gv.a(0, [1, 512]), in_=gv.a(0, [1, 512]), func=AF.Identity,
                                                           scale=sst.a(6, [1, 1]), bias=sst.a(7, [1, 1])),
                             reads=["gv", "statchain"], writes=["gv"])
                        P.op("dve", I("tensor_tensor", out=gv.a(0, [1, 512]), in0=gv.a(0, [1, 512]), in1=sg_g.a(0, [1, 512]),
                                                               op=ALU.mult), reads=["gv", "sg_g"], writes=["gv"])
                        P.op("dve", I("tensor_tensor", out=vn.a(0, [1, 512]), in0=gv.a(0, [1, 512]), in1=sg_b.a(0, [1, 512]),
                                                              op=ALU.add), reads=["gv", "sg_b"], writes=["vn"])
                        b11 = bk()
                        for g in range(4):
                            P.op("pe", I("matmul", ps[b11].a(g * 128, [1, 128]), vn.a(g * 128, [1, 128]), WsT.a(g * 128, [1, 128]),
                                start=True, stop=False), reads=["vn", "WsT"], writes=[pr(b11)])
                            P.op("pe", I("matmul", ps[b11].a(g * 128, [1, 128]), ones.a(0, [1, 128], np_=32), bs_hi.a(g * 128, [1, 128], np_=32),
                                start=False, stop=False), reads=["ones", "bs_hi"], writes=[pr(b11)])
                            P.op("pe", I("matmul", ps[b11].a(g * 128, [1, 128]), ones.a(0, [1, 128], np_=32), bs_lo.a(g * 128, [1, 128], np_=32),
                                start=False, stop=True), reads=["ones", "bs_lo"], writes=[pr(b11)])
                        P.op("dve", I("tensor_tensor", out=catT.a(512, [128, 4], [1, 128]), in0=ps[b11].a(0, [128, 4], [1, 128]),
                            in1=gsu.a(tc0, [512, 4], [1, 128]), op=ALU.mult), reads=[pr(b11), "gsu"], writes=["catT_g"])
                        dump(key, "vn", lambda: vn.a(0, [1, 512]), 512, ["vn"])
                        dump(key, "catT", lambda: catT.a(0, [1, 1024]), 1024, ["catT_o", "catT_g"])
                        mark(8)
                        for dh in range(2):
                            b12 = bk()
                            for cc in range(8):
                                P.op("pe", I("matmul", ps[b12].a(0, [1, 512]), catT.a(cc * 128, [1, 128]), Areg.a(cc * D + dh * 512, [1, 512]),
                                    start=(cc == 0), stop=(cc == 7)),
                                    reads=["catT_o", "catT_g", "A0", "A1"], writes=[pr(b12)])
                            P.op("dve", I("tensor_tensor", out=htok.a(ti * D + dh * 512, [1, 512]), in0=ps[b12].a(0, [1, 512]),
                                in1=htok.a(ti * D + dh * 512, [1, 512]), op=ALU.add),
                                reads=[pr(b12), f"htok{ti}"], writes=[f"htok{ti}"])
                        dump(key, "hpre", lambda: htok.a(ti * D, [1, D]), 1024, [f"htok{ti}"])
                    if part == 3:
                        layer_norm(htok.a(ti * D, [1, D]), f"htok{ti}", ti, False, seg)

                G = lambda jj, part: grab_with_pool([0, 1, 2, 3, 4] if part < 2 else [5, 6, 7], tile_body, jj, part)
                P.emit_merged(G(0, 0), [])
                P.emit_merged(G(0, 1), [])
                for jj in range(1, 4):
                    P.emit_merged(G(jj, 0), G(jj - 1, 2))
                    P.emit_merged(G(jj, 1), G(jj - 1, 3))
                P.emit_merged(G(3, 2), [])
                P.emit_merged(G(3, 3), [])

        def peer(seg, l, final):
            for g4 in range(4):
                P.op("pool", I("dma_start", out=W.a(g4 * 512, [2048, 8], [1, 512]),
                               in_=DAP(wq_h, l * D * 2048 + g4 * 512, [2048, 128], [128 * 2048, 8], [1, 512])),
                     writes=[f"Wq{g4}"], dma=f"Wq{g4}")
            P.op("pool", I("dma_start", out=K1T.a(0, [1, 128]), in_=DAP(k1T_h, l * 128 * 128, [128, 128], [1, 128])),
                 writes=["K1T"], dma="K1T")
            P.op("pool", I("dma_start", out=K2T.a(0, [1, 128]), in_=DAP(k2T_h, l * 128 * 128, [128, 128], [1, 128])),
                 writes=["K2T"], dma="K2T")
            load_ln_params(ln2_h, 2 * l, 1.0 if final else ALPHA)
            cgi = [0]

            def load_ut(cg):
                s = cg % 2
                P.op("pool", I("dma_start",
                    out=UT.a(s * 4096, [512, 8], [1, 512]),
                    in_=DAP(uT_h, l * D * NE + cg * 512, [NE, 128], [128 * NE, 8], [1, 512])),
                    writes=[f"UT{s}"], dma=f"UT{s}")

            def load_v(cg):
                s = cg % 2
                P.op("pool", I("dma_start",
                    out=Areg.a(s * 4096, [D, 4], [1, D]),
                    in_=DAP(vt_h, (l * NE + cg * 512) * D, [D, 128], [128 * D, 4], [1, D])),
                    writes=[f"A{s}"], dma=f"V{s}")

            ln2_pending = []
            for blk in range(2):
                t0 = blk * 4
                bc0 = blk * 512
                hres = [f"hT{t0 + j}" for j in range(4)]
                bank_pool[0] = [0, 1, 2, 3, 4, 5]
                P.capture = []
                for jq in range(16):
                    b = bk()
                    for kc in range(8):
                        P.op("pe", I("matmul", ps[b].a(0, [1, 512]), W.a(kc * 2048 + jq * 128, [1, 128]), hT.a(kc * SEG + bc0, [1, 512]),
                            start=(kc == 0), stop=(kc == 7)), reads=[f"Wq{jq // 4}"] + hres, writes=[pr(b)])
                    if jq % 2 == 0:
                        P.op("act", I("copy", out=qT.a(jq * 512, [1, 512]), in_=ps[b].a(0, [1, 512])),
                             reads=[pr(b)], writes=["qT"])
                    else:
                        P.op("dve", I("tensor_copy", out=qT.a(jq * 512, [1, 512]), in_=ps[b].a(0, [1, 512])),
                             reads=[pr(b)], writes=["qT"])
                for j in range(4):
                    tc0 = j * 128
                    banks = [bk() for _ in range(4)]
                    for jq in range(16):
                        b = banks[jq // 4]
                        KT = K1T if jq % 2 == 0 else K2T
                        P.op("pe", I("matmul", ps[b].a((jq % 4) * 128, [1, 128]), qT.a(jq * 512 + tc0, [1, 128]), KT.a(0, [1, 128]),
                            start=True, stop=True), reads=["qT", "K1T", "K2T"], writes=[pr(b)])
                    for jq in range(16):
                        b = banks[jq // 4]
                        sl = ps[b].a((jq % 4) * 128, [1, 128])
                        P.op("dve", I("max", out=v16.a(jq * 16, [1, 8]), in_=sl),
                             reads=[pr(b)], writes=["v16"])
                        P.op("dve", I("match_replace", out=scr.a(0, [1, 128]), in_to_replace=v16.a(jq * 16, [1, 8]), in_values=sl, imm_value=-1e30),
                            reads=[pr(b), "v16"], writes=["scr"])
                        P.op("dve", I("max", out=v16.a(jq * 16 + 8, [1, 8]), in_=scr.a(0, [1, 128])),
                             reads=["scr"], writes=["v16"])
                    cres = f"c16_{j}"
                    for h in range(8):
                        co = j * 128 + h * 16
                        P.op("dve", I("tensor_tensor", out=candb.a(0, [16, 16], [1, 16]), in0=v16.a(2 * h * 16, [1, 16], [0, 16]),
                            in1=v16.a((2 * h + 1) * 16, [0, 16], [1, 16]), op=ALU.add),
                            reads=["v16"], writes=["candb"])
                        P.op("dve", I("max", out=c16.a(co, [1, 8]), in_=candb.a(0, [1, 256])),
                             reads=["candb"], writes=[cres])
                        P.op("dve", I("match_replace", out=scr.a(0, [1, 256]), in_to_replace=c16.a(co, [1, 8]), in_values=candb.a(0, [1, 256]),
                            imm_value=-1e30), reads=["candb", cres], writes=["scr"])
                        P.op("dve", I("max", out=c16.a(co + 8, [1, 8]), in_=scr.a(0, [1, 256])),
                             reads=["scr"], writes=[cres])
                    P.op("dve", I("tensor_scalar", out=negm.a(0, [1, 8]), in0=c16.a(j * 128, [16, 8]), scalar1=-1.0,
                                                          scalar2=None, op0=ALU.mult), reads=[cres], writes=["negm"])
                    for h in range(8):
                        co = j * 128 + h * 16
                        P.op("act", I("activation", out=j16.a(0, [1, 16]), in_=c16.a(co, [1, 16]), func=AF.Exp, bias=negm.a(h, [1, 1]),
                            accum_out=Zs.a(h, [1, 1])), reads=[cres, "negm"], writes=["j16", "Zs"])
                    P.op("act", I("activation", out=lnZ.a(0, [1, 8]), in_=Zs.a(0, [1, 8]), func=AF.Ln),
                         reads=["Zs"], writes=["lnZ"])
                    P.op("dve", I("tensor_tensor", out=biasv.a(j * 8, [1, 8]), in0=negm.a(0, [1, 8]),
                                                          in1=lnZ.a(0, [1, 8]), op=ALU.subtract),
                         reads=["negm", "lnZ"], writes=[f"biasv{j}"])
                p12_ops = P.capture
                P.capture = None
                bank_pool[0] = list(range(8))
                P.emit_merged(p12_ops, ln2_pending)
                ln2_pending = []
                NY = 4
                YB = [0, 1, 2, 4]

                def emit_Y(idx, cg, j, h):
                    yb = YB[idx % NY]
                    tc0 = j * 128
                    P.op("pe", I("matmul", ps[yb].a(0, [1, 512]), qT.a((2 * h + 1) * 512 + tc0, [1, 128]),
                                 K2T.a(0, [0, 4], [1, 128]), start=True, stop=False),
                         reads=["qT", "K2T"], writes=[pr(yb)])
                    P.op("pe", I("matmul", ps[yb].a(0, [1, 512]), qT.a((2 * h) * 512 + tc0, [1, 128]),
                                 K1T.a(cg * 4, [1, 4], [0, 128]), start=False, stop=True),
                         reads=["qT", "K1T"], writes=[pr(yb)])

                def emit_EF(idx, cg, j, h):
                    yb = YB[idx % NY]
                    es = idx % 3
                    fs = idx % 8
                    P.op("act", I("activation", out=Esl.a(es * 512, [1, 512]), in_=ps[yb].a(0, [1, 512]), func=AF.Exp,
                                  bias=biasv.a(j * 8 + h, [1, 1])), reads=[pr(yb), f"biasv{j}"], writes=[f"E{es}"])
                    P.op("dve", I("scalar_tensor_tensor", out=Fsl.a(fs * 512, [1, 512]), in0=ps[yb].a(0, [1, 512]),
                                  scalar=c16.a(j * 128 + h * 16 + 15, [1, 1]), in1=Esl.a(es * 512, [1, 512]),
                                  op0=ALU.is_ge, op1=ALU.mult), reads=[pr(yb), f"c16_{j}", f"E{es}"], writes=[f"F{fs}"])

                def emit_T(idx, cg, j, h):
                    if h == 3:
                        P.op("pool", I("tensor_tensor", out=PAB.a(0, [1, 1024]), in0=Fsl.a(0, [1, 1024]),
                                       in1=Fsl.a(1024, [1, 1024]), op=ALU.add),
                             reads=["F0", "F1", "F2", "F3"], writes=["yn"])
                    elif h == 7:
                        P.op("pool", I("tensor_tensor", out=PAB.a(1024, [1, 1024]), in0=Fsl.a(2048, [1, 1024]),
                                       in1=Fsl.a(3072, [1, 1024]), op=ALU.add),
                             reads=["F4", "F5", "F6", "F7"], writes=["yn"])

                def tail0(cg, j):
                    P.op("pool", I("tensor_tensor", out=Ssum.a(0, [1, 1024]), in0=PAB.a(0, [1, 1024]),
                                   in1=PAB.a(1024, [1, 1024]), op=ALU.add), reads=["yn"], writes=["tt"])

                def emit_AT_mm(cg, c, kc):
                    s = cg % 2
                    P.op("pe", I("matmul", ps[5].a(0, [1, 512]), UT.a(s * 4096 + kc * 512 + c * 128, [1, 128]),
                                 hT.a(kc * SEG + bc0, [1, 512]), start=(kc == 0), stop=(kc == 7)),
                         reads=[f"UT{s}"] + hres, writes=[pr(5)])

                def emit_AT_copy(c):
                    P.op("act", I("copy", out=rawAT.a(c * 512, [1, 512]), in_=ps[5].a(0, [1, 512])),
                         reads=[pr(5)], writes=["xs"])

                def emit_AT_gelu(cg):
                    ga = cg % 2
                    P.op("act", I("activation", out=gAs.a(ga * 4 * 512, [1, 2048]), in_=rawAT.a(0, [1, 2048]),
                                  func=AF.Gelu), reads=["xs"], writes=[f"gA{ga}_{c}" for c in range(4)])

                octr = [0]

                def tail1(cg, j):
                    tcn = cg * 4 + j
                    gb = 3
                    hs = tcn % 2
                    for c in range(4):
                        for half in range(2):
                            P.op("pe", I("matmul", ps[gb].a(c * 128, [1, 128]), Ssum.a(half * 512 + c * 128, [1, 128]),
                                         ident.a(0, [1, 128]), start=(half == 0), stop=(half == 1)),
                                 reads=["tt", "ident"], writes=[pr(gb)])

                def tail2(cg, j):
                    s = cg % 2
                    ga = cg % 2
                    tcn = cg * 4 + j
                    gb = 3
                    hs = tcn % 2
                    P.op("dve", I("tensor_tensor", out=HT.a(hs * 512, [128, 4], [1, 128]),
                                  in0=ps[gb].a(0, [128, 4], [1, 128]),
                                  in1=gAs.a(ga * 4 * 512 + j * 128, [512, 4], [1, 128]), op=ALU.mult),
                         reads=[pr(gb)] + [f"gA{ga}_{c}" for c in range(4)], writes=[f"HT{hs}"])

                def tail_mm(cg, j, c):
                    s = cg % 2
                    hs = (cg * 4 + j) % 2
                    for dh in range(2):
                        ob = 6 + dh
                        P.op("pe", I("matmul", ps[ob].a(0, [1, 512]), HT.a(hs * 512 + c * 128, [1, 128]),
                                     Areg.a(s * 4096 + c * D + dh * 512, [1, 512]), start=(c == 0), stop=(c == 3)),
                             reads=[f"HT{hs}", f"A{s}"], writes=[pr(ob)])

                def tail3(cg, j, dh):
                    ti = t0 + j
                    ob = 6 + dh
                    P.op("dve", I("tensor_tensor", out=htok.a(ti * D + dh * 512, [1, 512]), in0=ps[ob].a(0, [1, 512]),
                                  in1=htok.a(ti * D + dh * 512, [1, 512]), op=ALU.add),
                         reads=[pr(ob), f"htok{ti}"], writes=[f"htok{ti}"])

                NCG = 32
                load_ut(0)
                load_ut(1)
                load_v(0)
                for c in range(4):
                    for kc in range(8):
                        emit_AT_mm(0, c, kc)
                    emit_AT_copy(c)
                emit_AT_gelu(0)
                seq = [(cg, j, h) for cg in range(NCG) for j in range(4) for h in range(8)]
                for k in range(NY - 1):
                    emit_Y(k, *seq[k])
                pend_copy = None
                pend_gelu = None
                for idx, (cg, j, h) in enumerate(seq):
                    if pend_copy is not None:
                        emit_AT_copy(pend_copy)
                        pend_copy = None
                        if pend_gelu is not None:
                            emit_AT_gelu(pend_gelu)
                            pend_gelu = None
                    if j == 0 and h == 0 and cg + 2 < NCG:
                        load_ut(cg + 2)
                    if idx >= 8:
                        pcg, pj, _ = seq[idx - 8]
                        if h == 0:
                            tail0(pcg, pj)
                        elif h == 4:
                            tail1(pcg, pj)
                        elif h == 5:
                            tail2(pcg, pj)
                        elif h >= 6:
                            tail_mm(pcg, pj, h - 6)
                    if idx >= 16:
                        ppcg, ppj, _ = seq[idx - 16]
                        if h < 2:
                            tail_mm(ppcg, ppj, h + 2)
                        elif h < 4:
                            tail3(ppcg, ppj, h - 2)
                    if j == 1 and h == 4 and cg + 1 < NCG:
                        load_v(cg + 1)
                    emit_EF(idx, cg, j, h)
                    emit_T(idx, cg, j, h)
                    if cg + 1 < NCG:
                        emit_AT_mm(cg + 1, j, h)
                        if h == 7:
                            pend_copy = j
                            if j == 3:
                                pend_gelu = cg + 1
                    if idx + NY - 1 < len(seq):
                        emit_Y(idx + NY - 1, *seq[idx + NY - 1])
                tail_mm(NCG - 1, 2, 2)
                tail_mm(NCG - 1, 2, 3)
                tail3(NCG - 1, 2, 0)
                tail3(NCG - 1, 2, 1)
                tail0(NCG - 1, 3)
                tail1(NCG - 1, 3)
                tail2(NCG - 1, 3)
                for c in range(4):
                    tail_mm(NCG - 1, 3, c)
                tail3(NCG - 1, 3, 0)
                tail3(NCG - 1, 3, 1)
                def ln2_block(t0=t0):
                    for j in range(4):
                        ti = t0 + j
                        layer_norm(htok.a(ti * D, [1, D]), f"htok{ti}", ti, final, seg)
                if blk == 0:
                    ln2_pending = grab_with_pool([6, 7], ln2_block)
                else:
                    ln2_block()

        for seg in range(nseg):
            load_ln_params(lnin_h, 0, ALPHA)
            for ti in range(NTS):
                row0 = seg * SEG + ti * 128
                P.op("sp", I("dma_start", out=xs.a(0, [1, D]), in_=DAP(x_h, row0 * D, [D, 128], [1, D])),
                     writes=["xs"], dma="xs")
                layer_norm(xs.a(0, [1, D]), "xs", ti, stop == "ln_in", seg)
            if stop == "ln_in":
                continue
            for l in range(nlayers):
                P.barrier()
                mixer(seg, l)
                mark(0)
                P.barrier()
                if stop == "mixer" and l == nlayers - 1:
                    for ti in range(NTS):
                        row0 = seg * SEG + ti * 128
                        P.op("sp", I("dma_start", out=DAP(out_h, row0 * D, [D, 128], [1, D]), in_=htok.a(ti * D, [1, D])),
                            reads=[f"htok{ti}"], dma=f"out{ti}")
                    continue
                peer(seg, l, final=(l == nlayers - 1))
        P.barrier()
        P.emit(st)
    return nc


def _consts():
    c = np.zeros((128, 5, 128), np.float32)
    s = np.arange(128)[:, None]
    t = np.arange(128)[None, :]
    c[:, 0] = np.eye(128)
    c[:, 1] = np.where(s <= t, -1.0 / 16.0, 0.0)
    c[:, 2] = np.where(s > t, -1.0 / 16.0, 0.0)
    c[:, 3] = np.where(s <= t, 1.0, 0.0)
    c[:, 4] = 1.0
    return np.ascontiguousarray(c.reshape(128, 640))


def prep_shared(inp):
    f = lambda a: np.ascontiguousarray(np.asarray(a, dtype=np.float32))
    sh = {
        "cst": _consts(),
        "lnin": f(np.stack([inp["ln_in_g"], inp["ln_in_b"]])),
        "w_in": f(np.asarray(inp["w_in"]).reshape(L * D, INC)),
        "w_out": f(np.asarray(inp["w_out"]).reshape(L * D, D)),
        "wq": f(np.asarray(inp["peer_wq"]).reshape(L * D, 2048)),
        "wgu": f(np.asarray(inp["w_gate_up"]).reshape(L * 16, 256)),
        "bgate": f(inp["b_gate"]),
        "glag": f(inp["gla_norm_g"]),
        "sgln": f(np.stack([np.asarray(inp["sgu_ln_g"]), np.asarray(inp["sgu_ln_b"])], axis=1).reshape(L * 2, 512)),
        "sgwT": f(np.asarray(inp["sgu_w"]).transpose(0, 1, 3, 2).reshape(L * 4 * 128, 128)),
        "sgb": f(np.asarray(inp["sgu_b"]).reshape(L, 512)),
        "ln1": f(np.stack([np.asarray(inp["ln1_g"]), np.asarray(inp["ln1_b"])], axis=1).reshape(L * 2, D)),
        "ln2": f(np.stack([np.asarray(inp["ln2_g"]), np.asarray(inp["ln2_b"])], axis=1).reshape(L * 2, D)),
        "k1T": f(np.asarray(inp["peer_k1"]).transpose(0, 2, 1).reshape(L * 128, 128)),
        "k2T": f(np.asarray(inp["peer_k2"]).transpose(0, 2, 1).reshape(L * 128, 128)),
        "uT": f(np.asarray(inp["peer_u"]).transpose(0, 2, 1).reshape(L * D, NE)),
        "vtab": f(np.asarray(inp["peer_v"]).reshape(L * NE, D)),
    }
    return sh


def kernel(**inputs):
    x = np.asarray(inputs["x"], dtype=np.float32)
    nb = x.shape[0]
    sh = prep_shared(inputs)
    nc = build()
    in_maps = []
    for b in range(nb):
        m = dict(sh)
        m["x"] = np.ascontiguousarray(x[b])
        in_maps.append(m)
    res = run_bass_kernel_spmd(nc, in_maps, core_ids=list(range(nb)))
    return np.stack([np.asarray(r["out"]) for r in res.results], axis=0).astype(np.float32)
```

```python
import numpy as np
from contextlib import ExitStack
import concourse.bass as bass
import concourse.mybir as mybir
from concourse.bass_utils import run_bass_kernel_spmd

F32 = mybir.dt.float32
BF16 = mybir.dt.bfloat16
AF = mybir.ActivationFunctionType
ALU = mybir.AluOpType

D = 1024
NTOK = 2048
L = 2
INC = 2576
SEG = 1024
NTS = 8
NE = 16384
ALPHA = (2.0 * L) ** 0.25
EPS = 1e-5
ENGS = ["pe", "act", "dve", "pool", "sp"]


class Prog:
    def __init__(self, nc, same_engine_sync=True):
        self.nc = nc
        self.streams = {e: [] for e in ENGS}
        self.cnt = {e: 0 for e in ENGS}
        self.dma_cnt = {}
        self.lastw = {}
        self.readers = {}
        self.seen = {e: {} for e in ENGS}
        self.same_engine_sync = same_engine_sync

    def _need(self, eng, tok, waits):
        if tok is None:
            return
        sem, val = tok
        if sem == "E_" + eng and (eng == "pe" or not self.same_engine_sync):
            return
        if self.seen[eng].get(sem, 0) >= val:
            return
        if waits.get(sem, 0) < val:
            waits[sem] = val

    enabled = True
    capture = None

    def op(self, eng, fn, reads=(), writes=(), dma=None):
        if not self.enabled:
            return None
        if self.capture is not None:
            self.capture.append((eng, fn, tuple(reads), tuple(writes), dma))
            return None
        waits = {}
        for r in reads:
            self._need(eng, self.lastw.get(r), waits)
        for w in writes:
            self._need(eng, self.lastw.get(w), waits)
            for sem, val in self.readers.get(w, {}).items():
                self._need(eng, (sem, val), waits)
        for sem, val in waits.items():
            self.seen[eng][sem] = val
        if dma is None:
            self.cnt[eng] += 1
            tok = ("E_" + eng, self.cnt[eng])
        else:
            self.dma_cnt[dma] = self.dma_cnt.get(dma, 0) + 16
            tok = ("D_" + dma, self.dma_cnt[dma])
        self.streams[eng].append((fn, waits, tok))
        for r in reads:
            d = self.readers.setdefault(r, {})
            if d.get(tok[0], 0) < tok[1]:
                d[tok[0]] = tok[1]
        for w in writes:
            self.lastw[w] = tok
            self.readers[w] = {}
        return tok

    def grab(self, fn, *args):
        self.capture = []
        fn(*args)
        ops = self.capture
        self.capture = None
        return ops

    def emit_merged(self, a, b):
        na, nb = len(a), len(b)
        ia = ib = 0
        while ia < na or ib < nb:
            if ib >= nb or (ia < na and ia * nb <= ib * na):
                self.op(*a[ia])
                ia += 1
            else:
                self.op(*b[ib])
                ib += 1

    def wait_only(self, eng, toks):
        waits = {}
        for t in toks:
            self._need(eng, t, waits)
        for sem, val in waits.items():
            self.seen[eng][sem] = val
        if waits:
            self.streams[eng].append((None, waits, None))

    def all_tokens(self):
        toks = [("E_" + e, self.cnt[e]) for e in ENGS if self.cnt[e] > 0]
        toks += [("D_" + k, v) for k, v in self.dma_cnt.items()]
        return toks

    def barrier(self):
        toks = self.all_tokens()
        for e in ENGS:
            self.wait_only(e, toks)

    def emit(self, stack):
        nc = self.nc
        names = ["E_" + e for e in ENGS if self.cnt[e] > 0] + ["D_" + k for k in self.dma_cnt]
        sems = {n: stack.enter_context(nc.semaphore(n)) for n in names}
        block = stack.enter_context(nc.Block())
        deco = {"pe": block.tensor, "act": block.scalar, "dve": block.vector,
                "pool": block.gpsimd, "sp": block.sync}
        for e in ENGS:
            stream = self.streams[e]
            if not stream:
                continue

            def body(eng, stream=stream):
                for fn, waits, tok in stream:
                    for sem, val in waits.items():
                        eng.wait_ge(sems[sem], val)
                    if fn is None:
                        continue
                    inst = fn(eng)
                    inst.then_inc(sems[tok[0]], 16 if tok[0].startswith("D_") else 1)

            deco[e](body)


def I(name, *args, **kw):
    return lambda e: getattr(e, name)(*args, **kw)


class Buf:
    def __init__(self, t, F, base=0):
        self.t = t
        self.F = F
        self.base = base

    def a(self, off, *dims, p0=0, np_=128):
        return bass.AP(self.t, p0 * self.F + self.base + off, [[self.F, np_]] + [list(d) for d in dims])

    def sub(self, base):
        return Buf(self.t, self.F, self.base + base)


DBG_MAP = {}


def build(nseg=2, nlayers=L, stop=None, cut=99, dbg=None):
    nc = bass.Bass("TRN2", target_bir_lowering=False)
    DBG_MAP.clear()
    dbg_h = nc.dram_tensor("dbg", [128, 8192], F32, kind="ExternalOutput") if dbg is not None else None
    dt = lambda n, s: nc.dram_tensor(n, s, F32, kind="ExternalInput")
    x_h = dt("x", [NTOK, D])
    cst_h = dt("cst", [128, 640])
    lnin_h = dt("lnin", [2, D])
    win_h = dt("w_in", [L * D, INC])
    wout_h = dt("w_out", [L * D, D])
    wq_h = dt("wq", [L * D, 2048])
    wgu_h = dt("wgu", [L * 16, 256])
    bgate_h = dt("bgate", [L, 256])
    glag_h = dt("glag", [L, 128])
    sgln_h = dt("sgln", [L * 2, 512])
    sgwT_h = dt("sgwT", [L * 4 * 128, 128])
    sgb_h = dt("sgb", [L, 512])
    ln1_h = dt("ln1", [L * 2, D])
    ln2_h = dt("ln2", [L * 2, D])
    k1T_h = dt("k1T", [L * 128, 128])
    k2T_h = dt("k2T", [L * 128, 128])
    uT_h = dt("uT", [L * D, NE])
    vt_h = dt("vtab", [L * NE, D])
    out_h = nc.dram_tensor("out", [NTOK, D], F32, kind="ExternalOutput")
    DAP = lambda h, off, *dims: bass.AP(h, off, [list(d) for d in dims])

    with ExitStack() as st:
        def sb(name, F, dtype):
            return Buf(st.enter_context(nc.sbuf_tensor(name, [128, F], dtype)), F)

        htok = sb("htok", NTS * D, F32)
        hT = sb("hT", 8 * SEG, BF16)
        W = sb("W", 8 * INC, BF16)
        Areg = sb("Areg", 8 * D, BF16)
        ARENA_BYTES = 54 * 1024
        arena_t = st.enter_context(nc.sbuf_tensor("arena", [128, ARENA_BYTES // 2], BF16))
        arena_f = arena_t.bitcast(F32)
        lnG = sb("lnG", D, F32)
        lnB = sb("lnB", D, F32)
        xs = sb("xs", D, F32)
        yn = sb("yn", D, F32)
        tt = sb("tt", D, F32)
        hbf = sb("hbf", D, BF16)
        junk = hbf
        cst = sb("cstf", 640, F32)
        ident = sb("ident", 128, BF16)
        ones = sb("onesb", 128, BF16)
        stt_ = sb("stat", 64, F32)
        Sf = sb("Sf", L * 512, F32)
        Sb = sb("Sb", L * 512, BF16)
        small = sb("small", 128, F32)
        ps = [Buf(st.enter_context(nc.psum_tensor(f"ps{i}", [128, 512], F32)), 512) for i in range(8)]

        class Arena:
            def __init__(self):
                self.off = 0

            def alloc(self, nelem, dtype):
                sz = 4 if dtype == F32 else 2
                self.off = (self.off + 63) // 64 * 64
                o = self.off
                self.off += nelem * sz
                assert self.off <= ARENA_BYTES, ("arena overflow", self.off)
                if dtype == F32:
                    return Buf(arena_f, ARENA_BYTES // 4, o // 4)
                return Buf(arena_t, ARENA_BYTES // 2, o // 2)

        am = Arena()
        qTf = am.alloc(2048, BF16)
        kTf = am.alloc(2048, BF16)
        silur = am.alloc(2048, BF16)
        gsu = am.alloc(2048, BF16)
        aT = am.alloc(512, BF16)
        Lt = am.alloc(256, F32)
        eb = am.alloc(512, F32)
        enb = am.alloc(512, F32)
        erem = am.alloc(256, F32)
        qtil = am.alloc(512, BF16)
        ktil = am.alloc(512, BF16)
        khat = am.alloc(256, BF16)
        v_bf = am.alloc(512, BF16)
        attn_bf = am.alloc(512, BF16)
        sq = am.alloc(512, BF16)
        rstd = am.alloc(512, F32)
        t1 = am.alloc(512, F32)
        catT = am.alloc(1024, BF16)
        gv = am.alloc(512, F32)
        vn = am.alloc(512, BF16)
        sg_g = am.alloc(512, F32)
        sg_b = am.alloc(512, F32)
        WsT = am.alloc(512, BF16)
        wgu = am.alloc(256, BF16)
        bgate = am.alloc(256, BF16)
        gcol = am.alloc(1, F32)
        bs_f = am.alloc(512, F32)
        bs_hi = am.alloc(512, BF16)
        bs_lo = am.alloc(512, BF16)
        bs_t = am.alloc(512, F32)

        ap_ = Arena()
        qT = ap_.alloc(16 * 512, BF16)
        UT = ap_.alloc(2 * 4096, BF16)
        HT = ap_.alloc(2 * 512, BF16)
        gAs = ap_.alloc(8 * 512, BF16)
        Gb = ap_.alloc(2 * 512, BF16)
        K1T = ap_.alloc(128, BF16)
        K2T = ap_.alloc(128, BF16)
        c16 = ap_.alloc(512, F32)
        candb = ap_.alloc(256, F32)
        scr = ap_.alloc(256, F32)
        v16 = ap_.alloc(256, F32)
        PAB = Buf(yn.t.bitcast(BF16), 2 * D)
        Ssum = Buf(tt.t.bitcast(BF16), 2 * D)
        rawAT = Buf(xs.t.bitcast(BF16), 2 * D)
        Esl = ap_.alloc(3 * 512, BF16)
        Fsl = W.sub(16384)
        biasv = small.sub(0)
        negm = small.sub(32)
        Zs = small.sub(40)
        lnZ = small.sub(48)
        j16 = small.sub(64)
        sst = small.sub(96)

        P = Prog(nc)
        bank_ctr = [0]

        bank_pool = [list(range(8))]

        def bk():
            pool = bank_pool[0]
            b = pool[bank_ctr[0] % len(pool)]
            bank_ctr[0] += 1
            return b

        def grab_with_pool(pool, fn, *args):
            bank_pool[0] = pool
            ops = P.grab(fn, *args)
            bank_pool[0] = list(range(8))
            return ops

        pr = lambda b: f"ps{b}"

        def mark(n):
            P.enabled = n <= cut

        dbg_col = [0]

        def dump(key, name, apfn, ncols, reads, np_=128):
            if dbg is None or tuple(dbg) != tuple(key) or not P.enabled:
                return
            c0 = dbg_col[0]
            dbg_col[0] += ncols
            DBG_MAP[name] = (c0, ncols, np_)
            P.op("pool", I("dma_start", out=bass.AP(dbg_h, c0, [[8192, np_], [1, ncols]]), in_=apfn()),
                 reads=reads, dma="dbg")

        P.op("sp", I("dma_start", out=cst.a(0, [1, 640]), in_=DAP(cst_h, 0, [640, 128], [1, 640])),
             writes=["cst"], dma="cst")
        P.op("pool", I("dma_start", out=ident.a(0, [1, 128]), in_=DAP(cst_h, 0, [640, 128], [1, 128])),
             writes=["ident"], dma="ident")
        P.op("pool", I("dma_start", out=ones.a(0, [1, 128]), in_=DAP(cst_h, 512, [640, 128], [1, 128])),
             writes=["ones"], dma="ones")
        P.op("dve", I("memset", Sf.a(0, [1, L * 512]), 0.0), writes=[f"Sf{l}" for l in range(L)])
        P.op("dve", I("memset", Sb.a(0, [1, L * 512]), 0.0), writes=[f"Sb{l}" for l in range(L)])
        CI, CTRI, CTRI2, CCAUS = 0, 128, 256, 384

        def load_ln_params(h, row0, scale):
            P.op("sp", I("dma_start", out=lnG.a(0, [1, D]), in_=DAP(h, row0 * D, [0, 128], [1, D])),
                 writes=["lnG"], dma="lnG")
            P.op("sp", I("dma_start", out=lnB.a(0, [1, D]), in_=DAP(h, (row0 + 1) * D, [0, 128], [1, D])),
                 writes=["lnB"], dma="lnB")
            if scale != 1.0:
                P.op("pool", I("tensor_scalar", out=lnG.a(0, [1, D]), in0=lnG.a(0, [1, D]), scalar1=scale,
                                                       scalar2=None, op0=ALU.mult), reads=["lnG"], writes=["lnG"])
                P.op("pool", I("tensor_scalar", out=lnB.a(0, [1, D]), in0=lnB.a(0, [1, D]), scalar1=scale,
                                                       scalar2=None, op0=ALU.mult), reads=["lnB"], writes=["lnB"])

        stat_ctr = [0]
        lnset = [0]
        ynB = Buf(arena_f, ARENA_BYTES // 4, 0)
        ttB = Buf(arena_f, ARENA_BYTES // 4, 1024)
        hbfB = Buf(arena_t, ARENA_BYTES // 2, 4096)
        xsB = Buf(arena_f, ARENA_BYTES // 4, 2560)

        def stats_chain(sbuf, s0, n, r="statchain"):
            c = lambda k: sbuf.a(s0 + k, [1, 1])
            P.op("dve", I("tensor_scalar", out=c(2), in0=c(0), scalar1=1.0 / n, scalar2=None, op0=ALU.mult),
                 reads=[r], writes=[r])
            P.op("dve", I("tensor_tensor", out=c(3), in0=c(2), in1=c(2), op=ALU.mult), reads=[r], writes=[r])
            P.op("dve", I("scalar_tensor_tensor", out=c(4), in0=c(1), scalar=1.0 / n, in1=c(3),
                                                         op0=ALU.mult, op1=ALU.subtract), reads=[r], writes=[r])
            P.op("act", I("activation", out=c(5), in_=c(4), func=AF.Ln, bias=EPS), reads=[r], writes=[r])
            P.op("act", I("activation", out=c(6), in_=c(5), func=AF.Exp, scale=-0.5), reads=[r], writes=[r])
            P.op("dve", I("scalar_tensor_tensor", out=c(7), in0=c(2), scalar=-1.0, in1=c(6),
                                                         op0=ALU.mult, op1=ALU.mult), reads=[r], writes=[r])

        def layer_norm(src, src_res, ti, final, seg):
            if lnset[0] == 0:
                yn_, tt_, hbf_, nyn, ntt, nhbf, rchain, sbase, acq = yn, tt, hbf, "yn", "tt", "hbf", "statchain", 0, []
            else:
                yn_, tt_, hbf_, nyn, ntt, nhbf, rchain, sbase, acq = ynB, ttB, hbfB, "ynB", "ttB", "hbfB", "statchainB", 32, ["qT"]
            junk_ = hbf_
            s0 = sbase + (stat_ctr[0] % 4) * 8
            stat_ctr[0] += 1
            hres = f"htok{ti}"
            P.op("act", I("activation", out=junk_.a(0, [1, D]), in_=src, func=AF.Identity,
                                               accum_out=stt_.a(s0, [1, 1])),
                 reads=[src_res], writes=[nhbf, rchain] + acq)
            P.op("act", I("activation", out=junk_.a(0, [1, D]), in_=src, func=AF.Square,
                                               accum_out=stt_.a(s0 + 1, [1, 1])),
                 reads=[src_res], writes=[nhbf, rchain])
            stats_chain(stt_, s0, float(D), rchain)
            P.op("act", I("activation", out=yn_.a(0, [1, D]), in_=src, func=AF.Identity,
                                               scale=stt_.a(s0 + 6, [1, 1]), bias=stt_.a(s0 + 7, [1, 1])),
                 reads=[src_res, rchain], writes=[nyn])
            P.op("dve", I("tensor_tensor", out=tt_.a(0, [1, D]), in0=yn_.a(0, [1, D]), in1=lnG.a(0, [1, D]),
                                                   op=ALU.mult), reads=[nyn, "lnG"], writes=[ntt])
            P.op("dve", I("tensor_tensor", out=htok.a(ti * D, [1, D]), in0=tt_.a(0, [1, D]),
                                                  in1=lnB.a(0, [1, D]), op=ALU.add),
                 reads=[ntt, "lnB", src_res], writes=[hres])
            if final:
                row0 = seg * SEG + ti * 128
                P.op("sp", I("dma_start", out=DAP(out_h, row0 * D, [D, 128], [1, D]), in_=htok.a(ti * D, [1, D])),
                     reads=[hres], dma=f"out{ti}")
                return
            P.op("act", I("activation", out=hbf_.a(0, [1, D]), in_=htok.a(ti * D, [1, D]), func=AF.Copy,
                                               scale=1.0 / ALPHA), reads=[hres], writes=[nhbf])
            for half in range(2):
                b = bk()
                for k4 in range(4):
                    kc = half * 4 + k4
                    P.op("pe", I("matmul", ps[b].a(k4 * 128, [1, 128]), hbf_.a(kc * 128, [1, 128]), ident.a(0, [1, 128]),
                        start=True, stop=True), reads=[nhbf, "ident"], writes=[pr(b)])
                eng = "act" if half == 0 else "dve"
                dst = hT.a((half * 4) * SEG + ti * 128, [SEG, 4], [1, 128])
                srcp = ps[b].a(0, [128, 4], [1, 128])
                if eng == "act":
                    P.op("act", I("copy", out=dst, in_=srcp), reads=[pr(b)], writes=[f"hT{ti}"])
                else:
                    P.op("dve", I("tensor_copy", out=dst, in_=srcp), reads=[pr(b)],
                         writes=[f"hT{ti}"])

        def mixer(seg, l):
            for c0, cw, rn in ((0, 1280, "Wa"), (1280, 1296, "Wb")):
                P.op("pool", I("dma_start", out=W.a(c0, [INC, 8], [1, cw]),
                    in_=DAP(win_h, l * D * INC + c0, [INC, 128], [128 * INC, 8], [1, cw])),
                    writes=[rn], dma=rn)

            def wres(c0, n):
                r = []
                if c0 < 1280:
                    r.append("Wa")
                if c0 + n > 1280:
                    r.append("Wb")
                return r
            P.op("pool", I("dma_start", out=Areg.a(0, [D, 8], [1, D]),
                                               in_=DAP(wout_h, l * D * D, [D, 128], [128 * D, 8], [1, D])),
                 writes=["A0", "A1"], dma="A")
            P.op("dve", I("memset", wgu.a(0, [1, 256], np_=32), 0.0), writes=["wgu"])
            P.op("dve", I("memset", bgate.a(0, [1, 256], np_=32), 0.0), writes=["bgate"])
            P.op("dve", I("memset", aT.a(0, [1, 512], np_=32), 0.0), writes=["aT"])
            P.op("dve", I("memset", bs_hi.a(0, [1, 512], np_=32), 0.0), writes=["bs_hi"])
            P.op("dve", I("memset", bs_lo.a(0, [1, 512], np_=32), 0.0), writes=["bs_lo"])
            P.op("pool", I("dma_start", out=wgu.a(0, [1, 256], np_=16), in_=DAP(wgu_h, l * 16 * 256, [256, 16], [1, 256])),
                 writes=["wgu"], dma="wgu")
            P.op("pool", I("dma_start", out=bgate.a(0, [1, 256], np_=1), in_=DAP(bgate_h, l * 256, [256, 1], [1, 256])),
                 writes=["bgate"], dma="bgate")
            P.op("pool", I("dma_start", out=WsT.a(0, [128, 4], [1, 128]),
                                               in_=DAP(sgwT_h, l * 4 * 128 * 128, [128, 128], [128 * 128, 4], [1, 128])),
                 writes=["WsT"], dma="WsT")
            P.op("sp", I("dma_start", out=gcol.a(0, [1, 1]), in_=DAP(glag_h, l * 128, [1, 128], [1, 1])),
                 writes=["gcol"], dma="gcol")
            P.op("sp", I("dma_start", out=sg_g.a(0, [1, 512]), in_=DAP(sgln_h, (2 * l) * 512, [0, 128], [1, 512])),
                 writes=["sg_g"], dma="sg_g")
            P.op("sp", I("dma_start", out=sg_b.a(0, [1, 512]), in_=DAP(sgln_h, (2 * l + 1) * 512, [0, 128], [1, 512])),
                 writes=["sg_b"], dma="sg_b")
            P.op("sp", I("dma_start", out=bs_f.a(0, [1, 512], np_=1), in_=DAP(sgb_h, l * 512, [512, 1], [1, 512])),
                 writes=["bs_f"], dma="bs_f")
            load_ln_params(ln1_h, 2 * l, ALPHA)
            P.op("dve", I("memset", WsT.a(0, [128, 4], [1, 64], p0=64, np_=64), 0.0), reads=["WsT"], writes=["WsT"])
            P.op("dve", I("tensor_copy", out=bs_hi.a(0, [1, 512], np_=1), in_=bs_f.a(0, [1, 512], np_=1)),
                 reads=["bs_f"], writes=["bs_hi"])
            P.op("dve", I("tensor_tensor", out=bs_t.a(0, [1, 512], np_=1), in0=bs_f.a(0, [1, 512], np_=1),
                                                  in1=bs_hi.a(0, [1, 512], np_=1), op=ALU.subtract),
                 reads=["bs_f", "bs_hi"], writes=["bs_t"])
            P.op("dve", I("tensor_copy", out=bs_lo.a(0, [1, 512], np_=1), in_=bs_t.a(0, [1, 512], np_=1)),
                 reads=["bs_t"], writes=["bs_lo"])

            def proj_fm(c0, m, q):
                b = bk()
                hres = [f"hT{4 * q + j}" for j in range(4)]
                for kc in range(8):
                    P.op("pe", I("matmul", ps[b].a(0, [1, 512], np_=m), W.a(kc * INC + c0, [1, m]), hT.a(kc * SEG + q * 512, [1, 512]),
                        start=(kc == 0), stop=(kc == 7)), reads=wres(c0, m) + hres, writes=[pr(b)])
                return b

            def proj_tm(c0, n, ti):
                b = bk()
                for kc in range(8):
                    P.op("pe", I("matmul", ps[b].a(0, [1, n]), hT.a(kc * SEG + ti * 128, [1, 128]), W.a(kc * INC + c0, [1, n]),
                        start=(kc == 0), stop=(kc == 7)), reads=wres(c0, n) + [f"hT{ti}"], writes=[pr(b)])
                return b

            for q in range(2):
                mark(1)
                for i in range(4):
                    b = proj_fm(i * 64, 64, q)
                    P.op("act", I("copy", out=qTf.a(i * 512, [1, 512], np_=64),
                                                           in_=ps[b].a(0, [1, 512], np_=64)),
                         reads=[pr(b)], writes=["qTf"])
                for i in range(4):
                    b = proj_fm(256 + i * 64, 64, q)
                    P.op("dve", I("tensor_copy", out=kTf.a(i * 512, [1, 512], np_=64),
                                                                  in_=ps[b].a(0, [1, 512], np_=64)),
                         reads=[pr(b)], writes=["kTf"])
                for i in range(4):
                    b = proj_fm(1024 + i * 128, 128, q)
                    P.op("act", I("activation", out=silur.a(i * 512, [1, 512]), in_=ps[b].a(0, [1, 512]),
                                                                 func=AF.Silu), reads=[pr(b)], writes=["silur"])
                for i in range(4):
                    b = proj_fm(1552 + i * 128, 128, q)
                    P.op("act", I("activation", out=gsu.a(i * 512, [1, 512]), in_=ps[b].a(0, [1, 512]),
                                                                 func=AF.Gelu), reads=[pr(b)], writes=["gsu"])
                b = proj_fm(1536, 16, q)
                P.op("act", I("copy", out=aT.a(0, [1, 512], np_=16), in_=ps[b].a(0, [1, 512], np_=16)),
                     reads=[pr(b)], writes=["aT"])

                def tile_body(j, part):
                    ti = 4 * q + j
                    tc0 = j * 128
                    key = (seg, l, ti)
                    if part == 0:
                        mark(2)
                        b = bk()
                        P.op("pe", I("matmul", ps[b].a(0, [1, 256]), aT.a(tc0, [1, 128], np_=32),
                                                           wgu.a(0, [1, 256], np_=32), start=True, stop=False),
                             reads=["aT", "wgu"], writes=[pr(b)])
                        P.op("pe", I("matmul", ps[b].a(0, [1, 256]), ones.a(0, [1, 128], np_=32),
                                                           bgate.a(0, [1, 256], np_=32), start=False, stop=True),
                             reads=["ones", "bgate"], writes=[pr(b)])
                        P.op("act", I("activation", out=Lt.a(0, [1, 256]), in_=ps[b].a(0, [1, 256]), func=AF.Exp,
                                                                scale=-1.0), reads=[pr(b)], writes=["Lt"])
                        dump(key, "aT", lambda: aT.a(0, [1, 512], np_=16), 512, ["aT"], np_=16)
                        dump(key, "wgu", lambda: wgu.a(0, [1, 256], np_=16), 256, ["wgu"], np_=16)
                        dump(key, "bgate", lambda: bgate.a(0, [1, 256], np_=1), 256, ["bgate"], np_=1)
                        dump(key, "expnx", lambda: Lt.a(0, [1, 256]), 256, ["Lt"])
                        P.op("act", I("activation", out=Lt.a(0, [1, 256]), in_=Lt.a(0, [1, 256]), func=AF.Ln, bias=1.0),
                             reads=["Lt"], writes=["Lt"])
                        mark(3)
                        dump(key, "Lt", lambda: Lt.a(0, [1, 256]), 256, ["Lt"])
                        b2 = bk()
                        for h in range(4):
                            P.op("pe", I("matmul", ps[b2].a(h * 128, [1, 128], np_=64), Lt.a(h * 64, [1, 64]),
                                                                      cst.a(CTRI, [1, 128]), start=True, stop=True),
                                 reads=["Lt", "cst"], writes=[pr(b2)])
                        b3 = bk()
                        P.op("pe", I("matmul", ps[b3].a(0, [1, 256]), cst.a(CTRI2, [1, 128]), Lt.a(0, [1, 256]),
                                                             start=True, stop=True), reads=["Lt", "cst"], writes=[pr(b3)])
                        P.op("act", I("activation", out=eb.a(0, [1, 512], np_=64), in_=ps[b2].a(0, [1, 512], np_=64),
                                                                  func=AF.Exp), reads=[pr(b2)], writes=["eb"])
                        P.op("act", I("activation", out=enb.a(0, [1, 512], np_=64), in_=ps[b2].a(0, [1, 512], np_=64),
                                                                  func=AF.Exp, scale=-1.0), reads=[pr(b2)], writes=["enb"])
                        P.op("act", I("activation", out=erem.a(0, [1, 256]), in_=ps[b3].a(0, [1, 256]), func=AF.Exp),
                             reads=[pr(b3)], writes=["erem"])
                        P.op("dve", I("scalar_tensor_tensor", out=qtil.a(0, [128, 4], [1, 128], np_=64), in0=qTf.a(tc0, [512, 4], [1, 128], np_=64), scalar=0.125,
                            in1=eb.a(0, [128, 4], [1, 128], np_=64), op0=ALU.mult, op1=ALU.mult),
                            reads=["qTf", "eb"], writes=["qtil"])
                        P.op("dve", I("tensor_tensor", out=ktil.a(0, [128, 4], [1, 128], np_=64), in0=kTf.a(tc0, [512, 4], [1, 128], np_=64),
                            in1=enb.a(0, [128, 4], [1, 128], np_=64), op=ALU.mult), reads=["kTf", "enb"], writes=["ktil"])
                        dump(key, "eb", lambda: eb.a(0, [1, 512], np_=64), 512, ["eb"], np_=64)
                        dump(key, "erem", lambda: erem.a(0, [1, 256]), 256, ["erem"])
                        dump(key, "qtil", lambda: qtil.a(0, [1, 512], np_=64), 512, ["qtil"], np_=64)
                        dump(key, "ktil", lambda: ktil.a(0, [1, 512], np_=64), 512, ["ktil"], np_=64)
                        mark(4)
                        b4 = proj_tm(256, 256, ti)
                        mark(4.02)
                        P.op("dve", I("tensor_tensor", out=khat.a(0, [1, 256]), in0=ps[b4].a(0, [1, 256]),
                                                                     in1=erem.a(0, [1, 256]), op=ALU.mult),
                             reads=[pr(b4), "erem"], writes=["khat"])
                        mark(4.03)
                        b5 = proj_tm(512, 512, ti)
                        mark(4.04)
                        P.op("act", I("copy", out=v_bf.a(0, [1, 512]), in_=ps[b5].a(0, [1, 512])),
                             reads=[pr(b5)], writes=["v_bf"])
                    if part == 1:
                        mark(4.2)
                        b6 = bk()
                        for h in range(4):
                            c, pb = h // 2, (h % 2) * 64
                            P.op("pe", I("matmul", ps[b6].a(h * 128, [1, 128]), ktil.a(h * 128, [1, 128], np_=64),
                                qtil.a(h * 128, [1, 128], np_=64), start=True, stop=True),
                                reads=["ktil", "qtil"], writes=[pr(b6)])
                        mark(4.4)
                        P.op("dve", I("tensor_tensor", out=attn_bf.a(0, [128, 4], [1, 128]), in0=ps[b6].a(0, [128, 4], [1, 128]),
                            in1=cst.a(CCAUS, [0, 4], [1, 128]), op=ALU.mult), reads=[pr(b6), "cst"], writes=["attn_bf"])
                        dump(key, "khat", lambda: khat.a(0, [1, 256]), 256, ["khat"])
                        dump(key, "v_bf", lambda: v_bf.a(0, [1, 512]), 512, ["v_bf"])
                        dump(key, "attn", lambda: attn_bf.a(0, [1, 512]), 512, ["attn_bf"])
                        mark(4.6)
                        b7 = bk()
                        for h in range(4):
                            c, pb = h // 2, (h % 2) * 64
                            P.op("pe", I("matmul", ps[b7].a(h * 128, [1, 128]), v_bf.a(h * 128, [1, 128]), attn_bf.a(h * 128, [1, 128]),
                                start=True, stop=False), reads=["v_bf", "attn_bf"], writes=[pr(b7)])
                            P.op("pe", I("matmul", ps[b7].a(h * 128, [1, 128]), Sb.a(l * 512 + h * 128, [1, 128], np_=64),
                                qtil.a(h * 128, [1, 128], np_=64), start=False, stop=True),
                                reads=[f"Sb{l}", "qtil"], writes=[pr(b7)])
                        mark(5)
                        b8 = bk()
                        for h in range(4):
                            P.op("pe", I("matmul", ps[b8].a(h * 128, [1, 128], np_=64), khat.a(h * 64, [1, 64]), v_bf.a(h * 128, [1, 128]),
                                start=True, stop=True), reads=["khat", "v_bf"], writes=[pr(b8)])
                        for h in range(4):
                            so = l * 512 + h * 128
                            P.op("dve", I("scalar_tensor_tensor", out=Sf.a(so, [1, 128], np_=64), in0=Sf.a(so, [1, 128], np_=64),
                                scalar=eb.a(h * 128 + 127, [1, 1], np_=64),
                                in1=ps[b8].a(h * 128, [1, 128], np_=64),
                                op0=ALU.mult, op1=ALU.add), reads=[f"Sf{l}", "eb", pr(b8)], writes=[f"Sf{l}"])
                        P.op("act", I("copy", out=Sb.a(l * 512, [1, 512], np_=64), in_=Sf.a(l * 512, [1, 512], np_=64)),
                             reads=[f"Sf{l}"], writes=[f"Sb{l}"])
                        mark(6)
                        P.op("act", I("activation", out=sq.a(0, [1, 512]), in_=ps[b7].a(0, [1, 512]), func=AF.Square),
                             reads=[pr(b7)], writes=["sq"])
                        b9 = bk()
                        P.op("pe", I("matmul", ps[b9].a(0, [1, 512]), ones.a(0, [1, 128]), sq.a(0, [1, 512]),
                                                             start=True, stop=True), reads=["ones", "sq"], writes=[pr(b9)])
                        P.op("act", I("activation", out=rstd.a(0, [1, 512]), in_=ps[b9].a(0, [1, 512]), func=AF.Ln,
                                                              scale=1.0 / 128.0, bias=EPS), reads=[pr(b9)], writes=["rstd"])
                        P.op("act", I("activation", out=rstd.a(0, [1, 512]), in_=rstd.a(0, [1, 512]), func=AF.Exp, scale=-0.5),
                             reads=["rstd"], writes=["rstd"])
                        P.op("dve", I("scalar_tensor_tensor", out=t1.a(0, [1, 512]), in0=ps[b7].a(0, [1, 512]), scalar=gcol.a(0, [1, 1]), in1=rstd.a(0, [1, 512]),
                            op0=ALU.mult, op1=ALU.mult), reads=[pr(b7), "gcol", "rstd"], writes=["t1"])
                        P.op("dve", I("tensor_tensor", out=catT.a(0, [128, 4], [1, 128]), in0=t1.a(0, [128, 4], [1, 128]),
                            in1=silur.a(tc0, [512, 4], [1, 128]), op=ALU.mult), reads=["t1", "silur"], writes=["catT_o"])
                        dump(key, "t1", lambda: t1.a(0, [1, 512]), 512, ["t1"])
                        dump(key, "Sf", lambda: Sf.a(l * 512, [1, 512], np_=64), 512, [f"Sf{l}"], np_=64)
                    if part == 2:
                        mark(7)
                        b10 = proj_tm(2064, 512, ti)
                        P.op("act", I("activation", out=gv.a(0, [1, 512]), in_=ps[b10].a(0, [1, 512]), func=AF.Gelu,
                                                                    accum_out=sst.a(0, [1, 1])),
                             reads=[pr(b10)], writes=["gv", "statchain"])
                        P.op("act", I("activation", out=junk.a(0, [1, 512]), in_=gv.a(0, [1, 512]), func=AF.Square,
                                                           accum_out=sst.a(1, [1, 1])),
                             reads=["gv"], writes=["hbf", "statchain"])
                        stats_chain(sst, 0, 512.0)
                        P.op("act", I("activation", out=gv.a(0, [1, 512]), in_=gv.a(0, [1, 512]), func=AF.Identity,
                                                           scale=sst.a(6, [1, 1]), bias=sst.a(7, [1, 1])),
                             reads=["gv", "statchain"], writes=["gv"])
                        P.op("dve", I("tensor_tensor", out=gv.a(0, [1, 512]), in0=gv.a(0, [1, 512]), in1=sg_g.a(0, [1, 512]),
                                                               op=ALU.mult), reads=["gv", "sg_g"], writes=["gv"])
                        P.op("dve", I("tensor_tensor", out=vn.a(0, [1, 512]), in0=gv.a(0, [1, 512]), in1=sg_b.a(0, [1, 512]),
                                                              op=ALU.add), reads=["gv", "sg_b"], writes=["vn"])
                        b11 = bk()
                        for g in range(4):
                            P.op("pe", I("matmul", ps[b11].a(g * 128, [1, 128]), vn.a(g * 128, [1, 128]), WsT.a(g * 128, [1, 128]),
                                start=True, stop=False), reads=["vn", "WsT"], writes=[pr(b11)])
                            P.op("pe", I("matmul", ps[b11].a(g * 128, [1, 128]), ones.a(0, [1, 128], np_=32), bs_hi.a(g * 128, [1, 128], np_=32),
                                start=False, stop=False), reads=["ones", "bs_hi"], writes=[pr(b11)])
                            P.op("pe", I("matmul", ps[b11].a(g * 128, [1, 128]), ones.a(0, [1, 128], np_=32), bs_lo.a(g * 128, [1, 128], np_=32),
                                start=False, stop=True), reads=["ones", "bs_lo"], writes=[pr(b11)])
                        P.op("dve", I("tensor_tensor", out=catT.a(512, [128, 4], [1, 128]), in0=ps[b11].a(0, [128, 4], [1, 128]),
                            in1=gsu.a(tc0, [512, 4], [1, 128]), op=ALU.mult), reads=[pr(b11), "gsu"], writes=["catT_g"])
                        dump(key, "vn", lambda: vn.a(0, [1, 512]), 512, ["vn"])
                        dump(key, "catT", lambda: catT.a(0, [1, 1024]), 1024, ["catT_o", "catT_g"])
                        mark(8)
                        for dh in range(2):
                            b12 = bk()
                            for cc in range(8):
                                P.op("pe", I("matmul", ps[b12].a(0, [1, 512]), catT.a(cc * 128, [1, 128]), Areg.a(cc * D + dh * 512, [1, 512]),
                                    start=(cc == 0), stop=(cc == 7)),
                                    reads=["catT_o", "catT_g", "A0", "A1"], writes=[pr(b12)])
                            P.op("dve", I("tensor_tensor", out=htok.a(ti * D + dh * 512, [1, 512]), in0=ps[b12].a(0, [1, 512]),
                                in1=htok.a(ti * D + dh * 512, [1, 512]), op=ALU.add),
                                reads=[pr(b12), f"htok{ti}"], writes=[f"htok{ti}"])
                        dump(key, "hpre", lambda: htok.a(ti * D, [1, D]), 1024, [f"htok{ti}"])
                    if part == 3:
                        layer_norm(htok.a(ti * D, [1, D]), f"htok{ti}", ti, False, seg)

                G = lambda jj, part: grab_with_pool([0, 1, 2, 3, 4] if part < 2 else [5, 6, 7], tile_body, jj, part)
                P.emit_merged(G(0, 0), [])
                P.emit_merged(G(0, 1), [])
                for jj in range(1, 4):
                    P.emit_merged(G(jj, 0), G(jj - 1, 2))
                    P.emit_merged(G(jj, 1), G(jj - 1, 3))
                P.emit_merged(G(3, 2), [])
                P.emit_merged(G(3, 3), [])

        def peer(seg, l, final):
            for g4 in range(4):
                P.op("pool", I("dma_start", out=W.a(g4 * 512, [2048, 8], [1, 512]),
                               in_=DAP(wq_h, l * D * 2048 + g4 * 512, [2048, 128], [128 * 2048, 8], [1, 512])),
                     writes=[f"Wq{g4}"], dma=f"Wq{g4}")
            P.op("pool", I("dma_start", out=K1T.a(0, [1, 128]), in_=DAP(k1T_h, l * 128 * 128, [128, 128], [1, 128])),
                 writes=["K1T"], dma="K1T")
            P.op("pool", I("dma_start", out=K2T.a(0, [1, 128]), in_=DAP(k2T_h, l * 128 * 128, [128, 128], [1, 128])),
                 writes=["K2T"], dma="K2T")
            load_ln_params(ln2_h, 2 * l, 1.0 if final else ALPHA)
            cgi = [0]

            def load_ut(cg):
                s = cg % 2
                P.op("pool", I("dma_start",
                    out=UT.a(s * 4096, [512, 8], [1, 512]),
                    in_=DAP(uT_h, l * D * NE + cg * 512, [NE, 128], [128 * NE, 8], [1, 512])),
                    writes=[f"UT{s}"], dma=f"UT{s}")

            def load_v(cg):
                s = cg % 2
                P.op("pool", I("dma_start",
                    out=Areg.a(s * 4096, [D, 4], [1, D]),
                    in_=DAP(vt_h, (l * NE + cg * 512) * D, [D, 128], [128 * D, 4], [1, D])),
                    writes=[f"A{s}"], dma=f"V{s}")

            ln2_pending = []
            for blk in range(2):
                t0 = blk * 4
                bc0 = blk * 512
                hres = [f"hT{t0 + j}" for j in range(4)]
                bank_pool[0] = [0, 1, 2, 3, 4, 5]
                P.capture = []
                for jq in range(16):
                    b = bk()
                    for kc in range(8):
                        P.op("pe", I("matmul", ps[b].a(0, [1, 512]), W.a(kc * 2048 + jq * 128, [1, 128]), hT.a(kc * SEG + bc0, [1, 512]),
                            start=(kc == 0), stop=(kc == 7)), reads=[f"Wq{jq // 4}"] + hres, writes=[pr(b)])
                    if jq % 2 == 0:
                        P.op("act", I("copy", out=qT.a(jq * 512, [1, 512]), in_=ps[b].a(0, [1, 512])),
                             reads=[pr(b)], writes=["qT"])
                    else:
                        P.op("dve", I("tensor_copy", out=qT.a(jq * 512, [1, 512]), in_=ps[b].a(0, [1, 512])),
                             reads=[pr(b)], writes=["qT"])
                for j in range(4):
                    tc0 = j * 128
                    banks = [bk() for _ in range(4)]
                    for jq in range(16):
                        b = banks[jq // 4]
                        KT = K1T if jq % 2 == 0 else K2T
                        P.op("pe", I("matmul", ps[b].a((jq % 4) * 128, [1, 128]), qT.a(jq * 512 + tc0, [1, 128]), KT.a(0, [1, 128]),
                            start=True, stop=True), reads=["qT", "K1T", "K2T"], writes=[pr(b)])
                    for jq in range(16):
                        b = banks[jq // 4]
                        sl = ps[b].a((jq % 4) * 128, [1, 128])
                        P.op("dve", I("max", out=v16.a(jq * 16, [1, 8]), in_=sl),
                             reads=[pr(b)], writes=["v16"])
                        P.op("dve", I("match_replace", out=scr.a(0, [1, 128]), in_to_replace=v16.a(jq * 16, [1, 8]), in_values=sl, imm_value=-1e30),
                            reads=[pr(b), "v16"], writes=["scr"])
                        P.op("dve", I("max", out=v16.a(jq * 16 + 8, [1, 8]), in_=scr.a(0, [1, 128])),
                             reads=["scr"], writes=["v16"])
                    cres = f"c16_{j}"
                    for h in range(8):
                        co = j * 128 + h * 16
                        P.op("dve", I("tensor_tensor", out=candb.a(0, [16, 16], [1, 16]), in0=v16.a(2 * h * 16, [1, 16], [0, 16]),
                            in1=v16.a((2 * h + 1) * 16, [0, 16], [1, 16]), op=ALU.add),
                            reads=["v16"], writes=["candb"])
                        P.op("dve", I("max", out=c16.a(co, [1, 8]), in_=candb.a(0, [1, 256])),
                             reads=["candb"], writes=[cres])
                        P.op("dve", I("match_replace", out=scr.a(0, [1, 256]), in_to_replace=c16.a(co, [1, 8]), in_values=candb.a(0, [1, 256]),
                            imm_value=-1e30), reads=["candb", cres], writes=["scr"])
                        P.op("dve", I("max", out=c16.a(co + 8, [1, 8]), in_=scr.a(0, [1, 256])),
                             reads=["scr"], writes=[cres])
                    P.op("dve", I("tensor_scalar", out=negm.a(0, [1, 8]), in0=c16.a(j * 128, [16, 8]), scalar1=-1.0,
                                                          scalar2=None, op0=ALU.mult), reads=[cres], writes=["negm"])
                    for h in range(8):
                        co = j * 128 + h * 16
                        P.op("act", I("activation", out=j16.a(0, [1, 16]), in_=c16.a(co, [1, 16]), func=AF.Exp, bias=negm.a(h, [1, 1]),
                            accum_out=Zs.a(h, [1, 1])), reads=[cres, "negm"], writes=["j16", "Zs"])
                    P.op("act", I("activation", out=lnZ.a(0, [1, 8]), in_=Zs.a(0, [1, 8]), func=AF.Ln),
                         reads=["Zs"], writes=["lnZ"])
                    P.op("dve", I("tensor_tensor", out=biasv.a(j * 8, [1, 8]), in0=negm.a(0, [1, 8]),
                                                          in1=lnZ.a(0, [1, 8]), op=ALU.subtract),
                         reads=["negm", "lnZ"], writes=[f"biasv{j}"])
                p12_ops = P.capture
                P.capture = None
                bank_pool[0] = list(range(8))
                P.emit_merged(p12_ops, ln2_pending)
                ln2_pending = []
                NY = 4
                YB = [0, 1, 2, 4]

                def emit_Y(idx, cg, j, h):
                    yb = YB[idx % NY]
                    tc0 = j * 128
                    P.op("pe", I("matmul", ps[yb].a(0, [1, 512]), qT.a((2 * h + 1) * 512 + tc0, [1, 128]),
                                 K2T.a(0, [0, 4], [1, 128]), start=True, stop=False),
                         reads=["qT", "K2T"], writes=[pr(yb)])
                    P.op("pe", I("matmul", ps[yb].a(0, [1, 512]), qT.a((2 * h) * 512 + tc0, [1, 128]),
                                 K1T.a(cg * 4, [1, 4], [0, 128]), start=False, stop=True),
                         reads=["qT", "K1T"], writes=[pr(yb)])

                def emit_EF(idx, cg, j, h):
                    yb = YB[idx % NY]
                    es = idx % 3
                    fs = idx % 8
                    P.op("act", I("activation", out=Esl.a(es * 512, [1, 512]), in_=ps[yb].a(0, [1, 512]), func=AF.Exp,
                                  bias=biasv.a(j * 8 + h, [1, 1])), reads=[pr(yb), f"biasv{j}"], writes=[f"E{es}"])
                    P.op("dve", I("scalar_tensor_tensor", out=Fsl.a(fs * 512, [1, 512]), in0=ps[yb].a(0, [1, 512]),
                                  scalar=c16.a(j * 128 + h * 16 + 15, [1, 1]), in1=Esl.a(es * 512, [1, 512]),
                                  op0=ALU.is_ge, op1=ALU.mult), reads=[pr(yb), f"c16_{j}", f"E{es}"], writes=[f"F{fs}"])

                def emit_T(idx, cg, j, h):
                    if h == 3:
                        P.op("pool", I("tensor_tensor", out=PAB.a(0, [1, 1024]), in0=Fsl.a(0, [1, 1024]),
                                       in1=Fsl.a(1024, [1, 1024]), op=ALU.add),
                             reads=["F0", "F1", "F2", "F3"], writes=["yn"])
                    elif h == 7:
                        P.op("pool", I("tensor_tensor", out=PAB.a(1024, [1, 1024]), in0=Fsl.a(2048, [1, 1024]),
                                       in1=Fsl.a(3072, [1, 1024]), op=ALU.add),
                             reads=["F4", "F5", "F6", "F7"], writes=["yn"])

                def tail0(cg, j):
                    P.op("pool", I("tensor_tensor", out=Ssum.a(0, [1, 1024]), in0=PAB.a(0, [1, 1024]),
                                   in1=PAB.a(1024, [1, 1024]), op=ALU.add), reads=["yn"], writes=["tt"])

                def emit_AT_mm(cg, c, kc):
                    s = cg % 2
                    P.op("pe", I("matmul", ps[5].a(0, [1, 512]), UT.a(s * 4096 + kc * 512 + c * 128, [1, 128]),
                                 hT.a(kc * SEG + bc0, [1, 512]), start=(kc == 0), stop=(kc == 7)),
                         reads=[f"UT{s}"] + hres, writes=[pr(5)])

                def emit_AT_copy(c):
                    P.op("act", I("copy", out=rawAT.a(c * 512, [1, 512]), in_=ps[5].a(0, [1, 512])),
                         reads=[pr(5)], writes=["xs"])

                def emit_AT_gelu(cg):
                    ga = cg % 2
                    P.op("act", I("activation", out=gAs.a(ga * 4 * 512, [1, 2048]), in_=rawAT.a(0, [1, 2048]),
                                  func=AF.Gelu), reads=["xs"], writes=[f"gA{ga}_{c}" for c in range(4)])

                octr = [0]

                def tail1(cg, j):
                    tcn = cg * 4 + j
                    gb = 3
                    hs = tcn % 2
                    for c in range(4):
                        for half in range(2):
                            P.op("pe", I("matmul", ps[gb].a(c * 128, [1, 128]), Ssum.a(half * 512 + c * 128, [1, 128]),
                                         ident.a(0, [1, 128]), start=(half == 0), stop=(half == 1)),
                                 reads=["tt", "ident"], writes=[pr(gb)])

                def tail2(cg, j):
                    s = cg % 2
                    ga = cg % 2
                    tcn = cg * 4 + j
                    gb = 3
                    hs = tcn % 2
                    P.op("dve", I("tensor_tensor", out=HT.a(hs * 512, [128, 4], [1, 128]),
                                  in0=ps[gb].a(0, [128, 4], [1, 128]),
                                  in1=gAs.a(ga * 4 * 512 + j * 128, [512, 4], [1, 128]), op=ALU.mult),
                         reads=[pr(gb)] + [f"gA{ga}_{c}" for c in range(4)], writes=[f"HT{hs}"])

                def tail_mm(cg, j, c):
                    s = cg % 2
                    hs = (cg * 4 + j) % 2
                    for dh in range(2):
                        ob = 6 + dh
                        P.op("pe", I("matmul", ps[ob].a(0, [1, 512]), HT.a(hs * 512 + c * 128, [1, 128]),
                                     Areg.a(s * 4096 + c * D + dh * 512, [1, 512]), start=(c == 0), stop=(c == 3)),
                             reads=[f"HT{hs}", f"A{s}"], writes=[pr(ob)])

                def tail3(cg, j, dh):
                    ti = t0 + j
                    ob = 6 + dh
                    P.op("dve", I("tensor_tensor", out=htok.a(ti * D + dh * 512, [1, 512]), in0=ps[ob].a(0, [1, 512]),
                                  in1=htok.a(ti * D + dh * 512, [1, 512]), op=ALU.add),
                         reads=[pr(ob), f"htok{ti}"], writes=[f"htok{ti}"])

                NCG = 32
                load_ut(0)
                load_ut(1)
                load_v(0)
                for c in range(4):
                    for kc in range(8):
                        emit_AT_mm(0, c, kc)
                    emit_AT_copy(c)
                emit_AT_gelu(0)
                seq = [(cg, j, h) for cg in range(NCG) for j in range(4) for h in range(8)]
                for k in range(NY - 1):
                    emit_Y(k, *seq[k])
                pend_copy = None
                pend_gelu = None
                for idx, (cg, j, h) in enumerate(seq):
                    if pend_copy is not None:
                        emit_AT_copy(pend_copy)
                        pend_copy = None
                        if pend_gelu is not None:
                            emit_AT_gelu(pend_gelu)
                            pend_gelu = None
                    if j == 0 and h == 0 and cg + 2 < NCG:
                        load_ut(cg + 2)
                    if idx >= 8:
                        pcg, pj, _ = seq[idx - 8]
                        if h == 0:
                            tail0(pcg, pj)
                        elif h == 4:
                            tail1(pcg, pj)
                        elif h == 5:
                            tail2(pcg, pj)
                        elif h >= 6:
                            tail_mm(pcg, pj, h - 6)
                    if idx >= 16:
                        ppcg, ppj, _ = seq[idx - 16]
                        if h < 2:
                            tail_mm(ppcg, ppj, h + 2)
                        elif h < 4:
                            tail3(ppcg, ppj, h - 2)
                    if j == 1 and h == 4 and cg + 1 < NCG:
                        load_v(cg + 1)
                    emit_EF(idx, cg, j, h)
                    emit_T(idx, cg, j, h)
                    if cg + 1 < NCG:
                        emit_AT_mm(cg + 1, j, h)
                        if h == 7:
                            pend_copy = j
                            if j == 3:
                                pend_gelu = cg + 1
                    if idx + NY - 1 < len(seq):
                        emit_Y(idx + NY - 1, *seq[idx + NY - 1])
                tail_mm(NCG - 1, 2, 2)
                tail_mm(NCG - 1, 2, 3)
                tail3(NCG - 1, 2, 0)
                tail3(NCG - 1, 2, 1)
                tail0(NCG - 1, 3)
                tail1(NCG - 1, 3)
                tail2(NCG - 1, 3)
                for c in range(4):
                    tail_mm(NCG - 1, 3, c)
                tail3(NCG - 1, 3, 0)
                tail3(NCG - 1, 3, 1)
                def ln2_block(t0=t0):
                    for j in range(4):
                        ti = t0 + j
                        layer_norm(htok.a(ti * D, [1, D]), f"htok{ti}", ti, final, seg)
                def ln2_tile(j, which, t0=t0):
                    lnset[0] = which
                    layer_norm(htok.a((t0 + j) * D, [1, D]), f"htok{t0 + j}", t0 + j, final, seg)
                    lnset[0] = 0
                if blk == 0:
                    ln2_pending = grab_with_pool([6, 7], ln2_block)
                else:
                    for jp in range(2):
                        a_ops = grab_with_pool([0, 1, 2, 3], ln2_tile, 2 * jp, 0)
                        b_ops = grab_with_pool([4, 5, 6, 7], ln2_tile, 2 * jp + 1, 1)
                        P.emit_merged(a_ops, b_ops)

        for seg in range(nseg):
            if seg > 0:
                P.barrier()
            load_ln_params(lnin_h, 0, ALPHA)

            def in_ln(ti, which):
                row0 = seg * SEG + ti * 128
                xb, xn = (xs, "xs") if which == 0 else (xsB, "xsB")
                lnset[0] = which
                P.op("sp", I("dma_start", out=xb.a(0, [1, D]), in_=DAP(x_h, row0 * D, [D, 128], [1, D])),
                     writes=[xn] + (["qT"] if which else []), dma=xn)
                layer_norm(xb.a(0, [1, D]), xn, ti, stop == "ln_in", seg)
                lnset[0] = 0

            for tp in range(NTS // 2):
                a_ops = grab_with_pool([0, 1, 2, 3], in_ln, 2 * tp, 0)
                b_ops = grab_with_pool([4, 5, 6, 7], in_ln, 2 * tp + 1, 1)
                P.emit_merged(a_ops, b_ops)
            if stop == "ln_in":
                continue
            for l in range(nlayers):
                P.barrier()
                mixer(seg, l)
                mark(0)
                P.barrier()
                if stop == "mixer" and l == nlayers - 1:
                    for ti in range(NTS):
                        row0 = seg * SEG + ti * 128
                        P.op("sp", I("dma_start", out=DAP(out_h, row0 * D, [D, 128], [1, D]), in_=htok.a(ti * D, [1, D])),
                            reads=[f"htok{ti}"], dma=f"out{ti}")
                    continue
                peer(seg, l, final=(l == nlayers - 1))
        P.barrier()
        P.emit(st)
    return nc


def _consts():
    c = np.zeros((128, 5, 128), np.float32)
    s = np.arange(128)[:, None]
    t = np.arange(128)[None, :]
    c[:, 0] = np.eye(128)
    c[:, 1] = np.where(s <= t, -1.0 / 16.0, 0.0)
    c[:, 2] = np.where(s > t, -1.0 / 16.0, 0.0)
    c[:, 3] = np.where(s <= t, 1.0, 0.0)
    c[:, 4] = 1.0
    return np.ascontiguousarray(c.reshape(128, 640))


def prep_shared(inp):
    f = lambda a: np.ascontiguousarray(np.asarray(a, dtype=np.float32))
    sh = {
        "cst": _consts(),
        "lnin": f(np.stack([inp["ln_in_g"], inp["ln_in_b"]])),
        "w_in": f(np.asarray(inp["w_in"]).reshape(L * D, INC)),
        "w_out": f(np.asarray(inp["w_out"]).reshape(L * D, D)),
        "wq": f(np.asarray(inp["peer_wq"]).reshape(L * D, 2048)),
        "wgu": f(np.asarray(inp["w_gate_up"]).reshape(L * 16, 256)),
        "bgate": f(inp["b_gate"]),
        "glag": f(inp["gla_norm_g"]),
        "sgln": f(np.stack([np.asarray(inp["sgu_ln_g"]), np.asarray(inp["sgu_ln_b"])], axis=1).reshape(L * 2, 512)),
        "sgwT": f(np.asarray(inp["sgu_w"]).transpose(0, 1, 3, 2).reshape(L * 4 * 128, 128)),
        "sgb": f(np.asarray(inp["sgu_b"]).reshape(L, 512)),
        "ln1": f(np.stack([np.asarray(inp["ln1_g"]), np.asarray(inp["ln1_b"])], axis=1).reshape(L * 2, D)),
        "ln2": f(np.stack([np.asarray(inp["ln2_g"]), np.asarray(inp["ln2_b"])], axis=1).reshape(L * 2, D)),
        "k1T": f(np.asarray(inp["peer_k1"]).transpose(0, 2, 1).reshape(L * 128, 128)),
        "k2T": f(np.asarray(inp["peer_k2"]).transpose(0, 2, 1).reshape(L * 128, 128)),
        "uT": f(np.asarray(inp["peer_u"]).transpose(0, 2, 1).reshape(L * D, NE)),
        "vtab": f(np.asarray(inp["peer_v"]).reshape(L * NE, D)),
    }
    return sh


def kernel(**inputs):
    x = np.asarray(inputs["x"], dtype=np.float32)
    nb = x.shape[0]
    sh = prep_shared(inputs)
    nc = build()
    in_maps = []
    for b in range(nb):
        m = dict(sh)
        m["x"] = np.ascontiguousarray(x[b])
        in_maps.append(m)
    res = run_bass_kernel_spmd(nc, in_maps, core_ids=list(range(nb)))
    return np.stack([np.asarray(r["out"]) for r in res.results], axis=0).astype(np.float32)
```

```python
import numpy as np
from contextlib import ExitStack
import concourse.bass as bass
import concourse.mybir as mybir
from concourse.bass_utils import run_bass_kernel_spmd

F32 = mybir.dt.float32
BF16 = mybir.dt.bfloat16
AF = mybir.ActivationFunctionType
ALU = mybir.AluOpType

D = 1024
NTOK = 2048
L = 2
INC = 2576
SEG = 1024
NTS = 8
NE = 16384
ALPHA = (2.0 * L) ** 0.25
EPS = 1e-5
ENGS = ["pe", "act", "dve", "pool", "sp"]


class Prog:
    def __init__(self, nc, same_engine_sync=True):
        self.nc = nc
        self.streams = {e: [] for e in ENGS}
        self.cnt = {e: 0 for e in ENGS}
        self.dma_cnt = {}
        self.lastw = {}
        self.readers = {}
        self.seen = {e: {} for e in ENGS}
        self.same_engine_sync = same_engine_sync

    def _need(self, eng, tok, waits):
        if tok is None:
            return
        sem, val = tok
        if sem == "E_" + eng and (eng == "pe" or not self.same_engine_sync):
            return
        if self.seen[eng].get(sem, 0) >= val:
            return
        if waits.get(sem, 0) < val:
            waits[sem] = val

    enabled = True
    capture = None

    def op(self, eng, fn, reads=(), writes=(), dma=None):
        if not self.enabled:
            return None
        if self.capture is not None:
            self.capture.append((eng, fn, tuple(reads), tuple(writes), dma))
            return None
        waits = {}
        for r in reads:
            self._need(eng, self.lastw.get(r), waits)
        for w in writes:
            self._need(eng, self.lastw.get(w), waits)
            for sem, val in self.readers.get(w, {}).items():
                self._need(eng, (sem, val), waits)
        for sem, val in waits.items():
            self.seen[eng][sem] = val
        if dma is None:
            self.cnt[eng] += 1
            tok = ("E_" + eng, self.cnt[eng])
        else:
            self.dma_cnt[dma] = self.dma_cnt.get(dma, 0) + 16
            tok = ("D_" + dma, self.dma_cnt[dma])
        self.streams[eng].append((fn, waits, tok))
        for r in reads:
            d = self.readers.setdefault(r, {})
            if d.get(tok[0], 0) < tok[1]:
                d[tok[0]] = tok[1]
        for w in writes:
            self.lastw[w] = tok
            self.readers[w] = {}
        return tok

    def grab(self, fn, *args):
        self.capture = []
        fn(*args)
        ops = self.capture
        self.capture = None
        return ops

    def emit_merged(self, a, b):
        na, nb = len(a), len(b)
        ia = ib = 0
        while ia < na or ib < nb:
            if ib >= nb or (ia < na and ia * nb <= ib * na):
                self.op(*a[ia])
                ia += 1
            else:
                self.op(*b[ib])
                ib += 1

    def wait_only(self, eng, toks):
        waits = {}
        for t in toks:
            self._need(eng, t, waits)
        for sem, val in waits.items():
            self.seen[eng][sem] = val
        if waits:
            self.streams[eng].append((None, waits, None))

    def all_tokens(self):
        toks = [("E_" + e, self.cnt[e]) for e in ENGS if self.cnt[e] > 0]
        toks += [("D_" + k, v) for k, v in self.dma_cnt.items()]
        return toks

    def barrier(self):
        toks = self.all_tokens()
        for e in ENGS:
            self.wait_only(e, toks)

    def emit(self, stack):
        nc = self.nc
        names = ["E_" + e for e in ENGS if self.cnt[e] > 0] + ["D_" + k for k in self.dma_cnt]
        sems = {n: stack.enter_context(nc.semaphore(n)) for n in names}
        block = stack.enter_context(nc.Block())
        deco = {"pe": block.tensor, "act": block.scalar, "dve": block.vector,
                "pool": block.gpsimd, "sp": block.sync}
        for e in ENGS:
            stream = self.streams[e]
            if not stream:
                continue

            def body(eng, stream=stream):
                for fn, waits, tok in stream:
                    for sem, val in waits.items():
                        eng.wait_ge(sems[sem], val)
                    if fn is None:
                        continue
                    inst = fn(eng)
                    inst.then_inc(sems[tok[0]], 16 if tok[0].startswith("D_") else 1)

            deco[e](body)


def I(name, *args, **kw):
    return lambda e: getattr(e, name)(*args, **kw)


class Buf:
    def __init__(self, t, F, base=0):
        self.t = t
        self.F = F
        self.base = base

    def a(self, off, *dims, p0=0, np_=128):
        return bass.AP(self.t, p0 * self.F + self.base + off, [[self.F, np_]] + [list(d) for d in dims])

    def sub(self, base):
        return Buf(self.t, self.F, self.base + base)


DBG_MAP = {}


def build(nseg=2, nlayers=L, stop=None, cut=99, dbg=None):
    nc = bass.Bass("TRN2", target_bir_lowering=False)
    DBG_MAP.clear()
    dbg_h = nc.dram_tensor("dbg", [128, 8192], F32, kind="ExternalOutput") if dbg is not None else None
    dt = lambda n, s: nc.dram_tensor(n, s, F32, kind="ExternalInput")
    x_h = dt("x", [NTOK, D])
    cst_h = dt("cst", [128, 640])
    lnin_h = dt("lnin", [2, D])
    win_h = dt("w_in", [L * D, INC])
    wout_h = dt("w_out", [L * D, D])
    wq_h = dt("wq", [L * D, 2048])
    wgu_h = dt("wgu", [L * 16, 256])
    bgate_h = dt("bgate", [L, 256])
    glag_h = dt("glag", [L, 128])
    sgln_h = dt("sgln", [L * 2, 512])
    sgwT_h = dt("sgwT", [L * 4 * 128, 128])
    sgb_h = dt("sgb", [L, 512])
    ln1_h = dt("ln1", [L * 2, D])
    ln2_h = dt("ln2", [L * 2, D])
    k1T_h = dt("k1T", [L * 128, 128])
    k2T_h = dt("k2T", [L * 128, 128])
    uT_h = dt("uT", [L * D, NE])
    vt_h = dt("vtab", [L * NE, D])
    out_h = nc.dram_tensor("out", [NTOK, D], F32, kind="ExternalOutput")
    DAP = lambda h, off, *dims: bass.AP(h, off, [list(d) for d in dims])

    with ExitStack() as st:
        def sb(name, F, dtype):
            return Buf(st.enter_context(nc.sbuf_tensor(name, [128, F], dtype)), F)

        htok = sb("htok", NTS * D, F32)
        hT = sb("hT", 8 * SEG, BF16)
        W = sb("W", 8 * INC, BF16)
        Areg = sb("Areg", 8 * D, BF16)
        ARENA_BYTES = 54 * 1024
        arena_t = st.enter_context(nc.sbuf_tensor("arena", [128, ARENA_BYTES // 2], BF16))
        arena_f = arena_t.bitcast(F32)
        lnG = sb("lnG", D, F32)
        lnB = sb("lnB", D, F32)
        xs = sb("xs", D, F32)
        yn = sb("yn", D, F32)
        tt = sb("tt", D, F32)
        hbf = sb("hbf", D, BF16)
        junk = hbf
        cst = sb("cstf", 640, F32)
        ident = sb("ident", 128, BF16)
        ones = sb("onesb", 128, BF16)
        stt_ = sb("stat", 64, F32)
        Sf = sb("Sf", L * 512, F32)
        Sb = sb("Sb", L * 512, BF16)
        small = sb("small", 128, F32)
        ps = [Buf(st.enter_context(nc.psum_tensor(f"ps{i}", [128, 512], F32)), 512) for i in range(8)]

        class Arena:
            def __init__(self):
                self.off = 0

            def alloc(self, nelem, dtype):
                sz = 4 if dtype == F32 else 2
                self.off = (self.off + 63) // 64 * 64
                o = self.off
                self.off += nelem * sz
                assert self.off <= ARENA_BYTES, ("arena overflow", self.off)
                if dtype == F32:
                    return Buf(arena_f, ARENA_BYTES // 4, o // 4)
                return Buf(arena_t, ARENA_BYTES // 2, o // 2)

        am = Arena()
        qTf = am.alloc(2048, BF16)
        kTf = am.alloc(2048, BF16)
        silur = am.alloc(2048, BF16)
        gsu = am.alloc(2048, BF16)
        aT = am.alloc(512, BF16)
        Lt = am.alloc(256, F32)
        eb = am.alloc(512, F32)
        enb = am.alloc(512, F32)
        erem = am.alloc(256, F32)
        qtil = am.alloc(512, BF16)
        ktil = am.alloc(512, BF16)
        khat = am.alloc(256, BF16)
        v_bf = am.alloc(512, BF16)
        attn_bf = am.alloc(512, BF16)
        sq = am.alloc(512, BF16)
        rstd = am.alloc(512, F32)
        t1 = am.alloc(512, F32)
        catT = am.alloc(1024, BF16)
        gv = am.alloc(512, F32)
        vn = am.alloc(512, BF16)
        sg_g = am.alloc(512, F32)
        sg_b = am.alloc(512, F32)
        WsT = am.alloc(512, BF16)
        wgu = am.alloc(256, BF16)
        bgate = am.alloc(256, BF16)
        gcol = am.alloc(1, F32)
        bs_f = am.alloc(512, F32)
        bs_hi = am.alloc(512, BF16)
        bs_lo = am.alloc(512, BF16)
        bs_t = am.alloc(512, F32)

        ap_ = Arena()
        qT = ap_.alloc(16 * 512, BF16)
        UT = ap_.alloc(2 * 4096, BF16)
        HT = ap_.alloc(2 * 512, BF16)
        gAs = ap_.alloc(8 * 512, BF16)
        K1T = ap_.alloc(128, BF16)
        K2T = ap_.alloc(128, BF16)
        c16 = ap_.alloc(512, F32)
        candb = ap_.alloc(256, F32)
        scr = ap_.alloc(256, F32)
        candb2 = ap_.alloc(256, F32)
        scr2 = ap_.alloc(256, F32)
        v16 = ap_.alloc(256, F32)
        PAB = Buf(yn.t.bitcast(BF16), 2 * D)
        Ssum = Buf(tt.t.bitcast(BF16), 2 * D)
        rawAT = Buf(xs.t.bitcast(BF16), 2 * D)
        Esl = ap_.alloc(3 * 512, BF16)
        Fsl = W.sub(16384)
        biasv = small.sub(0)
        negm = small.sub(32)
        Zs = small.sub(40)
        lnZ = small.sub(48)
        j16 = small.sub(64)
        sst = small.sub(96)

        P = Prog(nc)
        bank_ctr = [0]

        bank_pool = [list(range(8))]

        def bk():
            pool = bank_pool[0]
            b = pool[bank_ctr[0] % len(pool)]
            bank_ctr[0] += 1
            return b

        def grab_with_pool(pool, fn, *args):
            bank_pool[0] = pool
            ops = P.grab(fn, *args)
            bank_pool[0] = list(range(8))
            return ops

        pr = lambda b: f"ps{b}"

        def mark(n):
            P.enabled = n <= cut

        dbg_col = [0]

        def dump(key, name, apfn, ncols, reads, np_=128):
            if dbg is None or tuple(dbg) != tuple(key) or not P.enabled:
                return
            c0 = dbg_col[0]
            dbg_col[0] += ncols
            DBG_MAP[name] = (c0, ncols, np_)
            P.op("pool", I("dma_start", out=bass.AP(dbg_h, c0, [[8192, np_], [1, ncols]]), in_=apfn()),
                 reads=reads, dma="dbg")

        P.op("sp", I("dma_start", out=cst.a(0, [1, 640]), in_=DAP(cst_h, 0, [640, 128], [1, 640])),
             writes=["cst"], dma="cst")
        P.op("pool", I("dma_start", out=ident.a(0, [1, 128]), in_=DAP(cst_h, 0, [640, 128], [1, 128])),
             writes=["ident"], dma="ident")
        P.op("pool", I("dma_start", out=ones.a(0, [1, 128]), in_=DAP(cst_h, 512, [640, 128], [1, 128])),
             writes=["ones"], dma="ones")
        P.op("dve", I("memset", Sf.a(0, [1, L * 512]), 0.0), writes=[f"Sf{l}" for l in range(L)])
        P.op("dve", I("memset", Sb.a(0, [1, L * 512]), 0.0), writes=[f"Sb{l}" for l in range(L)])
        CI, CTRI, CTRI2, CCAUS = 0, 128, 256, 384

        def load_ln_params(h, row0, scale):
            P.op("sp", I("dma_start", out=lnG.a(0, [1, D]), in_=DAP(h, row0 * D, [0, 128], [1, D])),
                 writes=["lnG"], dma="lnG")
            P.op("sp", I("dma_start", out=lnB.a(0, [1, D]), in_=DAP(h, (row0 + 1) * D, [0, 128], [1, D])),
                 writes=["lnB"], dma="lnB")
            if scale != 1.0:
                P.op("pool", I("tensor_scalar", out=lnG.a(0, [1, D]), in0=lnG.a(0, [1, D]), scalar1=scale,
                                                       scalar2=None, op0=ALU.mult), reads=["lnG"], writes=["lnG"])
                P.op("pool", I("tensor_scalar", out=lnB.a(0, [1, D]), in0=lnB.a(0, [1, D]), scalar1=scale,
                                                       scalar2=None, op0=ALU.mult), reads=["lnB"], writes=["lnB"])

        stat_ctr = [0]
        lnset = [0]
        ynB = Buf(arena_f, ARENA_BYTES // 4, 0)
        ttB = Buf(arena_f, ARENA_BYTES // 4, 1024)
        hbfB = Buf(arena_t, ARENA_BYTES // 2, 4096)
        xsB = Buf(arena_f, ARENA_BYTES // 4, 2560)

        def stats_chain(sbuf, s0, n, r="statchain"):
            c = lambda k: sbuf.a(s0 + k, [1, 1])
            P.op("dve", I("tensor_scalar", out=c(2), in0=c(0), scalar1=1.0 / n, scalar2=None, op0=ALU.mult),
                 reads=[r], writes=[r])
            P.op("dve", I("tensor_tensor", out=c(3), in0=c(2), in1=c(2), op=ALU.mult), reads=[r], writes=[r])
            P.op("dve", I("scalar_tensor_tensor", out=c(4), in0=c(1), scalar=1.0 / n, in1=c(3),
                                                         op0=ALU.mult, op1=ALU.subtract), reads=[r], writes=[r])
            P.op("act", I("activation", out=c(5), in_=c(4), func=AF.Ln, bias=EPS), reads=[r], writes=[r])
            P.op("act", I("activation", out=c(6), in_=c(5), func=AF.Exp, scale=-0.5), reads=[r], writes=[r])
            P.op("dve", I("scalar_tensor_tensor", out=c(7), in0=c(2), scalar=-1.0, in1=c(6),
                                                         op0=ALU.mult, op1=ALU.mult), reads=[r], writes=[r])

        def layer_norm(src, src_res, ti, final, seg):
            if lnset[0] == 0:
                yn_, tt_, hbf_, nyn, ntt, nhbf, rchain, sbase, acq = yn, tt, hbf, "yn", "tt", "hbf", "statchain", 0, []
            else:
                yn_, tt_, hbf_, nyn, ntt, nhbf, rchain, sbase, acq = ynB, ttB, hbfB, "ynB", "ttB", "hbfB", "statchainB", 32, ["qT"]
            junk_ = hbf_
            s0 = sbase + (stat_ctr[0] % 4) * 8
            stat_ctr[0] += 1
            hres = f"htok{ti}"
            P.op("act", I("activation", out=junk_.a(0, [1, D]), in_=src, func=AF.Identity,
                                               accum_out=stt_.a(s0, [1, 1])),
                 reads=[src_res], writes=[nhbf, rchain] + acq)
            P.op("act", I("activation", out=junk_.a(0, [1, D]), in_=src, func=AF.Square,
                                               accum_out=stt_.a(s0 + 1, [1, 1])),
                 reads=[src_res], writes=[nhbf, rchain])
            stats_chain(stt_, s0, float(D), rchain)
            P.op("act", I("activation", out=yn_.a(0, [1, D]), in_=src, func=AF.Identity,
                                               scale=stt_.a(s0 + 6, [1, 1]), bias=stt_.a(s0 + 7, [1, 1])),
                 reads=[src_res, rchain], writes=[nyn])
            P.op("dve", I("tensor_tensor", out=tt_.a(0, [1, D]), in0=yn_.a(0, [1, D]), in1=lnG.a(0, [1, D]),
                                                   op=ALU.mult), reads=[nyn, "lnG"], writes=[ntt])
            P.op("dve", I("tensor_tensor", out=htok.a(ti * D, [1, D]), in0=tt_.a(0, [1, D]),
                                                  in1=lnB.a(0, [1, D]), op=ALU.add),
                 reads=[ntt, "lnB", src_res], writes=[hres])
            if final:
                row0 = seg * SEG + ti * 128
                P.op("sp", I("dma_start", out=DAP(out_h, row0 * D, [D, 128], [1, D]), in_=htok.a(ti * D, [1, D])),
                     reads=[hres], dma=f"out{ti}")
                return
            P.op("act", I("activation", out=hbf_.a(0, [1, D]), in_=htok.a(ti * D, [1, D]), func=AF.Copy,
                                               scale=1.0 / ALPHA), reads=[hres], writes=[nhbf])
            for half in range(2):
                b = bk()
                for k4 in range(4):
                    kc = half * 4 + k4
                    P.op("pe", I("matmul", ps[b].a(k4 * 128, [1, 128]), hbf_.a(kc * 128, [1, 128]), ident.a(0, [1, 128]),
                        start=True, stop=True), reads=[nhbf, "ident"], writes=[pr(b)])
                eng = "act" if half == 0 else "dve"
                dst = hT.a((half * 4) * SEG + ti * 128, [SEG, 4], [1, 128])
                srcp = ps[b].a(0, [128, 4], [1, 128])
                if eng == "act":
                    P.op("act", I("copy", out=dst, in_=srcp), reads=[pr(b)], writes=[f"hT{ti}"])
                else:
                    P.op("dve", I("tensor_copy", out=dst, in_=srcp), reads=[pr(b)],
                         writes=[f"hT{ti}"])

        def mixer(seg, l):
            for c0, cw, rn in ((0, 1280, "Wa"), (1280, 1296, "Wb")):
                P.op("pool", I("dma_start", out=W.a(c0, [INC, 8], [1, cw]),
                    in_=DAP(win_h, l * D * INC + c0, [INC, 128], [128 * INC, 8], [1, cw])),
                    writes=[rn], dma=rn)

            def wres(c0, n):
                r = []
                if c0 < 1280:
                    r.append("Wa")
                if c0 + n > 1280:
                    r.append("Wb")
                return r
            P.op("pool", I("dma_start", out=Areg.a(0, [D, 8], [1, D]),
                                               in_=DAP(wout_h, l * D * D, [D, 128], [128 * D, 8], [1, D])),
                 writes=["A0", "A1"], dma="A")
            P.op("dve", I("memset", wgu.a(0, [1, 256], np_=32), 0.0), writes=["wgu"])
            P.op("dve", I("memset", bgate.a(0, [1, 256], np_=32), 0.0), writes=["bgate"])
            P.op("dve", I("memset", aT.a(0, [1, 512], np_=32), 0.0), writes=["aT"])
            P.op("dve", I("memset", bs_hi.a(0, [1, 512], np_=32), 0.0), writes=["bs_hi"])
            P.op("dve", I("memset", bs_lo.a(0, [1, 512], np_=32), 0.0), writes=["bs_lo"])
            P.op("pool", I("dma_start", out=wgu.a(0, [1, 256], np_=16), in_=DAP(wgu_h, l * 16 * 256, [256, 16], [1, 256])),
                 writes=["wgu"], dma="wgu")
            P.op("pool", I("dma_start", out=bgate.a(0, [1, 256], np_=1), in_=DAP(bgate_h, l * 256, [256, 1], [1, 256])),
                 writes=["bgate"], dma="bgate")
            P.op("pool", I("dma_start", out=WsT.a(0, [128, 4], [1, 128]),
                                               in_=DAP(sgwT_h, l * 4 * 128 * 128, [128, 128], [128 * 128, 4], [1, 128])),
                 writes=["WsT"], dma="WsT")
            P.op("sp", I("dma_start", out=gcol.a(0, [1, 1]), in_=DAP(glag_h, l * 128, [1, 128], [1, 1])),
                 writes=["gcol"], dma="gcol")
            P.op("sp", I("dma_start", out=sg_g.a(0, [1, 512]), in_=DAP(sgln_h, (2 * l) * 512, [0, 128], [1, 512])),
                 writes=["sg_g"], dma="sg_g")
            P.op("sp", I("dma_start", out=sg_b.a(0, [1, 512]), in_=DAP(sgln_h, (2 * l + 1) * 512, [0, 128], [1, 512])),
                 writes=["sg_b"], dma="sg_b")
            P.op("sp", I("dma_start", out=bs_f.a(0, [1, 512], np_=1), in_=DAP(sgb_h, l * 512, [512, 1], [1, 512])),
                 writes=["bs_f"], dma="bs_f")
            load_ln_params(ln1_h, 2 * l, ALPHA)
            P.op("dve", I("memset", WsT.a(0, [128, 4], [1, 64], p0=64, np_=64), 0.0), reads=["WsT"], writes=["WsT"])
            P.op("dve", I("tensor_copy", out=bs_hi.a(0, [1, 512], np_=1), in_=bs_f.a(0, [1, 512], np_=1)),
                 reads=["bs_f"], writes=["bs_hi"])
            P.op("dve", I("tensor_tensor", out=bs_t.a(0, [1, 512], np_=1), in0=bs_f.a(0, [1, 512], np_=1),
                                                  in1=bs_hi.a(0, [1, 512], np_=1), op=ALU.subtract),
                 reads=["bs_f", "bs_hi"], writes=["bs_t"])
            P.op("dve", I("tensor_copy", out=bs_lo.a(0, [1, 512], np_=1), in_=bs_t.a(0, [1, 512], np_=1)),
                 reads=["bs_t"], writes=["bs_lo"])

            def proj_fm(c0, m, q):
                b = bk()
                hres = [f"hT{4 * q + j}" for j in range(4)]
                for kc in range(8):
                    P.op("pe", I("matmul", ps[b].a(0, [1, 512], np_=m), W.a(kc * INC + c0, [1, m]), hT.a(kc * SEG + q * 512, [1, 512]),
                        start=(kc == 0), stop=(kc == 7)), reads=wres(c0, m) + hres, writes=[pr(b)])
                return b

            def proj_tm(c0, n, ti):
                b = bk()
                for kc in range(8):
                    P.op("pe", I("matmul", ps[b].a(0, [1, n]), hT.a(kc * SEG + ti * 128, [1, 128]), W.a(kc * INC + c0, [1, n]),
                        start=(kc == 0), stop=(kc == 7)), reads=wres(c0, n) + [f"hT{ti}"], writes=[pr(b)])
                return b

            for q in range(2):
                mark(1)
                for i in range(4):
                    b = proj_fm(i * 64, 64, q)
                    P.op("act", I("copy", out=qTf.a(i * 512, [1, 512], np_=64),
                                                           in_=ps[b].a(0, [1, 512], np_=64)),
                         reads=[pr(b)], writes=["qTf"])
                for i in range(4):
                    b = proj_fm(256 + i * 64, 64, q)
                    P.op("dve", I("tensor_copy", out=kTf.a(i * 512, [1, 512], np_=64),
                                                                  in_=ps[b].a(0, [1, 512], np_=64)),
                         reads=[pr(b)], writes=["kTf"])
                for i in range(4):
                    b = proj_fm(1024 + i * 128, 128, q)
                    P.op("act", I("activation", out=silur.a(i * 512, [1, 512]), in_=ps[b].a(0, [1, 512]),
                                                                 func=AF.Silu), reads=[pr(b)], writes=["silur"])
                for i in range(4):
                    b = proj_fm(1552 + i * 128, 128, q)
                    P.op("act", I("activation", out=gsu.a(i * 512, [1, 512]), in_=ps[b].a(0, [1, 512]),
                                                                 func=AF.Gelu), reads=[pr(b)], writes=["gsu"])
                b = proj_fm(1536, 16, q)
                P.op("act", I("copy", out=aT.a(0, [1, 512], np_=16), in_=ps[b].a(0, [1, 512], np_=16)),
                     reads=[pr(b)], writes=["aT"])

                def tile_body(j, part):
                    ti = 4 * q + j
                    tc0 = j * 128
                    key = (seg, l, ti)
                    if part == 0:
                        mark(2)
                        b = bk()
                        P.op("pe", I("matmul", ps[b].a(0, [1, 256]), aT.a(tc0, [1, 128], np_=32),
                                                           wgu.a(0, [1, 256], np_=32), start=True, stop=False),
                             reads=["aT", "wgu"], writes=[pr(b)])
                        P.op("pe", I("matmul", ps[b].a(0, [1, 256]), ones.a(0, [1, 128], np_=32),
                                                           bgate.a(0, [1, 256], np_=32), start=False, stop=True),
                             reads=["ones", "bgate"], writes=[pr(b)])
                        P.op("act", I("activation", out=Lt.a(0, [1, 256]), in_=ps[b].a(0, [1, 256]), func=AF.Exp,
                                                                scale=-1.0), reads=[pr(b)], writes=["Lt"])
                        dump(key, "aT", lambda: aT.a(0, [1, 512], np_=16), 512, ["aT"], np_=16)
                        dump(key, "wgu", lambda: wgu.a(0, [1, 256], np_=16), 256, ["wgu"], np_=16)
                        dump(key, "bgate", lambda: bgate.a(0, [1, 256], np_=1), 256, ["bgate"], np_=1)
                        dump(key, "expnx", lambda: Lt.a(0, [1, 256]), 256, ["Lt"])
                        P.op("act", I("activation", out=Lt.a(0, [1, 256]), in_=Lt.a(0, [1, 256]), func=AF.Ln, bias=1.0),
                             reads=["Lt"], writes=["Lt"])
                        mark(3)
                        dump(key, "Lt", lambda: Lt.a(0, [1, 256]), 256, ["Lt"])
                        b2 = bk()
                        for h in range(4):
                            P.op("pe", I("matmul", ps[b2].a(h * 128, [1, 128], np_=64), Lt.a(h * 64, [1, 64]),
                                                                      cst.a(CTRI, [1, 128]), start=True, stop=True),
                                 reads=["Lt", "cst"], writes=[pr(b2)])
                        b3 = bk()
                        P.op("pe", I("matmul", ps[b3].a(0, [1, 256]), cst.a(CTRI2, [1, 128]), Lt.a(0, [1, 256]),
                                                             start=True, stop=True), reads=["Lt", "cst"], writes=[pr(b3)])
                        P.op("act", I("activation", out=eb.a(0, [1, 512], np_=64), in_=ps[b2].a(0, [1, 512], np_=64),
                                                                  func=AF.Exp), reads=[pr(b2)], writes=["eb"])
                        P.op("act", I("activation", out=enb.a(0, [1, 512], np_=64), in_=ps[b2].a(0, [1, 512], np_=64),
                                                                  func=AF.Exp, scale=-1.0), reads=[pr(b2)], writes=["enb"])
                        P.op("act", I("activation", out=erem.a(0, [1, 256]), in_=ps[b3].a(0, [1, 256]), func=AF.Exp),
                             reads=[pr(b3)], writes=["erem"])
                        P.op("dve", I("scalar_tensor_tensor", out=qtil.a(0, [128, 4], [1, 128], np_=64), in0=qTf.a(tc0, [512, 4], [1, 128], np_=64), scalar=0.125,
                            in1=eb.a(0, [128, 4], [1, 128], np_=64), op0=ALU.mult, op1=ALU.mult),
                            reads=["qTf", "eb"], writes=["qtil"])
                        P.op("dve", I("tensor_tensor", out=ktil.a(0, [128, 4], [1, 128], np_=64), in0=kTf.a(tc0, [512, 4], [1, 128], np_=64),
                            in1=enb.a(0, [128, 4], [1, 128], np_=64), op=ALU.mult), reads=["kTf", "enb"], writes=["ktil"])
                        dump(key, "eb", lambda: eb.a(0, [1, 512], np_=64), 512, ["eb"], np_=64)
                        dump(key, "erem", lambda: erem.a(0, [1, 256]), 256, ["erem"])
                        dump(key, "qtil", lambda: qtil.a(0, [1, 512], np_=64), 512, ["qtil"], np_=64)
                        dump(key, "ktil", lambda: ktil.a(0, [1, 512], np_=64), 512, ["ktil"], np_=64)
                        mark(4)
                        b4 = proj_tm(256, 256, ti)
                        mark(4.02)
                        P.op("dve", I("tensor_tensor", out=khat.a(0, [1, 256]), in0=ps[b4].a(0, [1, 256]),
                                                                     in1=erem.a(0, [1, 256]), op=ALU.mult),
                             reads=[pr(b4), "erem"], writes=["khat"])
                        mark(4.03)
                        b5 = proj_tm(512, 512, ti)
                        mark(4.04)
                        P.op("act", I("copy", out=v_bf.a(0, [1, 512]), in_=ps[b5].a(0, [1, 512])),
                             reads=[pr(b5)], writes=["v_bf"])
                    if part == 1:
                        mark(4.2)
                        b6 = bk()
                        for h in range(4):
                            c, pb = h // 2, (h % 2) * 64
                            P.op("pe", I("matmul", ps[b6].a(h * 128, [1, 128]), ktil.a(h * 128, [1, 128], np_=64),
                                qtil.a(h * 128, [1, 128], np_=64), start=True, stop=True),
                                reads=["ktil", "qtil"], writes=[pr(b6)])
                        mark(4.4)
                        P.op("dve", I("tensor_tensor", out=attn_bf.a(0, [128, 4], [1, 128]), in0=ps[b6].a(0, [128, 4], [1, 128]),
                            in1=cst.a(CCAUS, [0, 4], [1, 128]), op=ALU.mult), reads=[pr(b6), "cst"], writes=["attn_bf"])
                        dump(key, "khat", lambda: khat.a(0, [1, 256]), 256, ["khat"])
                        dump(key, "v_bf", lambda: v_bf.a(0, [1, 512]), 512, ["v_bf"])
                        dump(key, "attn", lambda: attn_bf.a(0, [1, 512]), 512, ["attn_bf"])
                        mark(4.6)
                        b7 = bk()
                        for h in range(4):
                            c, pb = h // 2, (h % 2) * 64
                            P.op("pe", I("matmul", ps[b7].a(h * 128, [1, 128]), v_bf.a(h * 128, [1, 128]), attn_bf.a(h * 128, [1, 128]),
                                start=True, stop=False), reads=["v_bf", "attn_bf"], writes=[pr(b7)])
                            P.op("pe", I("matmul", ps[b7].a(h * 128, [1, 128]), Sb.a(l * 512 + h * 128, [1, 128], np_=64),
                                qtil.a(h * 128, [1, 128], np_=64), start=False, stop=True),
                                reads=[f"Sb{l}", "qtil"], writes=[pr(b7)])
                        mark(5)
                        b8 = bk()
                        for h in range(4):
                            P.op("pe", I("matmul", ps[b8].a(h * 128, [1, 128], np_=64), khat.a(h * 64, [1, 64]), v_bf.a(h * 128, [1, 128]),
                                start=True, stop=True), reads=["khat", "v_bf"], writes=[pr(b8)])
                        for h in range(4):
                            so = l * 512 + h * 128
                            P.op("dve", I("scalar_tensor_tensor", out=Sf.a(so, [1, 128], np_=64), in0=Sf.a(so, [1, 128], np_=64),
                                scalar=eb.a(h * 128 + 127, [1, 1], np_=64),
                                in1=ps[b8].a(h * 128, [1, 128], np_=64),
                                op0=ALU.mult, op1=ALU.add), reads=[f"Sf{l}", "eb", pr(b8)], writes=[f"Sf{l}"])
                        P.op("act", I("copy", out=Sb.a(l * 512, [1, 512], np_=64), in_=Sf.a(l * 512, [1, 512], np_=64)),
                             reads=[f"Sf{l}"], writes=[f"Sb{l}"])
                        mark(6)
                        P.op("act", I("activation", out=sq.a(0, [1, 512]), in_=ps[b7].a(0, [1, 512]), func=AF.Square),
                             reads=[pr(b7)], writes=["sq"])
                        b9 = bk()
                        P.op("pe", I("matmul", ps[b9].a(0, [1, 512]), ones.a(0, [1, 128]), sq.a(0, [1, 512]),
                                                             start=True, stop=True), reads=["ones", "sq"], writes=[pr(b9)])
                        P.op("act", I("activation", out=rstd.a(0, [1, 512]), in_=ps[b9].a(0, [1, 512]), func=AF.Ln,
                                                              scale=1.0 / 128.0, bias=EPS), reads=[pr(b9)], writes=["rstd"])
                        P.op("act", I("activation", out=rstd.a(0, [1, 512]), in_=rstd.a(0, [1, 512]), func=AF.Exp, scale=-0.5),
                             reads=["rstd"], writes=["rstd"])
                        P.op("dve", I("scalar_tensor_tensor", out=t1.a(0, [1, 512]), in0=ps[b7].a(0, [1, 512]), scalar=gcol.a(0, [1, 1]), in1=rstd.a(0, [1, 512]),
                            op0=ALU.mult, op1=ALU.mult), reads=[pr(b7), "gcol", "rstd"], writes=["t1"])
                        P.op("dve", I("tensor_tensor", out=catT.a(0, [128, 4], [1, 128]), in0=t1.a(0, [128, 4], [1, 128]),
                            in1=silur.a(tc0, [512, 4], [1, 128]), op=ALU.mult), reads=["t1", "silur"], writes=["catT_o"])
                        dump(key, "t1", lambda: t1.a(0, [1, 512]), 512, ["t1"])
                        dump(key, "Sf", lambda: Sf.a(l * 512, [1, 512], np_=64), 512, [f"Sf{l}"], np_=64)
                    if part == 2:
                        mark(7)
                        b10 = proj_tm(2064, 512, ti)
                        P.op("act", I("activation", out=gv.a(0, [1, 512]), in_=ps[b10].a(0, [1, 512]), func=AF.Gelu,
                                                                    accum_out=sst.a(0, [1, 1])),
                             reads=[pr(b10)], writes=["gv", "statchain"])
                        P.op("act", I("activation", out=junk.a(0, [1, 512]), in_=gv.a(0, [1, 512]), func=AF.Square,
                                                           accum_out=sst.a(1, [1, 1])),
                             reads=["gv"], writes=["hbf", "statchain"])
                        stats_chain(sst, 0, 512.0)
                        P.op("act", I("activation", out=gv.a(0, [1, 512]), in_=gv.a(0, [1, 512]), func=AF.Identity,
                                                           scale=sst.a(6, [1, 1]), bias=sst.a(7, [1, 1])),
                             reads=["gv", "statchain"], writes=["gv"])
                        P.op("dve", I("tensor_tensor", out=gv.a(0, [1, 512]), in0=gv.a(0, [1, 512]), in1=sg_g.a(0, [1, 512]),
                                                               op=ALU.mult), reads=["gv", "sg_g"], writes=["gv"])
                        P.op("dve", I("tensor_tensor", out=vn.a(0, [1, 512]), in0=gv.a(0, [1, 512]), in1=sg_b.a(0, [1, 512]),
                                                              op=ALU.add), reads=["gv", "sg_b"], writes=["vn"])
                        b11 = bk()
                        for g in range(4):
                            P.op("pe", I("matmul", ps[b11].a(g * 128, [1, 128]), vn.a(g * 128, [1, 128]), WsT.a(g * 128, [1, 128]),
                                start=True, stop=False), reads=["vn", "WsT"], writes=[pr(b11)])
                            P.op("pe", I("matmul", ps[b11].a(g * 128, [1, 128]), ones.a(0, [1, 128], np_=32), bs_hi.a(g * 128, [1, 128], np_=32),
                                start=False, stop=False), reads=["ones", "bs_hi"], writes=[pr(b11)])
                            P.op("pe", I("matmul", ps[b11].a(g * 128, [1, 128]), ones.a(0, [1, 128], np_=32), bs_lo.a(g * 128, [1, 128], np_=32),
                                start=False, stop=True), reads=["ones", "bs_lo"], writes=[pr(b11)])
                        P.op("dve", I("tensor_tensor", out=catT.a(512, [128, 4], [1, 128]), in0=ps[b11].a(0, [128, 4], [1, 128]),
                            in1=gsu.a(tc0, [512, 4], [1, 128]), op=ALU.mult), reads=[pr(b11), "gsu"], writes=["catT_g"])
                        dump(key, "vn", lambda: vn.a(0, [1, 512]), 512, ["vn"])
                        dump(key, "catT", lambda: catT.a(0, [1, 1024]), 1024, ["catT_o", "catT_g"])
                        mark(8)
                        for dh in range(2):
                            b12 = bk()
                            for cc in range(8):
                                P.op("pe", I("matmul", ps[b12].a(0, [1, 512]), catT.a(cc * 128, [1, 128]), Areg.a(cc * D + dh * 512, [1, 512]),
                                    start=(cc == 0), stop=(cc == 7)),
                                    reads=["catT_o", "catT_g", "A0", "A1"], writes=[pr(b12)])
                            P.op("dve", I("tensor_tensor", out=htok.a(ti * D + dh * 512, [1, 512]), in0=ps[b12].a(0, [1, 512]),
                                in1=htok.a(ti * D + dh * 512, [1, 512]), op=ALU.add),
                                reads=[pr(b12), f"htok{ti}"], writes=[f"htok{ti}"])
                        dump(key, "hpre", lambda: htok.a(ti * D, [1, D]), 1024, [f"htok{ti}"])
                    if part == 3:
                        layer_norm(htok.a(ti * D, [1, D]), f"htok{ti}", ti, False, seg)

                G = lambda jj, part: grab_with_pool([0, 1, 2, 3, 4] if part < 2 else [5, 6, 7], tile_body, jj, part)
                P.emit_merged(G(0, 0), [])
                P.emit_merged(G(0, 1), [])
                for jj in range(1, 4):
                    P.emit_merged(G(jj, 0), G(jj - 1, 2))
                    P.emit_merged(G(jj, 1), G(jj - 1, 3))
                P.emit_merged(G(3, 2), [])
                P.emit_merged(G(3, 3), [])

        def peer(seg, l, final):
            for g4 in range(4):
                P.op("pool", I("dma_start", out=W.a(g4 * 512, [2048, 8], [1, 512]),
                               in_=DAP(wq_h, l * D * 2048 + g4 * 512, [2048, 128], [128 * 2048, 8], [1, 512])),
                     writes=[f"Wq{g4}"], dma=f"Wq{g4}")
            P.op("pool", I("dma_start", out=K1T.a(0, [1, 128]), in_=DAP(k1T_h, l * 128 * 128, [128, 128], [1, 128])),
                 writes=["K1T"], dma="K1T")
            P.op("pool", I("dma_start", out=K2T.a(0, [1, 128]), in_=DAP(k2T_h, l * 128 * 128, [128, 128], [1, 128])),
                 writes=["K2T"], dma="K2T")
            load_ln_params(ln2_h, 2 * l, 1.0 if final else ALPHA)
            cgi = [0]

            def load_ut(cg):
                s = cg % 2
                P.op("pool", I("dma_start",
                    out=UT.a(s * 4096, [512, 8], [1, 512]),
                    in_=DAP(uT_h, l * D * NE + cg * 512, [NE, 128], [128 * NE, 8], [1, 512])),
                    writes=[f"UT{s}"], dma=f"UT{s}")

            def load_v(cg):
                s = cg % 2
                P.op("pool", I("dma_start",
                    out=Areg.a(s * 4096, [D, 4], [1, D]),
                    in_=DAP(vt_h, (l * NE + cg * 512) * D, [D, 128], [128 * D, 4], [1, D])),
                    writes=[f"A{s}"], dma=f"V{s}")

            ln2_pending = []
            for blk in range(2):
                t0 = blk * 4
                bc0 = blk * 512
                hres = [f"hT{t0 + j}" for j in range(4)]
                bank_pool[0] = [0, 1, 2, 3, 4, 5]
                P.capture = []
                for jq in range(16):
                    b = bk()
                    for kc in range(8):
                        P.op("pe", I("matmul", ps[b].a(0, [1, 512]), W.a(kc * 2048 + jq * 128, [1, 128]), hT.a(kc * SEG + bc0, [1, 512]),
                            start=(kc == 0), stop=(kc == 7)), reads=[f"Wq{jq // 4}"] + hres, writes=[pr(b)])
                    if jq % 2 == 0:
                        P.op("act", I("copy", out=qT.a(jq * 512, [1, 512]), in_=ps[b].a(0, [1, 512])),
                             reads=[pr(b)], writes=["qT"])
                    else:
                        P.op("dve", I("tensor_copy", out=qT.a(jq * 512, [1, 512]), in_=ps[b].a(0, [1, 512])),
                             reads=[pr(b)], writes=["qT"])
                for j in range(4):
                    tc0 = j * 128
                    banks = [bk() for _ in range(4)]
                    for jq in range(16):
                        b = banks[jq // 4]
                        KT = K1T if jq % 2 == 0 else K2T
                        P.op("pe", I("matmul", ps[b].a((jq % 4) * 128, [1, 128]), qT.a(jq * 512 + tc0, [1, 128]), KT.a(0, [1, 128]),
                            start=True, stop=True), reads=["qT", "K1T", "K2T"], writes=[pr(b)])
                    for jq0 in range(0, 16, 2):
                        pr2 = [(jq0 + k, ps[banks[(jq0 + k) // 4]].a(((jq0 + k) % 4) * 128, [1, 128]), k) for k in range(2)]
                        for jq, sl, k in pr2:
                            P.op("dve", I("max", out=v16.a(jq * 16, [1, 8]), in_=sl),
                                 reads=[pr(banks[jq // 4])], writes=[f"v16a_{jq}"])
                        for jq, sl, k in pr2:
                            P.op("dve", I("match_replace", out=scr.a(k * 128, [1, 128]), in_to_replace=v16.a(jq * 16, [1, 8]),
                                          in_values=sl, imm_value=-1e30),
                                 reads=[pr(banks[jq // 4]), f"v16a_{jq}"], writes=[f"scr{k}"])
                        for jq, sl, k in pr2:
                            P.op("dve", I("max", out=v16.a(jq * 16 + 8, [1, 8]), in_=scr.a(k * 128, [1, 128])),
                                 reads=[f"scr{k}"], writes=[f"v16b_{jq}"])
                    cres = f"c16_{j}"
                    for h0 in range(0, 8, 2):
                        hp = [(h0 + k, j * 128 + (h0 + k) * 16, candb if k == 0 else candb2, scr if k == 0 else scr2, k) for k in range(2)]
                        for h, co, cb, sc, k in hp:
                            P.op("dve", I("tensor_tensor", out=cb.a(0, [16, 16], [1, 16]), in0=v16.a(2 * h * 16, [1, 16], [0, 16]),
                                          in1=v16.a((2 * h + 1) * 16, [0, 16], [1, 16]), op=ALU.add),
                                 reads=[f"v16a_{2 * h}", f"v16b_{2 * h}", f"v16a_{2 * h + 1}", f"v16b_{2 * h + 1}"],
                                 writes=[f"candb{k}"])
                        for h, co, cb, sc, k in hp:
                            P.op("dve", I("max", out=c16.a(co, [1, 8]), in_=cb.a(0, [1, 256])),
                                 reads=[f"candb{k}"], writes=[f"{cres}_{h}"])
                        for h, co, cb, sc, k in hp:
                            P.op("dve", I("match_replace", out=sc.a(0, [1, 256]), in_to_replace=c16.a(co, [1, 8]),
                                          in_values=cb.a(0, [1, 256]), imm_value=-1e30),
                                 reads=[f"candb{k}", f"{cres}_{h}"], writes=(["scr0", "scr1"] if k == 0 else []) + [f"scrc{k}"])
                        for h, co, cb, sc, k in hp:
                            P.op("dve", I("max", out=c16.a(co + 8, [1, 8]), in_=sc.a(0, [1, 256])),
                                 reads=[f"scrc{k}"], writes=[f"{cres}_{h}"])
                    P.op("dve", I("tensor_scalar", out=negm.a(0, [1, 8]), in0=c16.a(j * 128, [16, 8]), scalar1=-1.0,
                                                          scalar2=None, op0=ALU.mult), reads=[f"{cres}_{h}" for h in range(8)], writes=["negm"])
                    for h in range(8):
                        co = j * 128 + h * 16
                        P.op("act", I("activation", out=j16.a(0, [1, 16]), in_=c16.a(co, [1, 16]), func=AF.Exp, bias=negm.a(h, [1, 1]),
                            accum_out=Zs.a(h, [1, 1])), reads=[f"{cres}_{h}", "negm"], writes=["j16", "Zs"])
                    P.op("act", I("activation", out=lnZ.a(0, [1, 8]), in_=Zs.a(0, [1, 8]), func=AF.Ln),
                         reads=["Zs"], writes=["lnZ"])
                    P.op("dve", I("tensor_tensor", out=biasv.a(j * 8, [1, 8]), in0=negm.a(0, [1, 8]),
                                                          in1=lnZ.a(0, [1, 8]), op=ALU.subtract),
                         reads=["negm", "lnZ"], writes=[f"biasv{j}"])
                p12_ops = P.capture
                P.capture = None
                bank_pool[0] = list(range(8))
                P.emit_merged(p12_ops, ln2_pending)
                ln2_pending = []
                NY = 4
                YB = [0, 1, 2, 4]

                def emit_Y(idx, cg, j, h):
                    yb = YB[idx % NY]
                    tc0 = j * 128
                    P.op("pe", I("matmul", ps[yb].a(0, [1, 512]), qT.a((2 * h + 1) * 512 + tc0, [1, 128]),
                                 K2T.a(0, [0, 4], [1, 128]), start=True, stop=False),
                         reads=["qT", "K2T"], writes=[pr(yb)])
                    P.op("pe", I("matmul", ps[yb].a(0, [1, 512]), qT.a((2 * h) * 512 + tc0, [1, 128]),
                                 K1T.a(cg * 4, [1, 4], [0, 128]), start=False, stop=True),
                         reads=["qT", "K1T"], writes=[pr(yb)])

                def emit_EF(idx, cg, j, h):
                    yb = YB[idx % NY]
                    es = idx % 3
                    fs = idx % 8
                    P.op("act", I("activation", out=Esl.a(es * 512, [1, 512]), in_=ps[yb].a(0, [1, 512]), func=AF.Exp,
                                  bias=biasv.a(j * 8 + h, [1, 1])), reads=[pr(yb), f"biasv{j}"], writes=[f"E{es}"])
                    P.op("dve", I("scalar_tensor_tensor", out=Fsl.a(fs * 512, [1, 512]), in0=ps[yb].a(0, [1, 512]),
                                  scalar=c16.a(j * 128 + h * 16 + 15, [1, 1]), in1=Esl.a(es * 512, [1, 512]),
                                  op0=ALU.is_ge, op1=ALU.mult), reads=[pr(yb), f"c16_{j}_{h}", f"E{es}"], writes=[f"F{fs}"])

                def emit_T(idx, cg, j, h):
                    if h == 3:
                        P.op("pool", I("tensor_tensor", out=PAB.a(0, [1, 1024]), in0=Fsl.a(0, [1, 1024]),
                                       in1=Fsl.a(1024, [1, 1024]), op=ALU.add),
                             reads=["F0", "F1", "F2", "F3"], writes=["yn"])
                    elif h == 7:
                        P.op("pool", I("tensor_tensor", out=PAB.a(1024, [1, 1024]), in0=Fsl.a(2048, [1, 1024]),
                                       in1=Fsl.a(3072, [1, 1024]), op=ALU.add),
                             reads=["F4", "F5", "F6", "F7"], writes=["yn"])

                def tail0(cg, j):
                    P.op("pool", I("tensor_tensor", out=Ssum.a(0, [1, 1024]), in0=PAB.a(0, [1, 1024]),
                                   in1=PAB.a(1024, [1, 1024]), op=ALU.add), reads=["yn"], writes=["tt"])

                def emit_AT_mm(cg, c, kc):
                    s = cg % 2
                    P.op("pe", I("matmul", ps[5].a(0, [1, 512]), UT.a(s * 4096 + kc * 512 + c * 128, [1, 128]),
                                 hT.a(kc * SEG + bc0, [1, 512]), start=(kc == 0), stop=(kc == 7)),
                         reads=[f"UT{s}"] + hres, writes=[pr(5)])

                def emit_AT_copy(c):
                    P.op("act", I("copy", out=rawAT.a(c * 512, [1, 512]), in_=ps[5].a(0, [1, 512])),
                         reads=[pr(5)], writes=["xs"])

                def emit_AT_gelu(cg):
                    ga = cg % 2
                    P.op("act", I("activation", out=gAs.a(ga * 4 * 512, [1, 2048]), in_=rawAT.a(0, [1, 2048]),
                                  func=AF.Gelu), reads=["xs"], writes=[f"gA{ga}_{c}" for c in range(4)])

                octr = [0]

                def tail1(cg, j):
                    tcn = cg * 4 + j
                    gb = 3
                    hs = tcn % 2
                    for c in range(4):
                        for half in range(2):
                            P.op("pe", I("matmul", ps[gb].a(c * 128, [1, 128]), Ssum.a(half * 512 + c * 128, [1, 128]),
                                         ident.a(0, [1, 128]), start=(half == 0), stop=(half == 1)),
                                 reads=["tt", "ident"], writes=[pr(gb)])

                def tail2(cg, j):
                    s = cg % 2
                    ga = cg % 2
                    tcn = cg * 4 + j
                    gb = 3
                    hs = tcn % 2
                    P.op("dve", I("tensor_tensor", out=HT.a(hs * 512, [128, 4], [1, 128]),
                                  in0=ps[gb].a(0, [128, 4], [1, 128]),
                                  in1=gAs.a(ga * 4 * 512 + j * 128, [512, 4], [1, 128]), op=ALU.mult),
                         reads=[pr(gb)] + [f"gA{ga}_{c}" for c in range(4)], writes=[f"HT{hs}"])

                def tail_mm(cg, j, c):
                    s = cg % 2
                    hs = (cg * 4 + j) % 2
                    for dh in range(2):
                        ob = 6 + dh
                        P.op("pe", I("matmul", ps[ob].a(0, [1, 512]), HT.a(hs * 512 + c * 128, [1, 128]),
                                     Areg.a(s * 4096 + c * D + dh * 512, [1, 512]), start=(c == 0), stop=(c == 3)),
                             reads=[f"HT{hs}", f"A{s}"], writes=[pr(ob)])

                def tail3(cg, j, dh):
                    ti = t0 + j
                    ob = 6 + dh
                    P.op("dve", I("tensor_tensor", out=htok.a(ti * D + dh * 512, [1, 512]), in0=ps[ob].a(0, [1, 512]),
                                  in1=htok.a(ti * D + dh * 512, [1, 512]), op=ALU.add),
                         reads=[pr(ob), f"htok{ti}"], writes=[f"htok{ti}"])

                NCG = 32
                load_ut(0)
                load_ut(1)
                load_v(0)
                for c in range(4):
                    for kc in range(8):
                        emit_AT_mm(0, c, kc)
                    emit_AT_copy(c)
                emit_AT_gelu(0)
                seq = [(cg, j, h) for cg in range(NCG) for j in range(4) for h in range(8)]
                for k in range(NY - 1):
                    emit_Y(k, *seq[k])
                pend_copy = None
                pend_gelu = None
                for idx, (cg, j, h) in enumerate(seq):
                    if pend_copy is not None:
                        emit_AT_copy(pend_copy)
                        pend_copy = None
                        if pend_gelu is not None:
                            emit_AT_gelu(pend_gelu)
                            pend_gelu = None
                    if j == 0 and h == 0 and cg + 2 < NCG:
                        load_ut(cg + 2)
                    if idx >= 8:
                        pcg, pj, _ = seq[idx - 8]
                        if h == 0:
                            tail0(pcg, pj)
                        elif h == 4:
                            tail1(pcg, pj)
                        elif h == 5:
                            tail2(pcg, pj)
                        elif h >= 6:
                            tail_mm(pcg, pj, h - 6)
                    if idx >= 16:
                        ppcg, ppj, _ = seq[idx - 16]
                        if h < 2:
                            tail_mm(ppcg, ppj, h + 2)
                        elif h < 4:
                            tail3(ppcg, ppj, h - 2)
                    if j == 1 and h == 4 and cg + 1 < NCG:
                        load_v(cg + 1)
                    emit_EF(idx, cg, j, h)
                    emit_T(idx, cg, j, h)
                    if cg + 1 < NCG:
                        emit_AT_mm(cg + 1, j, h)
                        if h == 7:
                            pend_copy = j
                            if j == 3:
                                pend_gelu = cg + 1
                    if idx + NY - 1 < len(seq):
                        emit_Y(idx + NY - 1, *seq[idx + NY - 1])
                tail_mm(NCG - 1, 2, 2)
                tail_mm(NCG - 1, 2, 3)
                tail3(NCG - 1, 2, 0)
                tail3(NCG - 1, 2, 1)
                tail0(NCG - 1, 3)
                tail1(NCG - 1, 3)
                tail2(NCG - 1, 3)
                for c in range(4):
                    tail_mm(NCG - 1, 3, c)
                tail3(NCG - 1, 3, 0)
                tail3(NCG - 1, 3, 1)
                def ln2_block(t0=t0):
                    for j in range(4):
                        ti = t0 + j
                        layer_norm(htok.a(ti * D, [1, D]), f"htok{ti}", ti, final, seg)
                def ln2_tile(j, which, t0=t0):
                    lnset[0] = which
                    layer_norm(htok.a((t0 + j) * D, [1, D]), f"htok{t0 + j}", t0 + j, final, seg)
                    lnset[0] = 0
                if blk == 0:
                    ln2_pending = grab_with_pool([6, 7], ln2_block)
                else:
                    for jp in range(2):
                        a_ops = grab_with_pool([0, 1, 2, 3], ln2_tile, 2 * jp, 0)
                        b_ops = grab_with_pool([4, 5, 6, 7], ln2_tile, 2 * jp + 1, 1)
                        P.emit_merged(a_ops, b_ops)

        for seg in range(nseg):
            if seg > 0:
                P.barrier()
            load_ln_params(lnin_h, 0, ALPHA)

            def in_ln(ti, which):
                row0 = seg * SEG + ti * 128
                xb, xn = (xs, "xs") if which == 0 else (xsB, "xsB")
                lnset[0] = which
                P.op("sp", I("dma_start", out=xb.a(0, [1, D]), in_=DAP(x_h, row0 * D, [D, 128], [1, D])),
                     writes=[xn] + (["qT"] if which else []), dma=xn)
                layer_norm(xb.a(0, [1, D]), xn, ti, stop == "ln_in", seg)
                lnset[0] = 0

            for tp in range(NTS // 2):
                a_ops = grab_with_pool([0, 1, 2, 3], in_ln, 2 * tp, 0)
                b_ops = grab_with_pool([4, 5, 6, 7], in_ln, 2 * tp + 1, 1)
                P.emit_merged(a_ops, b_ops)
            if stop == "ln_in":
                continue
            for l in range(nlayers):
                P.barrier()
                mixer(seg, l)
                mark(0)
                P.barrier()
                if stop == "mixer" and l == nlayers - 1:
                    for ti in range(NTS):
                        row0 = seg * SEG + ti * 128
                        P.op("sp", I("dma_start", out=DAP(out_h, row0 * D, [D, 128], [1, D]), in_=htok.a(ti * D, [1, D])),
                            reads=[f"htok{ti}"], dma=f"out{ti}")
                    continue
                peer(seg, l, final=(l == nlayers - 1))
        P.barrier()
        P.emit(st)
    return nc


def _consts():
    c = np.zeros((128, 5, 128), np.float32)
    s = np.arange(128)[:, None]
    t = np.arange(128)[None, :]
    c[:, 0] = np.eye(128)
    c[:, 1] = np.where(s <= t, -1.0 / 16.0, 0.0)
    c[:, 2] = np.where(s > t, -1.0 / 16.0, 0.0)
    c[:, 3] = np.where(s <= t, 1.0, 0.0)
    c[:, 4] = 1.0
    return np.ascontiguousarray(c.reshape(128, 640))


def prep_shared(inp):
    f = lambda a: np.ascontiguousarray(np.asarray(a, dtype=np.float32))
    sh = {
        "cst": _consts(),
        "lnin": f(np.stack([inp["ln_in_g"], inp["ln_in_b"]])),
        "w_in": f(np.asarray(inp["w_in"]).reshape(L * D, INC)),
        "w_out": f(np.asarray(inp["w_out"]).reshape(L * D, D)),
        "wq": f(np.asarray(inp["peer_wq"]).reshape(L * D, 2048)),
        "wgu": f(np.asarray(inp["w_gate_up"]).reshape(L * 16, 256)),
        "bgate": f(inp["b_gate"]),
        "glag": f(inp["gla_norm_g"]),
        "sgln": f(np.stack([np.asarray(inp["sgu_ln_g"]), np.asarray(inp["sgu_ln_b"])], axis=1).reshape(L * 2, 512)),
        "sgwT": f(np.asarray(inp["sgu_w"]).transpose(0, 1, 3, 2).reshape(L * 4 * 128, 128)),
        "sgb": f(np.asarray(inp["sgu_b"]).reshape(L, 512)),
        "ln1": f(np.stack([np.asarray(inp["ln1_g"]), np.asarray(inp["ln1_b"])], axis=1).reshape(L * 2, D)),
        "ln2": f(np.stack([np.asarray(inp["ln2_g"]), np.asarray(inp["ln2_b"])], axis=1).reshape(L * 2, D)),
        "k1T": f(np.asarray(inp["peer_k1"]).transpose(0, 2, 1).reshape(L * 128, 128)),
        "k2T": f(np.asarray(inp["peer_k2"]).transpose(0, 2, 1).reshape(L * 128, 128)),
        "uT": f(np.asarray(inp["peer_u"]).transpose(0, 2, 1).reshape(L * D, NE)),
        "vtab": f(np.asarray(inp["peer_v"]).reshape(L * NE, D)),
    }
    return sh


def kernel(**inputs):
    x = np.asarray(inputs["x"], dtype=np.float32)
    nb = x.shape[0]
    sh = prep_shared(inputs)
    nc = build()
    in_maps = []
    for b in range(nb):
        m = dict(sh)
        m["x"] = np.ascontiguousarray(x[b])
        in_maps.append(m)
    res = run_bass_kernel_spmd(nc, in_maps, core_ids=list(range(nb)))
    return np.stack([np.asarray(r["out"]) for r in res.results], axis=0).astype(np.float32)
```

```python
import numpy as np
from contextlib import ExitStack
import concourse.bass as bass
import concourse.mybir as mybir
from concourse.bass_utils import run_bass_kernel_spmd

F32 = mybir.dt.float32
BF16 = mybir.dt.bfloat16
AF = mybir.ActivationFunctionType
ALU = mybir.AluOpType

D = 1024
NTOK = 2048
L = 2
INC = 2576
SEG = 1024
NTS = 8
NE = 16384
ALPHA = (2.0 * L) ** 0.25
EPS = 1e-5
ENGS = ["pe", "act", "dve", "pool", "sp"]


class Prog:
    def __init__(self, nc, same_engine_sync=True):
        self.nc = nc
        self.streams = {e: [] for e in ENGS}
        self.cnt = {e: 0 for e in ENGS}
        self.dma_cnt = {}
        self.lastw = {}
        self.readers = {}
        self.seen = {e: {} for e in ENGS}
        self.same_engine_sync = same_engine_sync

    def _need(self, eng, tok, waits):
        if tok is None:
            return
        sem, val = tok
        if sem == "E_" + eng and (eng == "pe" or not self.same_engine_sync):
            return
        if self.seen[eng].get(sem, 0) >= val:
            return
        if waits.get(sem, 0) < val:
            waits[sem] = val

    enabled = True
    capture = None

    def op(self, eng, fn, reads=(), writes=(), dma=None):
        if not self.enabled:
            return None
        if self.capture is not None:
            self.capture.append((eng, fn, tuple(reads), tuple(writes), dma))
            return None
        waits = {}
        for r in reads:
            self._need(eng, self.lastw.get(r), waits)
        for w in writes:
            self._need(eng, self.lastw.get(w), waits)
            for sem, val in self.readers.get(w, {}).items():
                self._need(eng, (sem, val), waits)
        for sem, val in waits.items():
            self.seen[eng][sem] = val
        if dma is None:
            self.cnt[eng] += 1
            tok = ("E_" + eng, self.cnt[eng])
        else:
            self.dma_cnt[dma] = self.dma_cnt.get(dma, 0) + 16
            tok = ("D_" + dma, self.dma_cnt[dma])
        self.streams[eng].append((fn, waits, tok))
        for r in reads:
            d = self.readers.setdefault(r, {})
            if d.get(tok[0], 0) < tok[1]:
                d[tok[0]] = tok[1]
        for w in writes:
            self.lastw[w] = tok
            self.readers[w] = {}
        return tok

    def grab(self, fn, *args):
        self.capture = []
        fn(*args)
        ops = self.capture
        self.capture = None
        return ops

    def emit_merged(self, a, b):
        na, nb = len(a), len(b)
        ia = ib = 0
        while ia < na or ib < nb:
            if ib >= nb or (ia < na and ia * nb <= ib * na):
                self.op(*a[ia])
                ia += 1
            else:
                self.op(*b[ib])
                ib += 1

    def wait_only(self, eng, toks):
        waits = {}
        for t in toks:
            self._need(eng, t, waits)
        for sem, val in waits.items():
            self.seen[eng][sem] = val
        if waits:
            self.streams[eng].append((None, waits, None))

    def all_tokens(self):
        toks = [("E_" + e, self.cnt[e]) for e in ENGS if self.cnt[e] > 0]
        toks += [("D_" + k, v) for k, v in self.dma_cnt.items()]
        return toks

    def barrier(self):
        toks = self.all_tokens()
        for e in ENGS:
            self.wait_only(e, toks)

    def emit(self, stack):
        nc = self.nc
        names = ["E_" + e for e in ENGS if self.cnt[e] > 0] + ["D_" + k for k in self.dma_cnt]
        sems = {n: stack.enter_context(nc.semaphore(n)) for n in names}
        block = stack.enter_context(nc.Block())
        deco = {"pe": block.tensor, "act": block.scalar, "dve": block.vector,
                "pool": block.gpsimd, "sp": block.sync}
        for e in ENGS:
            stream = self.streams[e]
            if not stream:
                continue

            def body(eng, stream=stream):
                for fn, waits, tok in stream:
                    for sem, val in waits.items():
                        eng.wait_ge(sems[sem], val)
                    if fn is None:
                        continue
                    inst = fn(eng)
                    inst.then_inc(sems[tok[0]], 16 if tok[0].startswith("D_") else 1)

            deco[e](body)


def I(name, *args, **kw):
    return lambda e: getattr(e, name)(*args, **kw)


class Buf:
    def __init__(self, t, F, base=0):
        self.t = t
        self.F = F
        self.base = base

    def a(self, off, *dims, p0=0, np_=128):
        return bass.AP(self.t, p0 * self.F + self.base + off, [[self.F, np_]] + [list(d) for d in dims])

    def sub(self, base):
        return Buf(self.t, self.F, self.base + base)


DBG_MAP = {}


def build(nseg=2, nlayers=L, stop=None, cut=99, dbg=None):
    nc = bass.Bass("TRN2", target_bir_lowering=False)
    DBG_MAP.clear()
    dbg_h = nc.dram_tensor("dbg", [128, 8192], F32, kind="ExternalOutput") if dbg is not None else None
    dt = lambda n, s: nc.dram_tensor(n, s, F32, kind="ExternalInput")
    x_h = dt("x", [NTOK, D])
    cst_h = dt("cst", [128, 640])
    lnin_h = dt("lnin", [2, D])
    win_h = dt("w_in", [L * D, INC])
    wout_h = dt("w_out", [L * D, D])
    wq_h = dt("wq", [L * D, 2048])
    wgu_h = dt("wgu", [L * 16, 256])
    bgate_h = dt("bgate", [L, 256])
    glag_h = dt("glag", [L, 128])
    sgln_h = dt("sgln", [L * 2, 512])
    sgwT_h = dt("sgwT", [L * 4 * 128, 128])
    sgb_h = dt("sgb", [L, 512])
    ln1_h = dt("ln1", [L * 2, D])
    ln2_h = dt("ln2", [L * 2, D])
    k1T_h = dt("k1T", [L * 128, 128])
    k2T_h = dt("k2T", [L * 128, 128])
    uT_h = dt("uT", [L * D, NE])
    vt_h = dt("vtab", [L * NE, D])
    out_h = nc.dram_tensor("out", [NTOK, D], F32, kind="ExternalOutput")
    DAP = lambda h, off, *dims: bass.AP(h, off, [list(d) for d in dims])

    with ExitStack() as st:
        def sb(name, F, dtype):
            return Buf(st.enter_context(nc.sbuf_tensor(name, [128, F], dtype)), F)

        htok = sb("htok", NTS * D, F32)
        hT = sb("hT", 8 * SEG, BF16)
        W = sb("W", 8 * INC, BF16)
        Areg = sb("Areg", 8 * D, BF16)
        ARENA_BYTES = 54 * 1024
        arena_t = st.enter_context(nc.sbuf_tensor("arena", [128, ARENA_BYTES // 2], BF16))
        arena_f = arena_t.bitcast(F32)
        lnG = sb("lnG", D, F32)
        lnB = sb("lnB", D, F32)
        xs = sb("xs", D, F32)
        yn = sb("yn", D, F32)
        tt = sb("tt", D, F32)
        hbf = sb("hbf", D, BF16)
        junk = hbf
        cst = sb("cstf", 640, F32)
        ident = sb("ident", 128, BF16)
        ones = sb("onesb", 128, BF16)
        stt_ = sb("stat", 64, F32)
        Sf = sb("Sf", L * 512, F32)
        Sb = sb("Sb", L * 512, BF16)
        small = sb("small", 128, F32)
        ps = [Buf(st.enter_context(nc.psum_tensor(f"ps{i}", [128, 512], F32)), 512) for i in range(8)]

        class Arena:
            def __init__(self):
                self.off = 0

            def alloc(self, nelem, dtype):
                sz = 4 if dtype == F32 else 2
                self.off = (self.off + 63) // 64 * 64
                o = self.off
                self.off += nelem * sz
                assert self.off <= ARENA_BYTES, ("arena overflow", self.off)
                if dtype == F32:
                    return Buf(arena_f, ARENA_BYTES // 4, o // 4)
                return Buf(arena_t, ARENA_BYTES // 2, o // 2)

        am = Arena()
        qTf = am.alloc(2048, BF16)
        kTf = am.alloc(2048, BF16)
        silur = am.alloc(2048, BF16)
        gsu = am.alloc(2048, BF16)
        aT = am.alloc(512, BF16)
        Lt = am.alloc(256, F32)
        eb = am.alloc(512, F32)
        enb = am.alloc(512, F32)
        erem = am.alloc(256, F32)
        qtil = am.alloc(512, BF16)
        ktil = am.alloc(512, BF16)
        khat = am.alloc(256, BF16)
        v_bf = am.alloc(512, BF16)
        attn_bf = am.alloc(512, BF16)
        sq = am.alloc(512, BF16)
        rstd = am.alloc(512, F32)
        t1 = am.alloc(512, F32)
        catT = am.alloc(1024, BF16)
        gv = am.alloc(512, F32)
        vn = am.alloc(512, BF16)
        sg_g = am.alloc(512, F32)
        sg_b = am.alloc(512, F32)
        WsT = am.alloc(512, BF16)
        wgu = am.alloc(256, BF16)
        bgate = am.alloc(256, BF16)
        gcol = am.alloc(1, F32)
        bs_f = am.alloc(512, F32)
        bs_hi = am.alloc(512, BF16)
        bs_lo = am.alloc(512, BF16)
        bs_t = am.alloc(512, F32)

        ap_ = Arena()
        qT = ap_.alloc(16 * 512, BF16)
        UT = ap_.alloc(2 * 4096, BF16)
        HT = ap_.alloc(2 * 512, BF16)
        gAs = ap_.alloc(8 * 512, BF16)
        K1T = ap_.alloc(128, BF16)
        K2T = ap_.alloc(128, BF16)
        c16 = ap_.alloc(512, F32)
        candb = ap_.alloc(256, F32)
        scr = ap_.alloc(256, F32)
        candb2 = ap_.alloc(256, F32)
        scr2 = ap_.alloc(256, F32)
        v16 = ap_.alloc(256, F32)
        PAB = Buf(yn.t.bitcast(BF16), 2 * D)
        Ssum = Buf(tt.t.bitcast(BF16), 2 * D)
        rawAT = Buf(xs.t.bitcast(BF16), 2 * D)
        Esl = ap_.alloc(3 * 512, BF16)
        Fsl = W.sub(16384)
        biasv = small.sub(0)
        negm = small.sub(32)
        Zs = small.sub(40)
        lnZ = small.sub(48)
        j16 = small.sub(64)
        sst = small.sub(96)

        P = Prog(nc)
        bank_ctr = [0]

        bank_pool = [list(range(8))]

        def bk():
            pool = bank_pool[0]
            b = pool[bank_ctr[0] % len(pool)]
            bank_ctr[0] += 1
            return b

        def grab_with_pool(pool, fn, *args):
            bank_pool[0] = pool
            ops = P.grab(fn, *args)
            bank_pool[0] = list(range(8))
            return ops

        pr = lambda b: f"ps{b}"

        def mark(n):
            P.enabled = n <= cut

        dbg_col = [0]

        def dump(key, name, apfn, ncols, reads, np_=128):
            if dbg is None or tuple(dbg) != tuple(key) or not P.enabled:
                return
            c0 = dbg_col[0]
            dbg_col[0] += ncols
            DBG_MAP[name] = (c0, ncols, np_)
            P.op("pool", I("dma_start", out=bass.AP(dbg_h, c0, [[8192, np_], [1, ncols]]), in_=apfn()),
                 reads=reads, dma="dbg")

        P.op("sp", I("dma_start", out=cst.a(0, [1, 640]), in_=DAP(cst_h, 0, [640, 128], [1, 640])),
             writes=["cst"], dma="cst")
        P.op("pool", I("dma_start", out=ident.a(0, [1, 128]), in_=DAP(cst_h, 0, [640, 128], [1, 128])),
             writes=["ident"], dma="ident")
        P.op("pool", I("dma_start", out=ones.a(0, [1, 128]), in_=DAP(cst_h, 512, [640, 128], [1, 128])),
             writes=["ones"], dma="ones")
        P.op("dve", I("memset", Sf.a(0, [1, L * 512]), 0.0), writes=[f"Sf{l}" for l in range(L)])
        P.op("dve", I("memset", Sb.a(0, [1, L * 512]), 0.0), writes=[f"Sb{l}" for l in range(L)])
        CI, CTRI, CTRI2, CCAUS = 0, 128, 256, 384

        def load_ln_params(h, row0, scale):
            P.op("sp", I("dma_start", out=lnG.a(0, [1, D]), in_=DAP(h, row0 * D, [0, 128], [1, D])),
                 writes=["lnG"], dma="lnG")
            P.op("sp", I("dma_start", out=lnB.a(0, [1, D]), in_=DAP(h, (row0 + 1) * D, [0, 128], [1, D])),
                 writes=["lnB"], dma="lnB")
            if scale != 1.0:
                P.op("pool", I("tensor_scalar", out=lnG.a(0, [1, D]), in0=lnG.a(0, [1, D]), scalar1=scale,
                                                       scalar2=None, op0=ALU.mult), reads=["lnG"], writes=["lnG"])
                P.op("pool", I("tensor_scalar", out=lnB.a(0, [1, D]), in0=lnB.a(0, [1, D]), scalar1=scale,
                                                       scalar2=None, op0=ALU.mult), reads=["lnB"], writes=["lnB"])

        stat_ctr = [0]
        lnset = [0]
        ynB = Buf(arena_f, ARENA_BYTES // 4, 0)
        ttB = Buf(arena_f, ARENA_BYTES // 4, 1024)
        hbfB = Buf(arena_t, ARENA_BYTES // 2, 4096)
        xsB = Buf(arena_f, ARENA_BYTES // 4, 2560)

        def stats_chain(sbuf, s0, n, r="statchain"):
            c = lambda k: sbuf.a(s0 + k, [1, 1])
            P.op("dve", I("tensor_scalar", out=c(2), in0=c(0), scalar1=1.0 / n, scalar2=None, op0=ALU.mult),
                 reads=[r], writes=[r])
            P.op("dve", I("tensor_tensor", out=c(3), in0=c(2), in1=c(2), op=ALU.mult), reads=[r], writes=[r])
            P.op("dve", I("scalar_tensor_tensor", out=c(4), in0=c(1), scalar=1.0 / n, in1=c(3),
                                                         op0=ALU.mult, op1=ALU.subtract), reads=[r], writes=[r])
            P.op("act", I("activation", out=c(5), in_=c(4), func=AF.Ln, bias=EPS), reads=[r], writes=[r])
            P.op("act", I("activation", out=c(6), in_=c(5), func=AF.Exp, scale=-0.5), reads=[r], writes=[r])
            P.op("dve", I("scalar_tensor_tensor", out=c(7), in0=c(2), scalar=-1.0, in1=c(6),
                                                         op0=ALU.mult, op1=ALU.mult), reads=[r], writes=[r])

        def layer_norm(src, src_res, ti, final, seg):
            if lnset[0] == 0:
                yn_, tt_, hbf_, nyn, ntt, nhbf, rchain, sbase, acq = yn, tt, hbf, "yn", "tt", "hbf", "statchain", 0, []
            else:
                yn_, tt_, hbf_, nyn, ntt, nhbf, rchain, sbase, acq = ynB, ttB, hbfB, "ynB", "ttB", "hbfB", "statchainB", 32, ["qT"]
            junk_ = hbf_
            s0 = sbase + (stat_ctr[0] % 4) * 8
            stat_ctr[0] += 1
            hres = f"htok{ti}"
            P.op("act", I("activation", out=junk_.a(0, [1, D]), in_=src, func=AF.Identity,
                                               accum_out=stt_.a(s0, [1, 1])),
                 reads=[src_res], writes=[nhbf, rchain] + acq)
            P.op("act", I("activation", out=junk_.a(0, [1, D]), in_=src, func=AF.Square,
                                               accum_out=stt_.a(s0 + 1, [1, 1])),
                 reads=[src_res], writes=[nhbf, rchain])
            stats_chain(stt_, s0, float(D), rchain)
            P.op("act", I("activation", out=yn_.a(0, [1, D]), in_=src, func=AF.Identity,
                                               scale=stt_.a(s0 + 6, [1, 1]), bias=stt_.a(s0 + 7, [1, 1])),
                 reads=[src_res, rchain], writes=[nyn])
            P.op("dve", I("tensor_tensor", out=tt_.a(0, [1, D]), in0=yn_.a(0, [1, D]), in1=lnG.a(0, [1, D]),
                                                   op=ALU.mult), reads=[nyn, "lnG"], writes=[ntt])
            P.op("dve", I("tensor_tensor", out=htok.a(ti * D, [1, D]), in0=tt_.a(0, [1, D]),
                                                  in1=lnB.a(0, [1, D]), op=ALU.add),
                 reads=[ntt, "lnB", src_res], writes=[hres])
            if final:
                row0 = seg * SEG + ti * 128
                P.op("sp", I("dma_start", out=DAP(out_h, row0 * D, [D, 128], [1, D]), in_=htok.a(ti * D, [1, D])),
                     reads=[hres], dma=f"out{ti}")
                return
            P.op("act", I("activation", out=hbf_.a(0, [1, D]), in_=htok.a(ti * D, [1, D]), func=AF.Copy,
                                               scale=1.0 / ALPHA), reads=[hres], writes=[nhbf])
            for half in range(2):
                b = bk()
                for k4 in range(4):
                    kc = half * 4 + k4
                    P.op("pe", I("matmul", ps[b].a(k4 * 128, [1, 128]), hbf_.a(kc * 128, [1, 128]), ident.a(0, [1, 128]),
                        start=True, stop=True), reads=[nhbf, "ident"], writes=[pr(b)])
                eng = "act" if half == 0 else "dve"
                dst = hT.a((half * 4) * SEG + ti * 128, [SEG, 4], [1, 128])
                srcp = ps[b].a(0, [128, 4], [1, 128])
                if eng == "act":
                    P.op("act", I("copy", out=dst, in_=srcp), reads=[pr(b)], writes=[f"hT{ti}"])
                else:
                    P.op("dve", I("tensor_copy", out=dst, in_=srcp), reads=[pr(b)],
                         writes=[f"hT{ti}"])

        def prefetch_wq(l):
            for g4 in range(4):
                P.op("pool", I("dma_start", out=W.a(g4 * 512, [2048, 8], [1, 512]),
                               in_=DAP(wq_h, l * D * 2048 + g4 * 512, [2048, 128], [128 * 2048, 8], [1, 512])),
                     writes=[f"Wq{g4}", "Wa", "Wb"], dma=f"Wq{g4}")

        def mixer(seg, l):
            for c0, cw, rn in ((0, 1280, "Wa"), (1280, 1296, "Wb")):
                P.op("pool", I("dma_start", out=W.a(c0, [INC, 8], [1, cw]),
                    in_=DAP(win_h, l * D * INC + c0, [INC, 128], [128 * INC, 8], [1, cw])),
                    writes=[rn], dma=rn)

            def wres(c0, n):
                r = []
                if c0 < 1280:
                    r.append("Wa")
                if c0 + n > 1280:
                    r.append("Wb")
                return r
            P.op("pool", I("dma_start", out=Areg.a(0, [D, 8], [1, D]),
                                               in_=DAP(wout_h, l * D * D, [D, 128], [128 * D, 8], [1, D])),
                 writes=["A0", "A1"], dma="A")
            P.op("dve", I("memset", wgu.a(0, [1, 256], np_=32), 0.0), writes=["wgu"])
            P.op("dve", I("memset", bgate.a(0, [1, 256], np_=32), 0.0), writes=["bgate"])
            P.op("dve", I("memset", aT.a(0, [1, 512], np_=32), 0.0), writes=["aT"])
            P.op("dve", I("memset", bs_hi.a(0, [1, 512], np_=32), 0.0), writes=["bs_hi"])
            P.op("dve", I("memset", bs_lo.a(0, [1, 512], np_=32), 0.0), writes=["bs_lo"])
            P.op("pool", I("dma_start", out=wgu.a(0, [1, 256], np_=16), in_=DAP(wgu_h, l * 16 * 256, [256, 16], [1, 256])),
                 writes=["wgu"], dma="wgu")
            P.op("pool", I("dma_start", out=bgate.a(0, [1, 256], np_=1), in_=DAP(bgate_h, l * 256, [256, 1], [1, 256])),
                 writes=["bgate"], dma="bgate")
            P.op("pool", I("dma_start", out=WsT.a(0, [128, 4], [1, 128]),
                                               in_=DAP(sgwT_h, l * 4 * 128 * 128, [128, 128], [128 * 128, 4], [1, 128])),
                 writes=["WsT"], dma="WsT")
            P.op("sp", I("dma_start", out=gcol.a(0, [1, 1]), in_=DAP(glag_h, l * 128, [1, 128], [1, 1])),
                 writes=["gcol"], dma="gcol")
            P.op("sp", I("dma_start", out=sg_g.a(0, [1, 512]), in_=DAP(sgln_h, (2 * l) * 512, [0, 128], [1, 512])),
                 writes=["sg_g"], dma="sg_g")
            P.op("sp", I("dma_start", out=sg_b.a(0, [1, 512]), in_=DAP(sgln_h, (2 * l + 1) * 512, [0, 128], [1, 512])),
                 writes=["sg_b"], dma="sg_b")
            P.op("sp", I("dma_start", out=bs_f.a(0, [1, 512], np_=1), in_=DAP(sgb_h, l * 512, [512, 1], [1, 512])),
                 writes=["bs_f"], dma="bs_f")
            load_ln_params(ln1_h, 2 * l, ALPHA)
            P.op("dve", I("memset", WsT.a(0, [128, 4], [1, 64], p0=64, np_=64), 0.0), reads=["WsT"], writes=["WsT"])
            P.op("dve", I("tensor_copy", out=bs_hi.a(0, [1, 512], np_=1), in_=bs_f.a(0, [1, 512], np_=1)),
                 reads=["bs_f"], writes=["bs_hi"])
            P.op("dve", I("tensor_tensor", out=bs_t.a(0, [1, 512], np_=1), in0=bs_f.a(0, [1, 512], np_=1),
                                                  in1=bs_hi.a(0, [1, 512], np_=1), op=ALU.subtract),
                 reads=["bs_f", "bs_hi"], writes=["bs_t"])
            P.op("dve", I("tensor_copy", out=bs_lo.a(0, [1, 512], np_=1), in_=bs_t.a(0, [1, 512], np_=1)),
                 reads=["bs_t"], writes=["bs_lo"])

            def proj_fm(c0, m, q):
                b = bk()
                hres = [f"hT{4 * q + j}" for j in range(4)]
                for kc in range(8):
                    P.op("pe", I("matmul", ps[b].a(0, [1, 512], np_=m), W.a(kc * INC + c0, [1, m]), hT.a(kc * SEG + q * 512, [1, 512]),
                        start=(kc == 0), stop=(kc == 7)), reads=wres(c0, m) + hres, writes=[pr(b)])
                return b

            def proj_tm(c0, n, ti):
                b = bk()
                for kc in range(8):
                    P.op("pe", I("matmul", ps[b].a(0, [1, n]), hT.a(kc * SEG + ti * 128, [1, 128]), W.a(kc * INC + c0, [1, n]),
                        start=(kc == 0), stop=(kc == 7)), reads=wres(c0, n) + [f"hT{ti}"], writes=[pr(b)])
                return b

            for q in range(2):
                mark(1)
                for i in range(4):
                    b = proj_fm(i * 64, 64, q)
                    P.op("act", I("copy", out=qTf.a(i * 512, [1, 512], np_=64),
                                                           in_=ps[b].a(0, [1, 512], np_=64)),
                         reads=[pr(b)], writes=["qTf"])
                for i in range(4):
                    b = proj_fm(256 + i * 64, 64, q)
                    P.op("dve", I("tensor_copy", out=kTf.a(i * 512, [1, 512], np_=64),
                                                                  in_=ps[b].a(0, [1, 512], np_=64)),
                         reads=[pr(b)], writes=["kTf"])
                for i in range(4):
                    b = proj_fm(1024 + i * 128, 128, q)
                    P.op("act", I("activation", out=silur.a(i * 512, [1, 512]), in_=ps[b].a(0, [1, 512]),
                                                                 func=AF.Silu), reads=[pr(b)], writes=["silur"])
                for i in range(4):
                    b = proj_fm(1552 + i * 128, 128, q)
                    P.op("act", I("activation", out=gsu.a(i * 512, [1, 512]), in_=ps[b].a(0, [1, 512]),
                                                                 func=AF.Gelu), reads=[pr(b)], writes=["gsu"])
                b = proj_fm(1536, 16, q)
                P.op("act", I("copy", out=aT.a(0, [1, 512], np_=16), in_=ps[b].a(0, [1, 512], np_=16)),
                     reads=[pr(b)], writes=["aT"])

                def tile_body(j, part):
                    ti = 4 * q + j
                    tc0 = j * 128
                    key = (seg, l, ti)
                    if part == 0:
                        mark(2)
                        b = bk()
                        P.op("pe", I("matmul", ps[b].a(0, [1, 256]), aT.a(tc0, [1, 128], np_=32),
                                                           wgu.a(0, [1, 256], np_=32), start=True, stop=False),
                             reads=["aT", "wgu"], writes=[pr(b)])
                        P.op("pe", I("matmul", ps[b].a(0, [1, 256]), ones.a(0, [1, 128], np_=32),
                                                           bgate.a(0, [1, 256], np_=32), start=False, stop=True),
                             reads=["ones", "bgate"], writes=[pr(b)])
                        P.op("act", I("activation", out=Lt.a(0, [1, 256]), in_=ps[b].a(0, [1, 256]), func=AF.Exp,
                                                                scale=-1.0), reads=[pr(b)], writes=["Lt"])
                        dump(key, "aT", lambda: aT.a(0, [1, 512], np_=16), 512, ["aT"], np_=16)
                        dump(key, "wgu", lambda: wgu.a(0, [1, 256], np_=16), 256, ["wgu"], np_=16)
                        dump(key, "bgate", lambda: bgate.a(0, [1, 256], np_=1), 256, ["bgate"], np_=1)
                        dump(key, "expnx", lambda: Lt.a(0, [1, 256]), 256, ["Lt"])
                        P.op("act", I("activation", out=Lt.a(0, [1, 256]), in_=Lt.a(0, [1, 256]), func=AF.Ln, bias=1.0),
                             reads=["Lt"], writes=["Lt"])
                        mark(3)
                        dump(key, "Lt", lambda: Lt.a(0, [1, 256]), 256, ["Lt"])
                        b2 = bk()
                        for h in range(4):
                            P.op("pe", I("matmul", ps[b2].a(h * 128, [1, 128], np_=64), Lt.a(h * 64, [1, 64]),
                                                                      cst.a(CTRI, [1, 128]), start=True, stop=True),
                                 reads=["Lt", "cst"], writes=[pr(b2)])
                        b3 = bk()
                        P.op("pe", I("matmul", ps[b3].a(0, [1, 256]), cst.a(CTRI2, [1, 128]), Lt.a(0, [1, 256]),
                                                             start=True, stop=True), reads=["Lt", "cst"], writes=[pr(b3)])
                        P.op("act", I("activation", out=eb.a(0, [1, 512], np_=64), in_=ps[b2].a(0, [1, 512], np_=64),
                                                                  func=AF.Exp), reads=[pr(b2)], writes=["eb"])
                        P.op("act", I("activation", out=enb.a(0, [1, 512], np_=64), in_=ps[b2].a(0, [1, 512], np_=64),
                                                                  func=AF.Exp, scale=-1.0), reads=[pr(b2)], writes=["enb"])
                        P.op("act", I("activation", out=erem.a(0, [1, 256]), in_=ps[b3].a(0, [1, 256]), func=AF.Exp),
                             reads=[pr(b3)], writes=["erem"])
                        P.op("dve", I("scalar_tensor_tensor", out=qtil.a(0, [128, 4], [1, 128], np_=64), in0=qTf.a(tc0, [512, 4], [1, 128], np_=64), scalar=0.125,
                            in1=eb.a(0, [128, 4], [1, 128], np_=64), op0=ALU.mult, op1=ALU.mult),
                            reads=["qTf", "eb"], writes=["qtil"])
                        P.op("dve", I("tensor_tensor", out=ktil.a(0, [128, 4], [1, 128], np_=64), in0=kTf.a(tc0, [512, 4], [1, 128], np_=64),
                            in1=enb.a(0, [128, 4], [1, 128], np_=64), op=ALU.mult), reads=["kTf", "enb"], writes=["ktil"])
                        dump(key, "eb", lambda: eb.a(0, [1, 512], np_=64), 512, ["eb"], np_=64)
                        dump(key, "erem", lambda: erem.a(0, [1, 256]), 256, ["erem"])
                        dump(key, "qtil", lambda: qtil.a(0, [1, 512], np_=64), 512, ["qtil"], np_=64)
                        dump(key, "ktil", lambda: ktil.a(0, [1, 512], np_=64), 512, ["ktil"], np_=64)
                        mark(4)
                        b4 = proj_tm(256, 256, ti)
                        mark(4.02)
                        P.op("dve", I("tensor_tensor", out=khat.a(0, [1, 256]), in0=ps[b4].a(0, [1, 256]),
                                                                     in1=erem.a(0, [1, 256]), op=ALU.mult),
                             reads=[pr(b4), "erem"], writes=["khat"])
                        mark(4.03)
                        b5 = proj_tm(512, 512, ti)
                        mark(4.04)
                        P.op("act", I("copy", out=v_bf.a(0, [1, 512]), in_=ps[b5].a(0, [1, 512])),
                             reads=[pr(b5)], writes=["v_bf"])
                    if part == 1:
                        mark(4.2)
                        b6 = bk()
                        for h in range(4):
                            c, pb = h // 2, (h % 2) * 64
                            P.op("pe", I("matmul", ps[b6].a(h * 128, [1, 128]), ktil.a(h * 128, [1, 128], np_=64),
                                qtil.a(h * 128, [1, 128], np_=64), start=True, stop=True),
                                reads=["ktil", "qtil"], writes=[pr(b6)])
                        mark(4.4)
                        P.op("dve", I("tensor_tensor", out=attn_bf.a(0, [128, 4], [1, 128]), in0=ps[b6].a(0, [128, 4], [1, 128]),
                            in1=cst.a(CCAUS, [0, 4], [1, 128]), op=ALU.mult), reads=[pr(b6), "cst"], writes=["attn_bf"])
                        dump(key, "khat", lambda: khat.a(0, [1, 256]), 256, ["khat"])
                        dump(key, "v_bf", lambda: v_bf.a(0, [1, 512]), 512, ["v_bf"])
                        dump(key, "attn", lambda: attn_bf.a(0, [1, 512]), 512, ["attn_bf"])
                        mark(4.6)
                        b7 = bk()
                        for h in range(4):
                            c, pb = h // 2, (h % 2) * 64
                            P.op("pe", I("matmul", ps[b7].a(h * 128, [1, 128]), v_bf.a(h * 128, [1, 128]), attn_bf.a(h * 128, [1, 128]),
                                start=True, stop=False), reads=["v_bf", "attn_bf"], writes=[pr(b7)])
                            P.op("pe", I("matmul", ps[b7].a(h * 128, [1, 128]), Sb.a(l * 512 + h * 128, [1, 128], np_=64),
                                qtil.a(h * 128, [1, 128], np_=64), start=False, stop=True),
                                reads=[f"Sb{l}", "qtil"], writes=[pr(b7)])
                        mark(5)
                        b8 = bk()
                        for h in range(4):
                            P.op("pe", I("matmul", ps[b8].a(h * 128, [1, 128], np_=64), khat.a(h * 64, [1, 64]), v_bf.a(h * 128, [1, 128]),
                                start=True, stop=True), reads=["khat", "v_bf"], writes=[pr(b8)])
                        for h in range(4):
                            so = l * 512 + h * 128
                            P.op("dve", I("scalar_tensor_tensor", out=Sf.a(so, [1, 128], np_=64), in0=Sf.a(so, [1, 128], np_=64),
                                scalar=eb.a(h * 128 + 127, [1, 1], np_=64),
                                in1=ps[b8].a(h * 128, [1, 128], np_=64),
                                op0=ALU.mult, op1=ALU.add), reads=[f"Sf{l}", "eb", pr(b8)], writes=[f"Sf{l}"])
                        P.op("act", I("copy", out=Sb.a(l * 512, [1, 512], np_=64), in_=Sf.a(l * 512, [1, 512], np_=64)),
                             reads=[f"Sf{l}"], writes=[f"Sb{l}"])
                        mark(6)
                        P.op("act", I("activation", out=sq.a(0, [1, 512]), in_=ps[b7].a(0, [1, 512]), func=AF.Square),
                             reads=[pr(b7)], writes=["sq"])
                        b9 = bk()
                        P.op("pe", I("matmul", ps[b9].a(0, [1, 512]), ones.a(0, [1, 128]), sq.a(0, [1, 512]),
                                                             start=True, stop=True), reads=["ones", "sq"], writes=[pr(b9)])
                        P.op("act", I("activation", out=rstd.a(0, [1, 512]), in_=ps[b9].a(0, [1, 512]), func=AF.Ln,
                                                              scale=1.0 / 128.0, bias=EPS), reads=[pr(b9)], writes=["rstd"])
                        P.op("act", I("activation", out=rstd.a(0, [1, 512]), in_=rstd.a(0, [1, 512]), func=AF.Exp, scale=-0.5),
                             reads=["rstd"], writes=["rstd"])
                        P.op("dve", I("scalar_tensor_tensor", out=t1.a(0, [1, 512]), in0=ps[b7].a(0, [1, 512]), scalar=gcol.a(0, [1, 1]), in1=rstd.a(0, [1, 512]),
                            op0=ALU.mult, op1=ALU.mult), reads=[pr(b7), "gcol", "rstd"], writes=["t1"])
                        P.op("dve", I("tensor_tensor", out=catT.a(0, [128, 4], [1, 128]), in0=t1.a(0, [128, 4], [1, 128]),
                            in1=silur.a(tc0, [512, 4], [1, 128]), op=ALU.mult), reads=["t1", "silur"], writes=["catT_o"])
                        dump(key, "t1", lambda: t1.a(0, [1, 512]), 512, ["t1"])
                        dump(key, "Sf", lambda: Sf.a(l * 512, [1, 512], np_=64), 512, [f"Sf{l}"], np_=64)
                    if part == 2:
                        mark(7)
                        b10 = proj_tm(2064, 512, ti)
                        P.op("act", I("activation", out=gv.a(0, [1, 512]), in_=ps[b10].a(0, [1, 512]), func=AF.Gelu,
                                                                    accum_out=sst.a(0, [1, 1])),
                             reads=[pr(b10)], writes=["gv", "statchain"])
                        P.op("act", I("activation", out=junk.a(0, [1, 512]), in_=gv.a(0, [1, 512]), func=AF.Square,
                                                           accum_out=sst.a(1, [1, 1])),
                             reads=["gv"], writes=["hbf", "statchain"])
                        stats_chain(sst, 0, 512.0)
                        P.op("act", I("activation", out=gv.a(0, [1, 512]), in_=gv.a(0, [1, 512]), func=AF.Identity,
                                                           scale=sst.a(6, [1, 1]), bias=sst.a(7, [1, 1])),
                             reads=["gv", "statchain"], writes=["gv"])
                        P.op("dve", I("tensor_tensor", out=gv.a(0, [1, 512]), in0=gv.a(0, [1, 512]), in1=sg_g.a(0, [1, 512]),
                                                               op=ALU.mult), reads=["gv", "sg_g"], writes=["gv"])
                        P.op("dve", I("tensor_tensor", out=vn.a(0, [1, 512]), in0=gv.a(0, [1, 512]), in1=sg_b.a(0, [1, 512]),
                                                              op=ALU.add), reads=["gv", "sg_b"], writes=["vn"])
                        b11 = bk()
                        for g in range(4):
                            P.op("pe", I("matmul", ps[b11].a(g * 128, [1, 128]), vn.a(g * 128, [1, 128]), WsT.a(g * 128, [1, 128]),
                                start=True, stop=False), reads=["vn", "WsT"], writes=[pr(b11)])
                            P.op("pe", I("matmul", ps[b11].a(g * 128, [1, 128]), ones.a(0, [1, 128], np_=32), bs_hi.a(g * 128, [1, 128], np_=32),
                                start=False, stop=False), reads=["ones", "bs_hi"], writes=[pr(b11)])
                            P.op("pe", I("matmul", ps[b11].a(g * 128, [1, 128]), ones.a(0, [1, 128], np_=32), bs_lo.a(g * 128, [1, 128], np_=32),
                                start=False, stop=True), reads=["ones", "bs_lo"], writes=[pr(b11)])
                        P.op("dve", I("tensor_tensor", out=catT.a(512, [128, 4], [1, 128]), in0=ps[b11].a(0, [128, 4], [1, 128]),
                            in1=gsu.a(tc0, [512, 4], [1, 128]), op=ALU.mult), reads=[pr(b11), "gsu"], writes=["catT_g"])
                        dump(key, "vn", lambda: vn.a(0, [1, 512]), 512, ["vn"])
                        dump(key, "catT", lambda: catT.a(0, [1, 1024]), 1024, ["catT_o", "catT_g"])
                        mark(8)
                        for dh in range(2):
                            b12 = bk()
                            for cc in range(8):
                                P.op("pe", I("matmul", ps[b12].a(0, [1, 512]), catT.a(cc * 128, [1, 128]), Areg.a(cc * D + dh * 512, [1, 512]),
                                    start=(cc == 0), stop=(cc == 7)),
                                    reads=["catT_o", "catT_g", "A0", "A1"], writes=[pr(b12)])
                            P.op("dve", I("tensor_tensor", out=htok.a(ti * D + dh * 512, [1, 512]), in0=ps[b12].a(0, [1, 512]),
                                in1=htok.a(ti * D + dh * 512, [1, 512]), op=ALU.add),
                                reads=[pr(b12), f"htok{ti}"], writes=[f"htok{ti}"])
                        dump(key, "hpre", lambda: htok.a(ti * D, [1, D]), 1024, [f"htok{ti}"])
                    if part == 3:
                        layer_norm(htok.a(ti * D, [1, D]), f"htok{ti}", ti, False, seg)

                G = lambda jj, part: grab_with_pool([0, 1, 2, 3, 4] if part < 2 else [5, 6, 7], tile_body, jj, part)
                P.emit_merged(G(0, 0), [])
                P.emit_merged(G(0, 1), [])
                for jj in range(1, 4):
                    P.emit_merged(G(jj, 0), G(jj - 1, 2))
                    P.emit_merged(G(jj, 1), G(jj - 1, 3))
                P.emit_merged(G(3, 2), [])
                P.emit_merged(G(3, 3), [])

        def peer(seg, l, final):
            P.op("pool", I("dma_start", out=K1T.a(0, [1, 128]), in_=DAP(k1T_h, l * 128 * 128, [128, 128], [1, 128])),
                 writes=["K1T"], dma="K1T")
            P.op("pool", I("dma_start", out=K2T.a(0, [1, 128]), in_=DAP(k2T_h, l * 128 * 128, [128, 128], [1, 128])),
                 writes=["K2T"], dma="K2T")
            load_ln_params(ln2_h, 2 * l, 1.0 if final else ALPHA)
            cgi = [0]

            def load_ut(cg):
                s = cg % 2
                P.op("pool", I("dma_start",
                    out=UT.a(s * 4096, [512, 8], [1, 512]),
                    in_=DAP(uT_h, l * D * NE + cg * 512, [NE, 128], [128 * NE, 8], [1, 512])),
                    writes=[f"UT{s}"], dma=f"UT{s}")

            def load_v(cg):
                s = cg % 2
                P.op("pool", I("dma_start",
                    out=Areg.a(s * 4096, [D, 4], [1, D]),
                    in_=DAP(vt_h, (l * NE + cg * 512) * D, [D, 128], [128 * D, 4], [1, D])),
                    writes=[f"A{s}"], dma=f"V{s}")

            ln2_pending = []
            for blk in range(2):
                t0 = blk * 4
                bc0 = blk * 512
                hres = [f"hT{t0 + j}" for j in range(4)]
                bank_pool[0] = [0, 1, 2, 3, 4, 5]
                P.capture = []
                for jq in range(16):
                    b = bk()
                    for kc in range(8):
                        P.op("pe", I("matmul", ps[b].a(0, [1, 512]), W.a(kc * 2048 + jq * 128, [1, 128]), hT.a(kc * SEG + bc0, [1, 512]),
                            start=(kc == 0), stop=(kc == 7)), reads=[f"Wq{jq // 4}"] + hres, writes=[pr(b)])
                    if jq % 2 == 0:
                        P.op("act", I("copy", out=qT.a(jq * 512, [1, 512]), in_=ps[b].a(0, [1, 512])),
                             reads=[pr(b)], writes=["qT"])
                    else:
                        P.op("dve", I("tensor_copy", out=qT.a(jq * 512, [1, 512]), in_=ps[b].a(0, [1, 512])),
                             reads=[pr(b)], writes=["qT"])
                for j in range(4):
                    tc0 = j * 128
                    banks = [bk() for _ in range(4)]
                    for jq in range(16):
                        b = banks[jq // 4]
                        KT = K1T if jq % 2 == 0 else K2T
                        P.op("pe", I("matmul", ps[b].a((jq % 4) * 128, [1, 128]), qT.a(jq * 512 + tc0, [1, 128]), KT.a(0, [1, 128]),
                            start=True, stop=True), reads=["qT", "K1T", "K2T"], writes=[pr(b)])
                    for jq0 in range(0, 16, 2):
                        pr2 = [(jq0 + k, ps[banks[(jq0 + k) // 4]].a(((jq0 + k) % 4) * 128, [1, 128]), k) for k in range(2)]
                        for jq, sl, k in pr2:
                            P.op("dve", I("max", out=v16.a(jq * 16, [1, 8]), in_=sl),
                                 reads=[pr(banks[jq // 4])], writes=[f"v16a_{jq}"])
                        for jq, sl, k in pr2:
                            P.op("dve", I("match_replace", out=scr.a(k * 128, [1, 128]), in_to_replace=v16.a(jq * 16, [1, 8]),
                                          in_values=sl, imm_value=-1e30),
                                 reads=[pr(banks[jq // 4]), f"v16a_{jq}"], writes=[f"scr{k}"])
                        for jq, sl, k in pr2:
                            P.op("dve", I("max", out=v16.a(jq * 16 + 8, [1, 8]), in_=scr.a(k * 128, [1, 128])),
                                 reads=[f"scr{k}"], writes=[f"v16b_{jq}"])
                    cres = f"c16_{j}"
                    for h0 in range(0, 8, 2):
                        hp = [(h0 + k, j * 128 + (h0 + k) * 16, candb if k == 0 else candb2, scr if k == 0 else scr2, k) for k in range(2)]
                        for h, co, cb, sc, k in hp:
                            P.op("dve", I("tensor_tensor", out=cb.a(0, [16, 16], [1, 16]), in0=v16.a(2 * h * 16, [1, 16], [0, 16]),
                                          in1=v16.a((2 * h + 1) * 16, [0, 16], [1, 16]), op=ALU.add),
                                 reads=[f"v16a_{2 * h}", f"v16b_{2 * h}", f"v16a_{2 * h + 1}", f"v16b_{2 * h + 1}"],
                                 writes=[f"candb{k}"])
                        for h, co, cb, sc, k in hp:
                            P.op("dve", I("max", out=c16.a(co, [1, 8]), in_=cb.a(0, [1, 256])),
                                 reads=[f"candb{k}"], writes=[f"{cres}_{h}"])
                        for h, co, cb, sc, k in hp:
                            P.op("dve", I("match_replace", out=sc.a(0, [1, 256]), in_to_replace=c16.a(co, [1, 8]),
                                          in_values=cb.a(0, [1, 256]), imm_value=-1e30),
                                 reads=[f"candb{k}", f"{cres}_{h}"], writes=(["scr0", "scr1"] if k == 0 else []) + [f"scrc{k}"])
                        for h, co, cb, sc, k in hp:
                            P.op("dve", I("max", out=c16.a(co + 8, [1, 8]), in_=sc.a(0, [1, 256])),
                                 reads=[f"scrc{k}"], writes=[f"{cres}_{h}"])
                    P.op("dve", I("tensor_scalar", out=negm.a(0, [1, 8]), in0=c16.a(j * 128, [16, 8]), scalar1=-1.0,
                                                          scalar2=None, op0=ALU.mult), reads=[f"{cres}_{h}" for h in range(8)], writes=["negm"])
                    for h in range(8):
                        co = j * 128 + h * 16
                        P.op("act", I("activation", out=j16.a(0, [1, 16]), in_=c16.a(co, [1, 16]), func=AF.Exp, bias=negm.a(h, [1, 1]),
                            accum_out=Zs.a(h, [1, 1])), reads=[f"{cres}_{h}", "negm"], writes=["j16", "Zs"])
                    P.op("act", I("activation", out=lnZ.a(0, [1, 8]), in_=Zs.a(0, [1, 8]), func=AF.Ln),
                         reads=["Zs"], writes=["lnZ"])
                    P.op("dve", I("tensor_tensor", out=biasv.a(j * 8, [1, 8]), in0=negm.a(0, [1, 8]),
                                                          in1=lnZ.a(0, [1, 8]), op=ALU.subtract),
                         reads=["negm", "lnZ"], writes=[f"biasv{j}"])
                p12_ops = P.capture
                P.capture = None
                bank_pool[0] = list(range(8))
                P.emit_merged(p12_ops, ln2_pending)
                ln2_pending = []
                NY = 4
                YB = [0, 1, 2, 4]

                def emit_Y(idx, cg, j, h):
                    yb = YB[idx % NY]
                    tc0 = j * 128
                    P.op("pe", I("matmul", ps[yb].a(0, [1, 512]), qT.a((2 * h + 1) * 512 + tc0, [1, 128]),
                                 K2T.a(0, [0, 4], [1, 128]), start=True, stop=False),
                         reads=["qT", "K2T"], writes=[pr(yb)])
                    P.op("pe", I("matmul", ps[yb].a(0, [1, 512]), qT.a((2 * h) * 512 + tc0, [1, 128]),
                                 K1T.a(cg * 4, [1, 4], [0, 128]), start=False, stop=True),
                         reads=["qT", "K1T"], writes=[pr(yb)])

                def emit_EF(idx, cg, j, h):
                    yb = YB[idx % NY]
                    es = idx % 3
                    fs = idx % 8
                    P.op("act", I("activation", out=Esl.a(es * 512, [1, 512]), in_=ps[yb].a(0, [1, 512]), func=AF.Exp,
                                  bias=biasv.a(j * 8 + h, [1, 1])), reads=[pr(yb), f"biasv{j}"], writes=[f"E{es}"])
                    P.op("dve", I("scalar_tensor_tensor", out=Fsl.a(fs * 512, [1, 512]), in0=ps[yb].a(0, [1, 512]),
                                  scalar=c16.a(j * 128 + h * 16 + 15, [1, 1]), in1=Esl.a(es * 512, [1, 512]),
                                  op0=ALU.is_ge, op1=ALU.mult), reads=[pr(yb), f"c16_{j}_{h}", f"E{es}"], writes=[f"F{fs}"])

                def emit_T(idx, cg, j, h):
                    if h == 3:
                        P.op("pool", I("tensor_tensor", out=PAB.a(0, [1, 1024]), in0=Fsl.a(0, [1, 1024]),
                                       in1=Fsl.a(1024, [1, 1024]), op=ALU.add),
                             reads=["F0", "F1", "F2", "F3"], writes=["yn"])
                    elif h == 7:
                        P.op("pool", I("tensor_tensor", out=PAB.a(1024, [1, 1024]), in0=Fsl.a(2048, [1, 1024]),
                                       in1=Fsl.a(3072, [1, 1024]), op=ALU.add),
                             reads=["F4", "F5", "F6", "F7"], writes=["yn"])

                def tail0(cg, j):
                    P.op("pool", I("tensor_tensor", out=Ssum.a(0, [1, 1024]), in0=PAB.a(0, [1, 1024]),
                                   in1=PAB.a(1024, [1, 1024]), op=ALU.add), reads=["yn"], writes=["tt"])

                def emit_AT_mm(cg, c, kc):
                    s = cg % 2
                    P.op("pe", I("matmul", ps[5].a(0, [1, 512]), UT.a(s * 4096 + kc * 512 + c * 128, [1, 128]),
                                 hT.a(kc * SEG + bc0, [1, 512]), start=(kc == 0), stop=(kc == 7)),
                         reads=[f"UT{s}"] + hres, writes=[pr(5)])

                def emit_AT_copy(c):
                    P.op("act", I("copy", out=rawAT.a(c * 512, [1, 512]), in_=ps[5].a(0, [1, 512])),
                         reads=[pr(5)], writes=["xs"])

                def emit_AT_gelu(cg):
                    ga = cg % 2
                    P.op("act", I("activation", out=gAs.a(ga * 4 * 512, [1, 2048]), in_=rawAT.a(0, [1, 2048]),
                                  func=AF.Gelu), reads=["xs"], writes=[f"gA{ga}_{c}" for c in range(4)])

                octr = [0]

                def tail1(cg, j):
                    tcn = cg * 4 + j
                    gb = 3
                    hs = tcn % 2
                    for c in range(4):
                        for half in range(2):
                            P.op("pe", I("matmul", ps[gb].a(c * 128, [1, 128]), Ssum.a(half * 512 + c * 128, [1, 128]),
                                         ident.a(0, [1, 128]), start=(half == 0), stop=(half == 1)),
                                 reads=["tt", "ident"], writes=[pr(gb)])

                def tail2(cg, j):
                    s = cg % 2
                    ga = cg % 2
                    tcn = cg * 4 + j
                    gb = 3
                    hs = tcn % 2
                    P.op("dve", I("tensor_tensor", out=HT.a(hs * 512, [128, 4], [1, 128]),
                                  in0=ps[gb].a(0, [128, 4], [1, 128]),
                                  in1=gAs.a(ga * 4 * 512 + j * 128, [512, 4], [1, 128]), op=ALU.mult),
                         reads=[pr(gb)] + [f"gA{ga}_{c}" for c in range(4)], writes=[f"HT{hs}"])

                def tail_mm(cg, j, c):
                    s = cg % 2
                    hs = (cg * 4 + j) % 2
                    for dh in range(2):
                        ob = 6 + dh
                        P.op("pe", I("matmul", ps[ob].a(0, [1, 512]), HT.a(hs * 512 + c * 128, [1, 128]),
                                     Areg.a(s * 4096 + c * D + dh * 512, [1, 512]), start=(c == 0), stop=(c == 3)),
                             reads=[f"HT{hs}", f"A{s}"], writes=[pr(ob)])

                def tail3(cg, j, dh):
                    ti = t0 + j
                    ob = 6 + dh
                    P.op("dve", I("tensor_tensor", out=htok.a(ti * D + dh * 512, [1, 512]), in0=ps[ob].a(0, [1, 512]),
                                  in1=htok.a(ti * D + dh * 512, [1, 512]), op=ALU.add),
                         reads=[pr(ob), f"htok{ti}"], writes=[f"htok{ti}"])

                NCG = 32
                load_ut(0)
                load_ut(1)
                load_v(0)
                for c in range(4):
                    for kc in range(8):
                        emit_AT_mm(0, c, kc)
                    emit_AT_copy(c)
                emit_AT_gelu(0)
                seq = [(cg, j, h) for cg in range(NCG) for j in range(4) for h in range(8)]
                for k in range(NY - 1):
                    emit_Y(k, *seq[k])
                pend_copy = None
                pend_gelu = None
                for idx, (cg, j, h) in enumerate(seq):
                    if pend_copy is not None:
                        emit_AT_copy(pend_copy)
                        pend_copy = None
                        if pend_gelu is not None:
                            emit_AT_gelu(pend_gelu)
                            pend_gelu = None
                    if j == 0 and h == 0 and cg + 2 < NCG:
                        load_ut(cg + 2)
                    if idx >= 8:
                        pcg, pj, _ = seq[idx - 8]
                        if h == 0:
                            tail0(pcg, pj)
                        elif h == 4:
                            tail1(pcg, pj)
                        elif h == 5:
                            tail2(pcg, pj)
                        elif h >= 6:
                            tail_mm(pcg, pj, h - 6)
                    if idx >= 16:
                        ppcg, ppj, _ = seq[idx - 16]
                        if h < 2:
                            tail_mm(ppcg, ppj, h + 2)
                        elif h < 4:
                            tail3(ppcg, ppj, h - 2)
                    if j == 1 and h == 4 and cg + 1 < NCG:
                        load_v(cg + 1)
                    emit_EF(idx, cg, j, h)
                    emit_T(idx, cg, j, h)
                    if cg + 1 < NCG:
                        emit_AT_mm(cg + 1, j, h)
                        if h == 7:
                            pend_copy = j
                            if j == 3:
                                pend_gelu = cg + 1
                    if idx + NY - 1 < len(seq):
                        emit_Y(idx + NY - 1, *seq[idx + NY - 1])
                tail_mm(NCG - 1, 2, 2)
                tail_mm(NCG - 1, 2, 3)
                tail3(NCG - 1, 2, 0)
                tail3(NCG - 1, 2, 1)
                tail0(NCG - 1, 3)
                tail1(NCG - 1, 3)
                tail2(NCG - 1, 3)
                for c in range(4):
                    tail_mm(NCG - 1, 3, c)
                tail3(NCG - 1, 3, 0)
                tail3(NCG - 1, 3, 1)
                def ln2_block(t0=t0):
                    for j in range(4):
                        ti = t0 + j
                        layer_norm(htok.a(ti * D, [1, D]), f"htok{ti}", ti, final, seg)
                def ln2_tile(j, which, t0=t0):
                    lnset[0] = which
                    layer_norm(htok.a((t0 + j) * D, [1, D]), f"htok{t0 + j}", t0 + j, final, seg)
                    lnset[0] = 0
                if blk == 0:
                    ln2_pending = grab_with_pool([6, 7], ln2_block)
                else:
                    for jp in range(2):
                        a_ops = grab_with_pool([0, 1, 2, 3], ln2_tile, 2 * jp, 0)
                        b_ops = grab_with_pool([4, 5, 6, 7], ln2_tile, 2 * jp + 1, 1)
                        P.emit_merged(a_ops, b_ops)

        for seg in range(nseg):
            if seg > 0:
                P.barrier()
            load_ln_params(lnin_h, 0, ALPHA)

            def in_ln(ti, which):
                row0 = seg * SEG + ti * 128
                xb, xn = (xs, "xs") if which == 0 else (xsB, "xsB")
                lnset[0] = which
                P.op("sp", I("dma_start", out=xb.a(0, [1, D]), in_=DAP(x_h, row0 * D, [D, 128], [1, D])),
                     writes=[xn] + (["qT"] if which else []), dma=xn)
                layer_norm(xb.a(0, [1, D]), xn, ti, stop == "ln_in", seg)
                lnset[0] = 0

            for tp in range(NTS // 2):
                a_ops = grab_with_pool([0, 1, 2, 3], in_ln, 2 * tp, 0)
                b_ops = grab_with_pool([4, 5, 6, 7], in_ln, 2 * tp + 1, 1)
                P.emit_merged(a_ops, b_ops)
            if stop == "ln_in":
                continue
            for l in range(nlayers):
                P.barrier()
                mixer(seg, l)
                if not (stop == "mixer" and l == nlayers - 1):
                    prefetch_wq(l)
                mark(0)
                P.barrier()
                if stop == "mixer" and l == nlayers - 1:
                    for ti in range(NTS):
                        row0 = seg * SEG + ti * 128
                        P.op("sp", I("dma_start", out=DAP(out_h, row0 * D, [D, 128], [1, D]), in_=htok.a(ti * D, [1, D])),
                            reads=[f"htok{ti}"], dma=f"out{ti}")
                    continue
                peer(seg, l, final=(l == nlayers - 1))
        P.barrier()
        P.emit(st)
    return nc


def _consts():
    c = np.zeros((128, 5, 128), np.float32)
    s = np.arange(128)[:, None]
    t = np.arange(128)[None, :]
    c[:, 0] = np.eye(128)
    c[:, 1] = np.where(s <= t, -1.0 / 16.0, 0.0)
    c[:, 2] = np.where(s > t, -1.0 / 16.0, 0.0)
    c[:, 3] = np.where(s <= t, 1.0, 0.0)
    c[:, 4] = 1.0
    return np.ascontiguousarray(c.reshape(128, 640))


def prep_shared(inp):
    f = lambda a: np.ascontiguousarray(np.asarray(a, dtype=np.float32))
    sh = {
        "cst": _consts(),
        "lnin": f(np.stack([inp["ln_in_g"], inp["ln_in_b"]])),
        "w_in": f(np.asarray(inp["w_in"]).reshape(L * D, INC)),
        "w_out": f(np.asarray(inp["w_out"]).reshape(L * D, D)),
        "wq": f(np.asarray(inp["peer_wq"]).reshape(L * D, 2048)),
        "wgu": f(np.asarray(inp["w_gate_up"]).reshape(L * 16, 256)),
        "bgate": f(inp["b_gate"]),
        "glag": f(inp["gla_norm_g"]),
        "sgln": f(np.stack([np.asarray(inp["sgu_ln_g"]), np.asarray(inp["sgu_ln_b"])], axis=1).reshape(L * 2, 512)),
        "sgwT": f(np.asarray(inp["sgu_w"]).transpose(0, 1, 3, 2).reshape(L * 4 * 128, 128)),
        "sgb": f(np.asarray(inp["sgu_b"]).reshape(L, 512)),
        "ln1": f(np.stack([np.asarray(inp["ln1_g"]), np.asarray(inp["ln1_b"])], axis=1).reshape(L * 2, D)),
        "ln2": f(np.stack([np.asarray(inp["ln2_g"]), np.asarray(inp["ln2_b"])], axis=1).reshape(L * 2, D)),
        "k1T": f(np.asarray(inp["peer_k1"]).transpose(0, 2, 1).reshape(L * 128, 128)),
        "k2T": f(np.asarray(inp["peer_k2"]).transpose(0, 2, 1).reshape(L * 128, 128)),
        "uT": f(np.asarray(inp["peer_u"]).transpose(0, 2, 1).reshape(L * D, NE)),
        "vtab": f(np.asarray(inp["peer_v"]).reshape(L * NE, D)),
    }
    return sh


def kernel(**inputs):
    x = np.asarray(inputs["x"], dtype=np.float32)
    nb = x.shape[0]
    sh = prep_shared(inputs)
    nc = build()
    in_maps = []
    for b in range(nb):
        m = dict(sh)
        m["x"] = np.ascontiguousarray(x[b])
        in_maps.append(m)
    res = run_bass_kernel_spmd(nc, in_maps, core_ids=list(range(nb)))
    return np.stack([np.asarray(r["out"]) for r in res.results], axis=0).astype(np.float32)
```
